# Optimizing a Trainium2 kernel written in Bass

```python
import math
import jax, jax.numpy as jnp
from jax import lax
import numpy as np

D_MODEL = 1024
BATCH = 8
SEQ = 2048
DEPTH = 2
DEC_BATCH = 128
DEC_SEQ = 4
PAST_LEN = 16384
PAGE_SIZE = 128

N_AB = (DEPTH + 1) // 2
N_C = DEPTH // 2
CHUNK = 64
EPS = 1e-6
D_FF = 2816

HGRN_HEADS = 4
HGRN_HEAD_DIM = 128
HGRN_WIDTH = HGRN_HEADS * HGRN_HEAD_DIM

SSM_HEADS = 8
SSM_HEAD_DIM = 64
SSM_INNER = SSM_HEADS * SSM_HEAD_DIM
SSM_GROUPS = 2
SSM_STATE = 128
CONV_W = 4
CONV_DIM = SSM_INNER + 2 * SSM_GROUPS * SSM_STATE

AB_PROJ = 4 * HGRN_WIDTH + SSM_INNER + CONV_DIM + SSM_HEADS
AB_SPLITS = (HGRN_WIDTH, 2 * HGRN_WIDTH, 3 * HGRN_WIDTH, 4 * HGRN_WIDTH,
             4 * HGRN_WIDTH + SSM_INNER, 4 * HGRN_WIDTH + SSM_INNER + CONV_DIM)
AB_WIDTH = HGRN_WIDTH + SSM_INNER

GLA_HEADS = 4
GLA_HEAD_K = 128
GLA_HEAD_V = 256
GLA_KEY = GLA_HEADS * GLA_HEAD_K
GLA_VAL = GLA_HEADS * GLA_HEAD_V
GK_RANK = 16
GK_NORMALIZER = 16.0
GLA_PROJ = 2 * GLA_KEY + 2 * GLA_VAL + GK_RANK
GLA_SPLITS = (GLA_KEY, 2 * GLA_KEY, 2 * GLA_KEY + GLA_VAL, 2 * GLA_KEY + 2 * GLA_VAL)

kernel_name = "hgrn2_mamba2_gla_macaron_step"


def rmsnorm(x, w):
    xf = x.astype(jnp.float32)
    y = xf * lax.rsqrt(jnp.mean(xf * xf, axis=-1, keepdims=True) + EPS)
    return (y * w.astype(jnp.float32)).astype(x.dtype)


def swiglu(x, w_in, w_out):
    gate, up = jnp.split(x @ w_in, 2, axis=-1)
    return (jax.nn.silu(gate) * up) @ w_out


def chunk_size(length):
    return math.gcd(length, CHUNK)


def chunked_gated_linear_attention(q, k, v, log_f, s0):
    B, L, H, K = q.shape
    V = v.shape[-1]
    C = chunk_size(L)
    n = L // C

    def blocks(t):
        return t.astype(jnp.float32).reshape(B, n, C, H, t.shape[-1])

    qc, kc, vc, gc = blocks(q), blocks(k), blocks(v), blocks(log_f)
    g = jnp.cumsum(gc, axis=2)
    g_last = g[:, :, -1:]
    q_dec = qc * jnp.exp(g)
    k_inv = kc * jnp.exp(-g)
    k_end = kc * jnp.exp(g_last - g)
    causal = jnp.tril(jnp.ones((C, C), bool))
    scores = jnp.where(causal, jnp.einsum('bnihk,bnjhk->bnhij', q_dec, k_inv), 0.0)
    o_intra = jnp.einsum('bnhij,bnjhv->bnihv', scores, vc)

    def step(s, inp):
        q_d, k_e, v_c, decay = inp
        o_inter = jnp.einsum('bihk,bhkv->bihv', q_d, s)
        s = decay[..., None] * s + jnp.einsum('bjhk,bjhv->bhkv', k_e, v_c)
        return s, o_inter

    xs = (jnp.moveaxis(q_dec, 1, 0), jnp.moveaxis(k_end, 1, 0), jnp.moveaxis(vc, 1, 0),
          jnp.moveaxis(jnp.exp(g_last[:, :, 0]), 1, 0))
    s_final, o_inter = lax.scan(step, s0.astype(jnp.float32), xs)
    o = o_intra + jnp.moveaxis(o_inter, 0, 1)
    return o.reshape(B, L, H, V), s_final


def chunked_ssd(x, dt, a, b, c, s0):
    B, L, H, P = x.shape
    G, N = b.shape[-2], b.shape[-1]
    R = H // G
    C = chunk_size(L)
    n = L // C
    f32 = jnp.float32
    xc = x.astype(f32).reshape(B, n, C, G, R, P)
    dtc = dt.astype(f32).reshape(B, n, C, G, R)
    bc = b.astype(f32).reshape(B, n, C, G, N)
    cc = c.astype(f32).reshape(B, n, C, G, N)
    cum = jnp.cumsum(dtc * a.astype(f32).reshape(G, R), axis=2)
    causal = jnp.tril(jnp.ones((C, C), bool))[:, :, None, None]
    seg = cum[:, :, :, None] - cum[:, :, None, :]
    decay = jnp.exp(jnp.where(causal, seg, -jnp.inf))
    xdt = xc * dtc[..., None]
    cb = jnp.einsum('bnigs,bnjgs->bnijg', cc, bc)
    y_intra = jnp.einsum('bnijgr,bnjgrp->bnigrp', cb[..., None] * decay, xdt)
    x_end = xdt * jnp.exp(cum[:, :, -1:] - cum)[..., None]
    chunk_decay = jnp.exp(cum[:, :, -1])

    def step(s, inp):
        c_i, cdec_i, b_i, xe_i, dec_i = inp
        y_inter = jnp.einsum('bigs,bigr,bgrps->bigrp', c_i, cdec_i, s)
        s = dec_i[..., None, None] * s + jnp.einsum('bjgrp,bjgs->bgrps', xe_i, b_i)
        return s, y_inter

    xs = (jnp.moveaxis(cc, 1, 0), jnp.moveaxis(jnp.exp(cum), 1, 0), jnp.moveaxis(bc, 1, 0),
          jnp.moveaxis(x_end, 1, 0), jnp.moveaxis(chunk_decay, 1, 0))
    s_final, y_inter = lax.scan(step, s0.astype(f32).reshape(B, G, R, P, N), xs)
    y = y_intra + jnp.moveaxis(y_inter, 0, 1)
    return y.reshape(B, L, H, P), s_final.reshape(B, H, P, N)


def hgrn2_mixer(q_raw, f_raw, i_raw, g_raw, lb, state, norm_w):
    B, L, _ = q_raw.shape

    def heads(t):
        return t.reshape(B, L, HGRN_HEADS, HGRN_HEAD_DIM)

    f = lb + (1.0 - lb) * jax.nn.sigmoid(f_raw.astype(jnp.float32))
    q = jax.nn.silu(q_raw)
    o, s_new = chunked_gated_linear_attention(heads(q), heads(1.0 - f), heads(i_raw), heads(jnp.log(f)), state)
    o = rmsnorm(o, norm_w).reshape(B, L, HGRN_WIDTH) * jax.nn.silu(g_raw.astype(jnp.float32))
    return o.astype(q_raw.dtype), s_new


def mamba2_mixer(z, xbc, dt_raw, conv_buf, ssm_state, conv_w, conv_b, dt_bias, a_log, d_skip, norm_w):
    B, L, _ = xbc.shape
    padded = jnp.concatenate([conv_buf.astype(xbc.dtype), xbc], axis=1)
    conv = conv_b + sum(padded[:, k:k + L] * conv_w[k] for k in range(CONV_W))
    new_buf = padded[:, -(CONV_W - 1):]
    xbc_act = jax.nn.silu(conv)
    xs, bs, cs = jnp.split(xbc_act, [SSM_INNER, SSM_INNER + SSM_GROUPS * SSM_STATE], axis=-1)
    dt = jax.nn.softplus(dt_raw.astype(jnp.float32) + dt_bias.astype(jnp.float32))
    a = -jnp.exp(a_log.astype(jnp.float32))
    xh = xs.reshape(B, L, SSM_HEADS, SSM_HEAD_DIM)
    y, s_new = chunked_ssd(xh, dt, a,
                           bs.reshape(B, L, SSM_GROUPS, SSM_STATE),
                           cs.reshape(B, L, SSM_GROUPS, SSM_STATE), ssm_state)
    y = y + d_skip.astype(jnp.float32)[:, None] * xh.astype(jnp.float32)
    y = y.reshape(B, L, SSM_INNER) * jax.nn.silu(z.astype(jnp.float32))
    y = rmsnorm(y.reshape(B, L, SSM_GROUPS, SSM_INNER // SSM_GROUPS),
                norm_w.reshape(SSM_GROUPS, SSM_INNER // SSM_GROUPS)).reshape(B, L, SSM_INNER)
    return y.astype(xbc.dtype), new_buf, s_new


def gla_mixer(proj, state, w_gk, b_gk, norm_w):
    B, L, _ = proj.shape
    q, k, v, g, gk_low = jnp.split(proj, GLA_SPLITS, axis=-1)
    log_f = jax.nn.log_sigmoid((gk_low @ w_gk + b_gk).astype(jnp.float32)) / GK_NORMALIZER
    hk = lambda t: t.reshape(B, L, GLA_HEADS, GLA_HEAD_K)
    o, s_new = chunked_gated_linear_attention(hk(q * GLA_HEAD_K ** -0.5), hk(k),
                                              v.reshape(B, L, GLA_HEADS, GLA_HEAD_V), hk(log_f), state)
    o = rmsnorm(o, norm_w).reshape(B, L, GLA_VAL) * jax.nn.silu(g.astype(jnp.float32))
    return o.astype(proj.dtype), s_new


def trunk(x, s_hgrn, s_ssm, s_conv, s_gla, p):
    dtype = x.dtype
    new_hgrn, new_ssm, new_conv, new_gla = [], [], [], []
    lb_all = jnp.cumsum(jax.nn.softmax(p["hgrn_lb_logits"].astype(jnp.float32), axis=0), axis=0)
    for layer in range(DEPTH):
        j = layer // 2
        x = x + 0.5 * swiglu(rmsnorm(x, p["norm_ffn1"][layer]), p["ffn1_w_in"][layer], p["ffn1_w_out"][layer])
        h = rmsnorm(x, p["norm_mix"][layer])
        if layer % 2 == 0:
            proj = h @ p["ab_w_in"][j]
            q_raw, f_raw, i_raw, g_raw, z, xbc, dt_raw = jnp.split(proj, AB_SPLITS, axis=-1)
            o_a, s_a = hgrn2_mixer(q_raw, f_raw, i_raw, g_raw, lb_all[j], s_hgrn[j], p["hgrn_norm"][j])
            o_b, buf_b, s_b = mamba2_mixer(z, xbc, dt_raw, s_conv[j], s_ssm[j], p["ssm_conv_w"][j],
                                           p["ssm_conv_b"][j], p["ssm_dt_bias"][j], p["ssm_a_log"][j],
                                           p["ssm_d"][j], p["ssm_norm"][j])
            x = x + jnp.concatenate([o_a, o_b], axis=-1) @ p["ab_w_out"][j]
            new_hgrn.append(s_a.astype(dtype))
            new_ssm.append(s_b.astype(dtype))
            new_conv.append(buf_b.astype(dtype))
        else:
            proj = h @ p["gla_w_in"][j]
            o_c, s_c = gla_mixer(proj, s_gla[j], p["gla_w_gk"][j], p["gla_b_gk"][j], p["gla_norm"][j])
            x = x + o_c @ p["gla_w_out"][j]
            new_gla.append(s_c.astype(dtype))
        x = x + 0.5 * swiglu(rmsnorm(x, p["norm_ffn2"][layer]), p["ffn2_w_in"][layer], p["ffn2_w_out"][layer])
    y = rmsnorm(x, p["norm_final"])
    return y, jnp.stack(new_hgrn), jnp.stack(new_ssm), jnp.stack(new_conv), jnp.stack(new_gla)


def setup_inputs(seed: int = 0) -> dict:
    key = jax.random.key(seed)
    keys = list(jax.random.split(key, 29))

    def normal(i, shape, scale):
        return jax.random.normal(keys[i], shape, jnp.float32) * scale

    def gain(i, shape):
        return 1.0 + normal(i, shape, 0.02)

    dt0 = jnp.exp(jax.random.uniform(keys[20], (N_AB, SSM_HEADS), jnp.float32,
                                     minval=math.log(1e-3), maxval=math.log(1e-1)))
    return {
        "x_prompt": normal(0, (BATCH, SEQ, D_MODEL), 1.0),
        "x_sample": normal(1, (DEC_BATCH, DEC_SEQ, D_MODEL), 1.0),
        "state_hgrn": normal(2, (N_AB, DEC_BATCH, HGRN_HEADS, HGRN_HEAD_DIM, HGRN_HEAD_DIM), 0.5),
        "state_ssm": normal(3, (N_AB, DEC_BATCH, SSM_HEADS, SSM_HEAD_DIM, SSM_STATE), 0.5),
        "state_conv": normal(4, (N_AB, DEC_BATCH, CONV_W - 1, CONV_DIM), 1.0),
        "state_gla": normal(5, (N_C, DEC_BATCH, GLA_HEADS, GLA_HEAD_K, GLA_HEAD_V), 1.0),
        "norm_ffn1": gain(6, (DEPTH, D_MODEL)),
        "norm_mix": gain(7, (DEPTH, D_MODEL)),
        "norm_ffn2": gain(8, (DEPTH, D_MODEL)),
        "norm_final": gain(9, (D_MODEL,)),
        "ffn1_w_in": normal(10, (DEPTH, D_MODEL, 2 * D_FF), D_MODEL ** -0.5),
        "ffn1_w_out": normal(11, (DEPTH, D_FF, D_MODEL), D_FF ** -0.5),
        "ffn2_w_in": normal(12, (DEPTH, D_MODEL, 2 * D_FF), D_MODEL ** -0.5),
        "ffn2_w_out": normal(13, (DEPTH, D_FF, D_MODEL), D_FF ** -0.5),
        "ab_w_in": normal(14, (N_AB, D_MODEL, AB_PROJ), D_MODEL ** -0.5),
        "ab_w_out": normal(15, (N_AB, AB_WIDTH, D_MODEL), AB_WIDTH ** -0.5),
        "hgrn_lb_logits": normal(16, (N_AB + 1, HGRN_WIDTH), 0.1),
        "hgrn_norm": gain(17, (N_AB, HGRN_HEAD_DIM)),
        "ssm_conv_w": normal(18, (N_AB, CONV_W, CONV_DIM), CONV_W ** -0.5),
        "ssm_conv_b": normal(19, (N_AB, CONV_DIM), 0.02),
        "ssm_dt_bias": dt0 + jnp.log(-jnp.expm1(-dt0)),
        "ssm_a_log": jnp.log(jax.random.uniform(keys[21], (N_AB, SSM_HEADS), jnp.float32, minval=1.0, maxval=16.0)),
        "ssm_d": gain(22, (N_AB, SSM_HEADS)),
        "ssm_norm": gain(23, (N_AB, SSM_INNER)),
        "gla_w_in": normal(24, (N_C, D_MODEL, GLA_PROJ), D_MODEL ** -0.5),
        "gla_w_gk": normal(25, (N_C, GK_RANK, GLA_KEY), GK_RANK ** -0.5),
        "gla_b_gk": normal(26, (N_C, GLA_KEY), 0.1),
        "gla_norm": gain(27, (N_C, GLA_HEAD_V)),
        "gla_w_out": normal(28, (N_C, GLA_VAL, D_MODEL), GLA_VAL ** -0.5),
    }


def reference(x_prompt, x_sample, state_hgrn, state_ssm, state_conv, state_gla,
              norm_ffn1, norm_mix, norm_ffn2, norm_final,
              ffn1_w_in, ffn1_w_out, ffn2_w_in, ffn2_w_out,
              ab_w_in, ab_w_out, hgrn_lb_logits, hgrn_norm,
              ssm_conv_w, ssm_conv_b, ssm_dt_bias, ssm_a_log, ssm_d, ssm_norm,
              gla_w_in, gla_w_gk, gla_b_gk, gla_norm, gla_w_out):
    p = dict(norm_ffn1=norm_ffn1, norm_mix=norm_mix, norm_ffn2=norm_ffn2, norm_final=norm_final,
             ffn1_w_in=ffn1_w_in, ffn1_w_out=ffn1_w_out, ffn2_w_in=ffn2_w_in, ffn2_w_out=ffn2_w_out,
             ab_w_in=ab_w_in, ab_w_out=ab_w_out, hgrn_lb_logits=hgrn_lb_logits, hgrn_norm=hgrn_norm,
             ssm_conv_w=ssm_conv_w, ssm_conv_b=ssm_conv_b, ssm_dt_bias=ssm_dt_bias, ssm_a_log=ssm_a_log,
             ssm_d=ssm_d, ssm_norm=ssm_norm, gla_w_in=gla_w_in, gla_w_gk=gla_w_gk, gla_b_gk=gla_b_gk,
             gla_norm=gla_norm, gla_w_out=gla_w_out)
    b = x_prompt.shape[0]
    dt = x_prompt.dtype
    zh = jnp.zeros((N_AB, b, HGRN_HEADS, HGRN_HEAD_DIM, HGRN_HEAD_DIM), dt)
    zs = jnp.zeros((N_AB, b, SSM_HEADS, SSM_HEAD_DIM, SSM_STATE), dt)
    zc = jnp.zeros((N_AB, b, CONV_W - 1, CONV_DIM), dt)
    zg = jnp.zeros((N_C, b, GLA_HEADS, GLA_HEAD_K, GLA_HEAD_V), dt)
    y_prompt, hgrn_p, ssm_p, conv_p, gla_p = trunk(x_prompt, zh, zs, zc, zg, p)
    y_sample, hgrn_s, ssm_s, conv_s, gla_s = trunk(x_sample, state_hgrn, state_ssm, state_conv, state_gla, p)
    return (y_prompt, y_sample, hgrn_p, hgrn_s, ssm_p, ssm_s, conv_p, conv_s, gla_p, gla_s)
```

```python
import contextlib
import numpy as np
import ml_dtypes
import concourse.bass as bass
import concourse.mybir as mybir
from concourse.bass_utils import run_bass_kernel_spmd

F32 = mybir.dt.float32
BF16 = mybir.dt.bfloat16
AF = mybir.ActivationFunctionType
ALU = mybir.AluOpType

N_CORES = 8
D = 1024
DFF = 2816
NFC = 22
SEQ = 2048
NT = 17
TTOK = 2112
EPS = 1e-6
STILES = [(0, 512), (512, 512), (1024, 512), (1536, 512), (2048, 64)]
FSTILES = [(0, 448), (448, 448), (896, 448), (1344, 448), (1792, 320)]

ENGS = ("pe", "act", "dve", "pool", "sp")
NLANES = {"sp": 12, "pool": 12, "act": 6}

DEBUG = {"mixers": True, "strict": True, "hgrn": True, "gla": True}


class _Op:
    __slots__ = ("eng", "fn", "waits", "signal", "ticket", "dma", "lane", "lane_ticket", "idx")


class Prog:
    def __init__(self, nc, strict_same_engine=True):
        self.nc = nc
        self.ops = {e: [] for e in ENGS}
        self.last_w = {}
        self.readers = {}
        self.children = {}
        self.dma_count = {"sp": 0, "pool": 0, "act": 0}
        self.strict = strict_same_engine

    def _conflicts(self, key):
        out = [key[:i] for i in range(1, len(key) + 1)]
        out.extend(self.children.get(key, ()))
        return out

    def _register(self, key):
        for i in range(1, len(key)):
            self.children.setdefault(key[:i], set()).add(key)

    def op(self, eng, fn, reads=(), writes=(), dma=False):
        o = _Op()
        o.eng, o.fn, o.dma, o.signal, o.ticket = eng, fn, dma, False, None
        o.idx = len(self.ops[eng])
        deps = []
        for k in reads:
            for c in self._conflicts(k):
                t = self.last_w.get(c)
                if t is not None:
                    deps.append((t, "raw"))
        for k in writes:
            for c in self._conflicts(k):
                t = self.last_w.get(c)
                if t is not None:
                    deps.append((t, "waw"))
                for t in self.readers.get(c, {}).values():
                    deps.append((t, "war"))
        if dma:
            n = self.dma_count[eng]
            self.dma_count[eng] = n + 1
            nl = NLANES[eng]
            o.lane = n % nl
            o.lane_ticket = 16 * (n // nl + 1)
            tok = ("d", eng, o.lane, o.lane_ticket, o.idx)
            if n >= nl:
                deps.append((("d", eng, o.lane, o.lane_ticket - 16, -1), "lane"))
        else:
            tok = ("c", eng, o.idx)
        o.waits = []
        for t, kind in deps:
            if t[0] == "c" and t[1] == eng and not dma:
                if eng == "pe" or not self.strict:
                    continue
            o.waits.append(t)
        for k in writes:
            self._register(k)
            self.last_w[k] = tok
            self.readers[k] = {}
            for ch in list(self.children.get(k, ())):
                self.last_w.pop(ch, None)
                self.readers.pop(ch, None)
        for k in reads:
            self._register(k)
            src = (tok[0], tok[1], tok[2] if tok[0] == "d" else 0)
            self.readers.setdefault(k, {})[src] = tok
        self.ops[eng].append(o)
        return o

    def emit(self, final_wait_eng="sp"):
        nc = self.nc
        for e in ENGS:
            for o in self.ops[e]:
                for t in o.waits:
                    if t[0] == "c":
                        self.ops[t[1]][t[2]].signal = True
        for e in ENGS:
            n = 0
            for o in self.ops[e]:
                if o.signal and not o.dma:
                    n += 1
                    o.ticket = n
        with contextlib.ExitStack() as st:
            csem = {e: st.enter_context(nc.semaphore("c_" + e)) for e in ENGS}
            lsem = {}
            for q, nl in NLANES.items():
                for l in range(nl):
                    lsem[(q, l)] = st.enter_context(nc.semaphore("l_%s_%d" % (q, l)))
            block = st.enter_context(nc.Block())
            engobj = {"pe": "tensor", "act": "scalar", "dve": "vector", "pool": "gpsimd", "sp": "sync"}

            def emit_engine(e, eng):
                seen = {}
                for o in self.ops[e]:
                    need = {}
                    for t in o.waits:
                        if t[0] == "c":
                            key = ("c", t[1])
                            val = self.ops[t[1]][t[2]].ticket
                        else:
                            key = ("d", t[1], t[2])
                            val = t[3]
                        if seen.get(key, 0) >= val:
                            continue
                        if need.get(key, 0) < val:
                            need[key] = val
                    for key, val in need.items():
                        seen[key] = val
                        sem = csem[key[1]] if key[0] == "c" else lsem[(key[1], key[2])]
                        eng.wait_ge(sem, val)
                    inst = o.fn(eng)
                    if o.dma:
                        inst.then_inc(lsem[(e, o.lane)], 16)
                    elif o.signal:
                        inst.then_inc(csem[e], 1)
                if e == final_wait_eng:
                    for q, nl in NLANES.items():
                        n = self.dma_count[q]
                        for l in range(nl):
                            cnt = (n - l + nl - 1) // nl if n > l else 0
                            if cnt > 0:
                                eng.wait_ge(lsem[(q, l)], 16 * cnt)

            for e in ENGS:
                if not self.ops[e] and e != final_wait_eng:
                    continue

                def body(eng, e=e):
                    emit_engine(e, eng)
                getattr(block, engobj[e])(body)


def build_program():
    nc = bass.Bass("TRN2", target_bir_lowering=False)

    def din(name, shape, dt=F32):
        return nc.dram_tensor(name, list(shape), dt, kind="ExternalInput").ap()

    def dout(name, shape):
        return nc.dram_tensor(name, list(shape), F32, kind="ExternalOutput").ap()

    xp = din("xp", [SEQ, D])
    xs = din("xs", [64, D])
    norm_ffn1 = din("norm_ffn1", [2, D])
    norm_mix = din("norm_mix", [2, D])
    norm_ffn2 = din("norm_ffn2", [2, D])
    norm_final = din("norm_final", [D])
    ffn_w_in = [din("ffn1_w_in", [2, D, 2 * DFF]), din("ffn2_w_in", [2, D, 2 * DFF])]
    ffn_w_out = [din("ffn1_w_out", [2, DFF, D]), din("ffn2_w_out", [2, DFF, D])]
    ident_d = din("ident", [128, 128], BF16)
    ones_d = din("ones_bf", [128, 128], BF16)
    maskP_d = din("maskP", [128, 128])
    maskS_d = din("maskS", [128, 128])
    resetP_d = din("resetP", [128, 512])
    resetS_d = din("resetS", [128, 512])
    ab_w_in = din("ab_w_in", [1, D, 3592])
    ab_w_out = din("ab_w_out", [1, D, D])
    lb_logits = din("hgrn_lb_logits", [2, 512])
    hgrn_norm = din("hgrn_norm", [1, 128])
    gla_w_in = din("gla_w_in", [1, D, 3088])
    gla_w_gk = din("gla_w_gk", [1, 16, 512])
    gla_b_gk = din("gla_b_gk", [1, 512])
    gla_norm = din("gla_norm", [1, 256])
    gla_w_out = din("gla_w_out", [1, D, D])
    hs0 = din("hs0", [16, 4, 128, 128])
    gs0 = din("gs0", [16, 4, 128, 256])
    hgp = dout("hgp", [4, 128, 128])
    hgs = dout("hgs", [16, 4, 128, 128])
    glp = dout("glp", [4, 128, 256])
    gls = dout("gls", [16, 4, 128, 256])
    cst_d = din("cst", [128, 384])
    conv_w = din("ssm_conv_w", [1, 4, 1024])
    conv_b = din("ssm_conv_b", [1, 1024])
    dt_bias = din("ssm_dt_bias", [1, 8])
    a_log = din("ssm_a_log", [1, 8])
    ssm_d = din("ssm_d", [1, 8])
    ssm_norm = din("ssm_norm", [1, 512])
    ss0 = din("ss0", [16, 8, 64, 128])
    cs0 = din("cs0", [16, 3, 1024])
    ssp = dout("ssp", [8, 64, 128])
    sss = dout("sss", [16, 8, 64, 128])
    cvp = dout("cvp", [3, 1024])
    cvs = dout("cvs", [16, 3, 1024])

    yp = dout("yp", [SEQ, D])
    ys = dout("ys", [64, D])

    with contextlib.ExitStack() as st:
        def sb(name, shape, dt=F32):
            return st.enter_context(nc.sbuf_tensor(name, list(shape), dt))

        X = sb("X", [128, NT, D])
        hT = sb("hT", [128, 8, TTOK], BF16)
        aT = sb("aT", [128, 11, TTOK], BF16)
        Wo = sb("Wo", [128, 11, D], BF16)
        Wr = [sb("Wr%d" % i, [128, 8, 128], BF16) for i in range(8)]
        sg = [sb("sg%d" % i, [128, 512]) for i in range(2)]
        hn = [sb("hn%d" % i, [128, D], BF16) for i in range(2)]
        junk = sb("junk", [128, D], BF16)
        ident = sb("ident_sb", [128, 128], BF16)
        gcol = sb("gcol", [128, 6, 8])
        gfin = sb("gfin", [128, D])
        ss = sb("ss", [128, 32])
        rs = sb("rs", [128, 32])
        ones_bf = sb("ones_sb", [128, 128], BF16)
        maskP = sb("maskP_sb", [128, 128])
        maskS = sb("maskS_sb", [128, 128])
        resetP = sb("resetP_sb", [128, 512])
        resetS = sb("resetS_sb", [128, 512])
        cols = sb("cols", [128, 64])
        dummy = sb("dummy_sb", [128, 8])
        CST = sb("cst_sb", [128, 384])
        cols2 = sb("cols2", [128, 64])
        TAIL = sb("tail_sb", [128, 8, 4])
        PS = [st.enter_context(nc.psum_tensor("ps%d" % i, [128, 512], F32)) for i in range(8)]

        P = Prog(nc, strict_same_engine=DEBUG["strict"])

        P.op("act", lambda e: e.dma_start(out=ident[:], in_=ident_d), writes=[("ident",)], dma=True)
        gains = [norm_ffn1[0], norm_mix[0], norm_ffn2[0], norm_ffn1[1], norm_mix[1], norm_ffn2[1]]
        for i, g in enumerate(gains):
            P.op("act", lambda e, i=i, g=g: e.dma_start(out=gcol[:, i, :], in_=g.rearrange("(c p) -> p c", p=128)),
                 writes=[("gcol", i)], dma=True)
        P.op("act", lambda e: e.dma_start(out=gfin[:], in_=norm_final.partition_broadcast(128)),
             writes=[("gfin",)], dma=True)
        for tt in range(16):
            P.op("sp", lambda e, tt=tt: e.dma_start(out=X[:, tt, :], in_=xp[tt * 128:(tt + 1) * 128, :]),
                 writes=[("X", tt)], dma=True)
        P.op("sp", lambda e: e.dma_start(out=X[0:64, 16, :], in_=xs), writes=[("X", 16)], dma=True)

        def rows(tt):
            return 64 if tt == 16 else 128

        nrm_ctr = [0]

        def norm_to_hT(gi):
            for (t0, n) in STILES:
                norm_group(gi, t0, n)

        def norm_group(gi, t0, n):
            if True:
                ntile = max(1, n // 128)
                r = min(128, n)
                pvs = [PS[bk][:].bitcast(BF16) for bk in range(4)]
                for ti in range(ntile):
                    tt = t0 // 128 + ti
                    k = nrm_ctr[0] % 32
                    nrm_ctr[0] += 1
                    P.op("act", lambda e, tt=tt, k=k, r=r: e.activation(
                        out=junk[0:r, :], in_=X[0:r, tt, :], func=AF.Square, accum_out=ss[0:r, k:k + 1]),
                        reads=[("X", tt)], writes=[("junk",), ("ss", k)])
                    P.op("act", lambda e, k=k, r=r: e.activation(
                        out=rs[0:r, k:k + 1], in_=ss[0:r, k:k + 1], func=AF.Sqrt, scale=1.0 / D, bias=EPS),
                        reads=[("ss", k)], writes=[("rs", k)])
                    P.op("dve", lambda e, k=k, r=r: e.reciprocal(out=rs[0:r, k:k + 1], in_=rs[0:r, k:k + 1]),
                         reads=[("rs", k)], writes=[("rs", k)])
                    hb = tt % 2
                    P.op("dve", lambda e, tt=tt, k=k, hb=hb, r=r: e.tensor_scalar(
                        out=hn[hb][0:r, :], in0=X[0:r, tt, :], scalar1=rs[0:r, k:k + 1], scalar2=None, op0=ALU.mult),
                        reads=[("X", tt), ("rs", k)], writes=[("hn", hb)])
                    for c in range(8):
                        o0 = (c % 2) * 512 + ti * 128
                        P.op("pe", lambda e, c=c, hb=hb, o0=o0, r=r: e.transpose(
                            out=pvs[c // 2][:, o0:o0 + r], in_=hn[hb][0:r, c * 128:(c + 1) * 128], identity=ident[0:r, 0:r]),
                            reads=[("hn", hb), ("ident",)], writes=[("ps", c // 2)])
                tkeys = [("hT", t0 // 128 + ti) for ti in range(ntile)]
                for c in range(8):
                    src = pvs[c // 2][:, (c % 2) * 512:(c % 2) * 512 + n]
                    if c % 2 == 0:
                        P.op("act", lambda e, c=c, src=src, t0=t0, n=n: e.activation(
                            out=hT[:, c, t0:t0 + n], in_=src, func=AF.Copy, scale=gcol[:, gi, c:c + 1]),
                            reads=[("ps", c // 2), ("gcol", gi)], writes=tkeys)
                    else:
                        P.op("dve", lambda e, c=c, src=src, t0=t0, n=n: e.tensor_scalar(
                            out=hT[:, c, t0:t0 + n], in0=src, scalar1=gcol[:, gi, c:c + 1], scalar2=None, op0=ALU.mult),
                            reads=[("ps", c // 2), ("gcol", gi)], writes=tkeys)

        def final_tile(tt):
            r = rows(tt)
            k = nrm_ctr[0] % 32
            nrm_ctr[0] += 1
            P.op("act", lambda e, tt=tt, r=r, k=k: e.activation(
                out=junk[0:r, :], in_=X[0:r, tt, :], func=AF.Square, accum_out=ss[0:r, k:k + 1]),
                reads=[("X", tt)], writes=[("junk",), ("ss", k)])
            P.op("act", lambda e, r=r, k=k: e.activation(
                out=rs[0:r, k:k + 1], in_=ss[0:r, k:k + 1], func=AF.Sqrt, scale=1.0 / D, bias=EPS),
                reads=[("ss", k)], writes=[("rs", k)])
            P.op("dve", lambda e, r=r, k=k: e.reciprocal(out=rs[0:r, k:k + 1], in_=rs[0:r, k:k + 1]),
                 reads=[("rs", k)], writes=[("rs", k)])
            P.op("dve", lambda e, tt=tt, r=r, k=k: e.scalar_tensor_tensor(
                out=X[0:r, tt, :], in0=X[0:r, tt, :], scalar=rs[0:r, k:k + 1], op0=ALU.mult,
                in1=gfin[0:r, :], op1=ALU.mult),
                reads=[("X", tt), ("rs", k), ("gfin",)], writes=[("X", tt)])
            if tt < 16:
                P.op("sp", lambda e, tt=tt: e.dma_start(out=yp[tt * 128:(tt + 1) * 128, :], in_=X[:, tt, :]),
                     reads=[("X", tt)], dma=True)
            else:
                P.op("sp", lambda e: e.dma_start(out=ys, in_=X[0:64, 16, :]), reads=[("X", 16)], dma=True)


        wslot = [0]
        psrot = [0]

        def wload(wmat, col0, ncols=128):
            s = wslot[0] % 8
            wslot[0] += 1
            P.op("pool", lambda e, s=s: e.dma_start(
                out=Wr[s][:, :, 0:ncols], in_=wmat[:, col0:col0 + ncols].rearrange("(c p) n -> p c n", p=128)),
                writes=[("Wr", s)], dma=True)
            return s

        def ffn(w_in, w_out, norm_gi=None, tile_epilogue=None):
            pend = list(range(len(STILES))) if norm_gi is not None else []

            def norm_upto(idx):
                while pend and pend[0] <= idx:
                    gi_ = pend.pop(0)
                    norm_group(norm_gi, *STILES[gi_])
            for g in range(2):
                j0 = g * 11
                for jl in range(11):
                    j = j0 + jl
                    if jl == 4:
                        P.op("pool", lambda e, j0=j0: e.dma_start(
                            out=Wo[:], in_=w_out[j0 * 128:(j0 + 11) * 128, :].rearrange("(j p) d -> p j d", p=128)),
                            writes=[("Wo",)], dma=True)
                    sG = wload(w_in, j * 128)
                    sU = wload(w_in, DFF + j * 128)
                    for (t0, n) in FSTILES:
                        norm_upto(min(4, (t0 + n - 1) // 512) + 1)
                        bg = 4 + (psrot[0] % 2) * 2
                        bu = bg + 1
                        sgi = psrot[0] % 2
                        psrot[0] += 1
                        tiles = [("hT", tt) for tt in range(t0 // 128, (t0 + n - 1) // 128 + 1)]
                        for c in range(8):
                            P.op("pe", lambda e, c=c, s=sG, t0=t0, n=n, bg=bg: e.matmul(
                                PS[bg][:, 0:n], lhsT=Wr[s][:, c, :], rhs=hT[:, c, t0:t0 + n],
                                start=(c == 0), stop=(c == 7)),
                                reads=[("Wr", sG)] + tiles, writes=[("ps", bg)])
                        for c in range(8):
                            P.op("pe", lambda e, c=c, s=sU, t0=t0, n=n, bu=bu: e.matmul(
                                PS[bu][:, 0:n], lhsT=Wr[s][:, c, :], rhs=hT[:, c, t0:t0 + n],
                                start=(c == 0), stop=(c == 7)),
                                reads=[("Wr", sU)] + tiles, writes=[("ps", bu)])
                        P.op("act", lambda e, n=n, bg=bg, sgi=sgi: e.activation(
                            out=sg[sgi][:, 0:n], in_=PS[bg][:, 0:n], func=AF.Silu),
                            reads=[("ps", bg)], writes=[("sg", sgi)])
                        P.op("dve", lambda e, n=n, bu=bu, sgi=sgi, jl=jl, t0=t0: e.tensor_tensor(
                            out=aT[:, jl, t0:t0 + n], in0=sg[sgi][:, 0:n], in1=PS[bu][:, 0:n], op=ALU.mult),
                            reads=[("sg", sgi), ("ps", bu)], writes=[("aT", jl, t0)])
                for tt in range(NT):
                    r = rows(tt)
                    akeys = [(f0) for (f0, fn) in FSTILES if f0 < tt * 128 + r and f0 + fn > tt * 128]
                    for dh in range(2):
                        bo = dh
                        bo = (tt % 2) * 2 + dh
                        for jl in range(11):
                            P.op("pe", lambda e, jl=jl, tt=tt, r=r, dh=dh, bo=bo: e.matmul(
                                PS[bo][0:r, :], lhsT=aT[:, jl, tt * 128:tt * 128 + r],
                                rhs=Wo[:, jl, dh * 512:(dh + 1) * 512], start=(jl == 0), stop=(jl == 10)),
                                reads=[("aT", jl, f0) for f0 in akeys] + [("Wo",)], writes=[("ps", bo)])
                        P.op("dve", lambda e, tt=tt, r=r, dh=dh, bo=bo: e.scalar_tensor_tensor(
                            out=X[0:r, tt, dh * 512:(dh + 1) * 512], in0=PS[bo][0:r, :], scalar=0.5, op0=ALU.mult,
                            in1=X[0:r, tt, dh * 512:(dh + 1) * 512], op1=ALU.add),
                            reads=[("ps", bo), ("X", tt)], writes=[("X", tt)])
                    if g == 1 and tile_epilogue is not None:
                        tile_epilogue(tt)

        for nm, t_sb, t_d in (("ones", ones_bf, ones_d), ("maskP", maskP, maskP_d), ("maskS", maskS, maskS_d),
                              ("resetP", resetP, resetP_d), ("resetS", resetS, resetS_d)):
            P.op("sp", lambda e, t_sb=t_sb, t_d=t_d: e.dma_start(out=t_sb[:], in_=t_d), writes=[(nm,)], dma=True)
        P.op("sp", lambda e: e.dma_start(out=cols[:, 0:4], in_=lb_logits[0].rearrange("(h p) -> p h", p=128)),
             writes=[("cols", "lb")], dma=True)
        P.op("sp", lambda e: e.dma_start(out=cols[:, 4:8], in_=lb_logits[1].rearrange("(h p) -> p h", p=128)),
             writes=[("cols", "l1")], dma=True)
        P.op("sp", lambda e: e.dma_start(out=cols[:, 12:13], in_=hgrn_norm[0].rearrange("(h p) -> p h", p=128)),
             writes=[("cols", "hn")], dma=True)
        P.op("sp", lambda e: e.dma_start(out=cols[:, 16:20], in_=gla_b_gk[0].rearrange("(h p) -> p h", p=128)),
             writes=[("cols", "bgk")], dma=True)
        P.op("sp", lambda e: e.dma_start(out=cols[:, 20:22], in_=gla_norm[0].rearrange("(h p) -> p h", p=128)),
             writes=[("cols", "gn")], dma=True)
        P.op("dve", lambda e: e.tensor_tensor(out=cols[:, 0:4], in0=cols[:, 0:4], in1=cols[:, 4:8], op=ALU.subtract),
             reads=[("cols", "lb"), ("cols", "l1")], writes=[("cols", "lb")])
        P.op("act", lambda e: e.activation(out=cols[:, 0:4], in_=cols[:, 0:4], func=AF.Sigmoid),
             reads=[("cols", "lb")], writes=[("cols", "lb")])
        P.op("dve", lambda e: e.tensor_scalar(out=cols[:, 8:12], in0=cols[:, 0:4], scalar1=-1.0, scalar2=1.0,
                                              op0=ALU.mult, op1=ALU.add),
             reads=[("cols", "lb")], writes=[("cols", "oml")])
        P.op("dve", lambda e: e.tensor_scalar(out=cols[:, 16:20], in0=cols[:, 16:20], scalar1=-1.0, scalar2=None,
                                              op0=ALU.mult),
             reads=[("cols", "bgk")], writes=[("cols", "bgk")])

        aTf = aT[:].rearrange("p a b -> p (a b)").bitcast(F32)
        Wob = Wo[:].rearrange("p a b -> p (a b)")

        def fA(i, w=512):
            return aTf[:, i * 512:i * 512 + w]
        QV, KV, LF, GG, T1, T2, O32a, O32b, GTa, GTb, RSTD = [fA(i) for i in range(11)]
        SST = aTf[:, 5632:6656].rearrange("p (h v) -> p h v", h=4)
        S0F = aTf[:, 6656:10752].rearrange("p (s v) -> p s v", s=16)
        QD = Wob[:, 0:512]
        KI = Wob[:, 512:1024]
        KE = Wob[:, 1024:1536]
        STm = [Wob[:, 1536:1664], Wob[:, 1664:1792], Wob[:, 11008:11136], Wob[:, 11136:11264]]
        SBFp = [Wob[:, 7424:7680], Wob[:, 7680:7936]]
        SST2 = [SST, aTf[:, 6656:7680].rearrange("p (h v) -> p h v", h=4)]
        VT = Wob[:, 1792:2816].rearrange("p (t v) -> p t v", t=4)
        KEt = Wob[:, 2816:3328].rearrange("p (t k) -> p t k", t=4)
        OAT = Wob[:, 3328:7424].rearrange("p (c n) -> p c n", c=8)
        SBF = Wob[:, 7424:8448].rearrange("p (h v) -> p h v", h=4)
        S0C = [Wob[:, 8448:8704], Wob[:, 8704:8960]]
        GKL = Wob[0:16, 8960:9472]
        WGK = Wob[0:16, 9472:9984]
        VBLK = Wob[0:64, 9984:11008].rearrange("p (s v) -> p s v", s=4)
        bankrr = [0]

        def nb():
            b = bankrr[0] % 6
            bankrr[0] += 1
            return b

        def K(*a):
            return ("aT", "mx") + a

        def KW(*a):
            return ("Wo", "mx") + a

        def barrier():
            P.op("dve", lambda e: e.memset(dummy[:, 0:1], 0.0), writes=[("aT",), ("Wo",), ("dummy",)])

        def proj_fm(wmat, col0, t0, n, ncols=128):
            sW = wload(wmat, col0, ncols)
            b = nb()
            tiles = [("hT", tt) for tt in range(t0 // 128, t0 // 128 + max(1, n // 128))]
            for c in range(8):
                P.op("pe", lambda e, c=c, b=b, sW=sW: e.matmul(
                    PS[b][0:ncols, 0:n], lhsT=Wr[sW][:, c, 0:ncols], rhs=hT[:, c, t0:t0 + n],
                    start=(c == 0), stop=(c == 7)), reads=[("Wr", sW)] + tiles, writes=[("ps", b)])
            return b

        def proj_tm(wmat, col0, t0, n, dst, dkey, func=AF.Copy):
            sW = wload(wmat, col0, 128)
            for ti in range(max(1, n // 128)):
                r = min(128, n)
                tt = t0 // 128 + ti
                b = nb()
                for c in range(8):
                    P.op("pe", lambda e, c=c, b=b, sW=sW, tt=tt, r=r: e.matmul(
                        PS[b][0:r, 0:128], lhsT=hT[:, c, tt * 128:tt * 128 + r], rhs=Wr[sW][:, c, :],
                        start=(c == 0), stop=(c == 7)), reads=[("Wr", sW), ("hT", tt)], writes=[("ps", b)])
                P.op("act", lambda e, b=b, ti=ti, r=r: e.activation(out=dst(ti, r), in_=PS[b][0:r, 0:128], func=func),
                     reads=[("ps", b)], writes=[dkey])

        def gla_core(si, t0, n, h, V, sample, s0_d, sp_d, ss_d, normcol0, oc0):
            nvc = V // 128
            reset = resetS if sample else resetP
            C = 4 if sample else 64
            nch = n // C
            P.op("dve", lambda e: e.tensor_tensor_scan(out=GG[:, 0:n], data0=reset[:, 0:n], data1=LF[:, 0:n],
                                                       initial=0.0, op0=ALU.mult, op1=ALU.add),
                 reads=[K("LF"), ("resetS" if sample else "resetP",)], writes=[K("GG")])
            P.op("act", lambda e: e.activation(out=T1[:, 0:n], in_=GG[:, 0:n], func=AF.Exp),
                 reads=[K("GG")], writes=[K("T1")])
            P.op("dve", lambda e: e.tensor_tensor(out=QD[:, 0:n], in0=QV[:, 0:n], in1=T1[:, 0:n], op=ALU.mult),
                 reads=[K("QV"), K("T1")], writes=[KW("QD")])
            P.op("act", lambda e: e.activation(out=T2[:, 0:n], in_=GG[:, 0:n], func=AF.Exp, scale=-1.0),
                 reads=[K("GG")], writes=[K("T2")])
            P.op("dve", lambda e: e.tensor_tensor(out=KI[:, 0:n], in0=KV[:, 0:n], in1=T2[:, 0:n], op=ALU.mult),
                 reads=[K("KV"), K("T2")], writes=[KW("KI")])
            g3 = GG[:, 0:n].rearrange("p (c j) -> p c j", j=C)
            P.op("dve", lambda e: e.tensor_tensor(out=T2[:, 0:n].rearrange("p (c j) -> p c j", j=C),
                                                  in0=g3[:, :, C - 1:C].to_broadcast([128, nch, C]), in1=g3,
                                                  op=ALU.subtract),
                 reads=[K("GG")], writes=[K("T2")])
            P.op("act", lambda e: e.activation(out=T2[:, 0:n], in_=T2[:, 0:n], func=AF.Exp),
                 reads=[K("T2")], writes=[K("T2")])
            P.op("dve", lambda e: e.tensor_tensor(out=KE[:, 0:n], in0=KV[:, 0:n], in1=T2[:, 0:n], op=ALU.mult),
                 reads=[K("KV"), K("T2")], writes=[KW("KE")])
            ntile = max(1, n // 128)
            r = min(128, n)
            for ti in range(ntile):
                b = nb()
                pv = PS[b][:].bitcast(BF16)
                P.op("pe", lambda e, ti=ti, pv=pv: e.transpose(out=pv[0:r, 0:128], in_=KE[:, ti * 128:ti * 128 + r],
                                                               identity=ident[:, :]),
                     reads=[KW("KE"), ("ident",)], writes=[("ps", b)])
                P.op("act", lambda e, ti=ti, pv=pv: e.activation(out=KEt[0:r, ti, :], in_=pv[0:r, 0:128], func=AF.Copy),
                     reads=[("ps", b)], writes=[KW("KEt", ti)])
            mask = maskS if sample else maskP
            if sample:
                P.op("sp", lambda e: e.dma_start(out=S0F[:, :, 0:V], in_=s0_d[:, h].rearrange("s k v -> k s v")),
                     writes=[K("S0F")], dma=True)
            if False:
                for ti in range(ntile):
                    c0 = ti * 128
                    b = nb()
                    P.op("pe", lambda e, b=b, c0=c0: e.matmul(PS[b][:, 0:128], lhsT=KI[:, c0:c0 + 128], rhs=QD[:, c0:c0 + 128],
                                                              start=True, stop=True),
                         reads=[KW("KI"), KW("QD")], writes=[("ps", b)])
                    P.op("dve", lambda e, b=b, ti=ti: e.tensor_tensor(out=STm[ti][:, 0:128], in0=PS[b][:, 0:128],
                                                                     in1=maskP[:, :], op=ALU.mult),
                         reads=[("ps", b), ("maskP",)], writes=[KW("ST", ti)])
                per = 512 // V
                ub = {}
                for c in range(2 * ntile):
                    ti, cc = divmod(c, 2)
                    if c % per == 0:
                        bU = nb()
                    col = (c % per) * V
                    P.op("pe", lambda e, bU=bU, col=col, cc=cc, ti=ti, c=c: e.matmul(
                        PS[bU][:, col:col + V], lhsT=KEt[cc * 64:cc * 64 + 64, ti, :], rhs=VT[cc * 64:cc * 64 + 64, ti, 0:V],
                        start=(c % per == 0), stop=True, skip_group_check=True),
                        reads=[KW("KEt", ti), KW("VT", ti)], writes=[("ps", bU)])
                    ub[c] = (bU, col)
                P.op("act", lambda e: e.activation(out=SBFp[0][:, 0:V], in_=SST2[0][:, h, 0:V], func=AF.Copy),
                     reads=[K("SST", 0, h)], writes=[KW("SBFp", 0)])
                for c in range(2 * ntile):
                    ti, cc = divmod(c, 2)
                    q0 = c * 64
                    if cc == 0:
                        bo = nb()
                        for vc in range(nvc):
                            P.op("pe", lambda e, bo=bo, ti=ti, vc=vc: e.matmul(
                                PS[bo][:, vc * 128:vc * 128 + 128], lhsT=VT[:, ti, vc * 128:(vc + 1) * 128], rhs=STm[ti][:, 0:128],
                                start=(vc == 0), stop=False, skip_group_check=True),
                                reads=[KW("VT", ti), KW("ST", ti)], writes=[("ps", bo)])
                    for vc in range(nvc):
                        P.op("pe", lambda e, vc=vc, q0=q0, cc=cc, bo=bo, c=c: e.matmul(
                            PS[bo][:, vc * 128 + cc * 64:vc * 128 + cc * 64 + 64],
                            lhsT=SBFp[c % 2][:, vc * 128:(vc + 1) * 128], rhs=QD[:, q0:q0 + 64],
                            start=False, stop=True, skip_group_check=True),
                            reads=[KW("SBFp", c % 2), KW("QD")], writes=[("ps", bo)])
                    bU, col = ub[c]
                    P.op("dve", lambda e, bU=bU, col=col, q0=q0, c=c: e.scalar_tensor_tensor(
                        out=SST2[(c + 1) % 2][:, h, 0:V], in0=SST2[c % 2][:, h, 0:V], scalar=T1[:, q0 + 63:q0 + 64], op0=ALU.mult,
                        in1=PS[bU][:, col:col + V], op1=ALU.add),
                        reads=[K("SST", c % 2, h), K("T1"), ("ps", bU)], writes=[K("SST", (c + 1) % 2, h)])
                    if c < 2 * ntile - 1:
                        P.op("act", lambda e, c=c: e.activation(out=SBFp[(c + 1) % 2][:, 0:V], in_=SST2[(c + 1) % 2][:, h, 0:V],
                                                              func=AF.Copy),
                             reads=[K("SST", (c + 1) % 2, h)], writes=[KW("SBFp", (c + 1) % 2)])
                    if cc == 1:
                        c0 = ti * 128
                        for vc, o32 in zip(range(nvc), (O32a, O32b)):
                            P.op("act", lambda e, vc=vc, o32=o32, c0=c0, bo=bo: e.activation(
                                out=o32[:, c0:c0 + 128], in_=PS[bo][:, vc * 128:vc * 128 + 128], func=AF.Copy),
                                reads=[("ps", bo)], writes=[K("O32", vc, ti)])
            if not sample:
                for ti in range(ntile):
                    c0 = ti * 128
                    b = nb()
                    P.op("pe", lambda e, b=b, c0=c0: e.matmul(PS[b][:, 0:128], lhsT=KI[:, c0:c0 + 128], rhs=QD[:, c0:c0 + 128],
                                                              start=True, stop=True),
                         reads=[KW("KI"), KW("QD")], writes=[("ps", b)])
                    P.op("dve", lambda e, b=b, ti=ti: e.tensor_tensor(out=STm[ti][:, 0:128], in0=PS[b][:, 0:128],
                                                                     in1=maskP[:, :], op=ALU.mult),
                         reads=[("ps", b), ("maskP",)], writes=[KW("ST", ti)])
                per = 512 // V
                ub = {}
                ubank = {}
                for c in range(2 * ntile):
                    ti, cc = divmod(c, 2)
                    slot = (cc, ti // per)
                    if slot not in ubank:
                        ubank[slot] = nb()
                    bU = ubank[slot]
                    col = (ti % per) * V
                    P.op("pe", lambda e, bU=bU, col=col, cc=cc, ti=ti: e.matmul(
                        PS[bU][:, col:col + V], lhsT=KEt[cc * 64:cc * 64 + 64, ti, :], rhs=VT[cc * 64:cc * 64 + 64, ti, 0:V],
                        start=(ti % per == 0), stop=True, skip_group_check=True),
                        reads=[KW("KEt", ti), KW("VT", ti)], writes=[("ps", bU)])
                    ub[c] = (bU, col)
                P.op("act", lambda e: e.activation(out=SBFp[0][:, 0:V], in_=SST2[0][:, h, 0:V], func=AF.Copy),
                     reads=[K("SST", 0, h)], writes=[KW("SBFp", 0)])
                for c in range(2 * ntile):
                    ti, cc = divmod(c, 2)
                    q0 = c * 64
                    if cc == 0:
                        bo = nb()
                        for vc in range(nvc):
                            P.op("pe", lambda e, bo=bo, ti=ti, vc=vc: e.matmul(
                                PS[bo][:, vc * 128:vc * 128 + 128], lhsT=VT[:, ti, vc * 128:(vc + 1) * 128], rhs=STm[ti][:, 0:128],
                                start=(vc == 0), stop=False, skip_group_check=True),
                                reads=[KW("VT", ti), KW("ST", ti)], writes=[("ps", bo)])
                    for vc in range(nvc):
                        P.op("pe", lambda e, vc=vc, q0=q0, cc=cc, bo=bo, c=c: e.matmul(
                            PS[bo][:, vc * 128 + cc * 64:vc * 128 + cc * 64 + 64],
                            lhsT=SBFp[c % 2][:, vc * 128:(vc + 1) * 128], rhs=QD[:, q0:q0 + 64],
                            start=False, stop=True, skip_group_check=True),
                            reads=[KW("SBFp", c % 2), KW("QD")], writes=[("ps", bo)])
                    bU, col = ub[c]
                    P.op("dve", lambda e, bU=bU, col=col, q0=q0, c=c: e.scalar_tensor_tensor(
                        out=SST2[(c + 1) % 2][:, h, 0:V], in0=SST2[c % 2][:, h, 0:V], scalar=T1[:, q0 + 63:q0 + 64], op0=ALU.mult,
                        in1=PS[bU][:, col:col + V], op1=ALU.add),
                        reads=[K("SST", c % 2, h), K("T1"), ("ps", bU)], writes=[K("SST", (c + 1) % 2, h)])
                    if c < 2 * ntile - 1:
                        P.op("act", lambda e, c=c: e.activation(out=SBFp[(c + 1) % 2][:, 0:V], in_=SST2[(c + 1) % 2][:, h, 0:V],
                                                              func=AF.Copy),
                             reads=[K("SST", (c + 1) % 2, h)], writes=[KW("SBFp", (c + 1) % 2)])
                    if cc == 1:
                        c0 = ti * 128
                        for vc, o32 in zip(range(nvc), (O32a, O32b)):
                            P.op("act", lambda e, vc=vc, o32=o32, c0=c0, bo=bo: e.activation(
                                out=o32[:, c0:c0 + 128], in_=PS[bo][:, vc * 128:vc * 128 + 128], func=AF.Copy),
                                reads=[("ps", bo)], writes=[K("O32", vc, ti)])
            for ti in (range(ntile) if sample else ()):
                c0 = ti * 128
                b = nb()
                P.op("pe", lambda e, b=b, c0=c0: e.matmul(PS[b][0:r, 0:r], lhsT=KI[:, c0:c0 + r], rhs=QD[:, c0:c0 + r],
                                                          start=True, stop=True),
                     reads=[KW("KI"), KW("QD")], writes=[("ps", b)])
                sm = STm[ti % 2]
                P.op("dve", lambda e, b=b, sm=sm: e.tensor_tensor(out=sm[0:r, 0:r], in0=PS[b][0:r, 0:r],
                                                                 in1=mask[0:r, 0:r], op=ALU.mult),
                     reads=[("ps", b), ("maskS" if sample else "maskP",)], writes=[KW("ST", ti % 2)])
                bo = nb()
                for vc in range(nvc):
                    ov = PS[bo][:, vc * 128:vc * 128 + r]
                    P.op("pe", lambda e, ov=ov, ti=ti, vc=vc, sm=sm: e.matmul(
                        ov, lhsT=VT[0:r, ti, vc * 128:(vc + 1) * 128], rhs=sm[0:r, 0:r], start=(vc == 0), stop=False,
                        skip_group_check=True),
                        reads=[KW("VT", ti), KW("ST", ti % 2)], writes=[("ps", bo)])
                if not sample:
                    for cc in range(2):
                        q0 = c0 + cc * 64
                        for vc in range(nvc):
                            P.op("pe", lambda e, vc=vc, q0=q0, cc=cc, bo=bo: e.matmul(
                                PS[bo][:, vc * 128 + cc * 64:vc * 128 + cc * 64 + 64],
                                lhsT=SBF[:, h, vc * 128:(vc + 1) * 128], rhs=QD[:, q0:q0 + 64],
                                start=False, stop=True, skip_group_check=True),
                                reads=[KW("SBF", h), KW("QD")], writes=[("ps", bo)])
                        bs = nb()
                        P.op("pe", lambda e, bs=bs, cc=cc, ti=ti: e.matmul(
                            PS[bs][:, 0:V], lhsT=KEt[cc * 64:cc * 64 + 64, ti, :], rhs=VT[cc * 64:cc * 64 + 64, ti, 0:V],
                            start=True, stop=True), reads=[KW("KEt", ti), KW("VT", ti)], writes=[("ps", bs)])
                        P.op("dve", lambda e, bs=bs, q0=q0: e.scalar_tensor_tensor(
                            out=SST[:, h, 0:V], in0=SST[:, h, 0:V], scalar=T1[:, q0 + 63:q0 + 64], op0=ALU.mult,
                            in1=PS[bs][:, 0:V], op1=ALU.add),
                            reads=[K("SST", 0, h), K("T1"), ("ps", bs)], writes=[K("SST", 0, h)])
                        P.op("act", lambda e: e.activation(out=SBF[:, h, 0:V], in_=SST[:, h, 0:V], func=AF.Copy),
                             reads=[K("SST", 0, h)], writes=[KW("SBF", h)])
                else:
                    nper = 1024 // V
                    for q0_ in range(0, 16, nper):
                        hb_ = (q0_ // nper) % 2
                        P.op("act", lambda e, q0_=q0_, hb_=hb_: e.activation(
                            out=hn[hb_][:, 0:nper * V].rearrange("p (s v) -> p s v", v=V), in_=S0F[:, q0_:q0_ + nper, 0:V], func=AF.Copy),
                            reads=[K("S0F")], writes=[("hn", hb_)])
                        for sq in range(q0_, q0_ + nper):
                            for vc in range(nvc):
                                o_ = (sq - q0_) * V + vc * 128
                                P.op("pe", lambda e, vc=vc, sq=sq, hb_=hb_, o_=o_, bo=bo: e.matmul(
                                    PS[bo][:, vc * 128 + sq * 4:vc * 128 + sq * 4 + 4],
                                    lhsT=hn[hb_][:, o_:o_ + 128], rhs=QD[:, sq * 4:sq * 4 + 4],
                                    start=False, stop=True, skip_group_check=True),
                                    reads=[("hn", hb_), KW("QD")], writes=[("ps", bo)])
                for vc, o32 in zip(range(nvc), (O32a, O32b)):
                    P.op("act", lambda e, vc=vc, o32=o32, c0=c0, bo=bo: e.activation(
                        out=o32[:, c0:c0 + r], in_=PS[bo][:, vc * 128:vc * 128 + r], func=AF.Copy),
                        reads=[("ps", bo)], writes=[K("O32", vc, ti)])
            if sample:
                for q4 in range(4):
                    P.op("dve", lambda e, q4=q4: e.tensor_tensor(
                        out=VBLK[:, :, 0:V], in0=VT[0:64, 0:1, 0:V].to_broadcast([64, 4, V]),
                        in1=maskS[0:64, 64 + q4 * 4:64 + q4 * 4 + 4].unsqueeze(2).to_broadcast([64, 4, V]), op=ALU.mult),
                        reads=[KW("VT", 0), ("maskS",)], writes=[KW("VBLK")])
                    for s4 in range(4):
                        sq = q4 * 4 + s4
                        bs = nb()
                        P.op("pe", lambda e, bs=bs, s4=s4: e.matmul(
                            PS[bs][:, 0:V], lhsT=KEt[0:64, 0, :], rhs=VBLK[:, s4, 0:V], start=True, stop=True),
                            reads=[KW("KEt", 0), KW("VBLK")], writes=[("ps", bs)])
                        P.op("dve", lambda e, bs=bs, sq=sq: e.scalar_tensor_tensor(
                            out=S0F[:, sq, 0:V], in0=S0F[:, sq, 0:V], scalar=T1[:, sq * 4 + 3:sq * 4 + 4], op0=ALU.mult,
                            in1=PS[bs][:, 0:V], op1=ALU.add),
                            reads=[K("S0F"), K("T1"), ("ps", bs)], writes=[K("S0F")])
                P.op("sp", lambda e: e.dma_start(out=ss_d[:, h].rearrange("s k v -> k s v"), in_=S0F[:, :, 0:V]),
                     reads=[K("S0F")], dma=True)
            bq = nb()
            for vc, o32 in zip(range(nvc), (O32a, O32b)):
                sqb, sqk = (QD, KW("QD")) if vc == 0 else (KI, KW("KI"))
                P.op("act", lambda e, o32=o32, sqb=sqb: e.activation(out=sqb[:, 0:n], in_=o32[:, 0:n], func=AF.Square),
                     reads=[K("O32", vc)], writes=[sqk])
                P.op("pe", lambda e, vc=vc, sqb=sqb: e.matmul(PS[bq][:, 0:n], lhsT=ones_bf[:, :], rhs=sqb[:, 0:n],
                                                              start=(vc == 0), stop=(vc == nvc - 1)),
                     reads=[sqk, ("ones",)], writes=[("ps", bq)])
            P.op("dve", lambda e: e.tensor_scalar(out=RSTD[:, 0:n], in0=PS[bq][:, 0:n], scalar1=1.0 / V, scalar2=EPS,
                                                  op0=ALU.mult, op1=ALU.add),
                 reads=[("ps", bq)], writes=[K("RSTD")])
            P.op("act", lambda e: e.activation(out=RSTD[:, 0:n], in_=RSTD[:, 0:n], func=AF.Ln),
                 reads=[K("RSTD")], writes=[K("RSTD")])
            P.op("act", lambda e: e.activation(out=RSTD[:, 0:n], in_=RSTD[:, 0:n], func=AF.Exp, scale=-0.5),
                 reads=[K("RSTD")], writes=[K("RSTD")])
            for vc, o32, gt in zip(range(nvc), (O32a, O32b), (GTa, GTb)):
                P.op("dve", lambda e, o32=o32, vc=vc: e.scalar_tensor_tensor(
                    out=o32[:, 0:n], in0=o32[:, 0:n], scalar=cols[:, normcol0 + vc:normcol0 + vc + 1], op0=ALU.mult,
                    in1=RSTD[:, 0:n], op1=ALU.mult),
                    reads=[K("O32", vc), K("RSTD"), ("cols",)], writes=[K("O32", vc)])
                P.op("dve", lambda e, o32=o32, gt=gt, vc=vc: e.tensor_tensor(
                    out=OAT[:, oc0 + vc, 0:n], in0=o32[:, 0:n], in1=gt[:, 0:n], op=ALU.mult),
                    reads=[K("O32", vc), K("GT", vc)], writes=[KW("OAT", oc0 + vc)])

        def out_proj(w_out, t0, n, nfc):
            slots = []
            for fc in range(nfc):
                s_ = wslot[0] % 8
                wslot[0] += 1
                P.op("pool", lambda e, s_=s_, fc=fc: e.dma_start(
                    out=Wr[s_][:].rearrange("p c n -> p (c n)"), in_=w_out[fc * 128:(fc + 1) * 128, :]),
                    writes=[("Wr", s_)], dma=True)
                slots.append(s_)
            for ti in range(max(1, n // 128)):
                r = min(128, n)
                tt = t0 // 128 + ti
                for dh in range(2):
                    b = nb()
                    for fc in range(nfc):
                        P.op("pe", lambda e, fc=fc, b=b, ti=ti, dh=dh: e.matmul(
                            PS[b][0:r, :], lhsT=OAT[:, fc, ti * 128:ti * 128 + r],
                            rhs=Wr[slots[fc]][:].rearrange("p c n -> p (c n)")[:, dh * 512:(dh + 1) * 512],
                            start=(fc == 0), stop=(fc == nfc - 1)),
                            reads=[KW("OAT", fc), ("Wr", slots[fc])], writes=[("ps", b)])
                    P.op("dve", lambda e, b=b, tt=tt, dh=dh: e.tensor_tensor(
                        out=X[0:r, tt, dh * 512:(dh + 1) * 512], in0=PS[b][0:r, :],
                        in1=X[0:r, tt, dh * 512:(dh + 1) * 512], op=ALU.add),
                        reads=[("ps", b), ("X", tt)], writes=[("X", tt)])

        def evac(func, dst, dkey, b, n, rows_=128, **kw):
            P.op("act", lambda e: e.activation(out=dst, in_=PS[b][0:rows_, 0:n], func=func, **kw),
                 reads=[("ps", b)], writes=[dkey])

        def hgrn_head(si, t0, n, h, sample):
            W = ab_w_in[0]
            b = proj_fm(W, h * 128, t0, n)
            evac(AF.Silu, QV[:, 0:n], K("QV"), b, n)
            b = proj_fm(W, 1536 + h * 128, t0, n)
            evac(AF.Silu, GTa[:, 0:n], K("GT", 0), b, n)
            b = proj_fm(W, 512 + h * 128, t0, n)
            evac(AF.Exp, T1[:, 0:n], K("T1"), b, n, scale=-1.0)
            P.op("dve", lambda e: e.tensor_scalar(out=T1[:, 0:n], in0=T1[:, 0:n], scalar1=1.0, scalar2=None, op0=ALU.add),
                 reads=[K("T1")], writes=[K("T1")])
            P.op("act", lambda e: e.activation(out=T1[:, 0:n], in_=T1[:, 0:n], func=AF.Ln),
                 reads=[K("T1")], writes=[K("T1")])
            P.op("act", lambda e: e.activation(out=T1[:, 0:n], in_=T1[:, 0:n], func=AF.Exp, scale=-1.0),
                 reads=[K("T1")], writes=[K("T1")])
            P.op("dve", lambda e: e.tensor_scalar(out=T1[:, 0:n], in0=T1[:, 0:n], scalar1=cols[:, 8 + h:9 + h],
                                                  scalar2=cols[:, h:h + 1], op0=ALU.mult, op1=ALU.add),
                 reads=[K("T1"), ("cols",)], writes=[K("T1")])
            P.op("dve", lambda e: e.tensor_scalar(out=KV[:, 0:n], in0=T1[:, 0:n], scalar1=-1.0, scalar2=1.0,
                                                  op0=ALU.mult, op1=ALU.add),
                 reads=[K("T1")], writes=[K("KV")])
            P.op("act", lambda e: e.activation(out=LF[:, 0:n], in_=T1[:, 0:n], func=AF.Ln),
                 reads=[K("T1")], writes=[K("LF")])
            proj_tm(W, 1024 + h * 128, t0, n, lambda ti, r: VT[0:r, ti, 0:128], KW("VT"))
            gla_core(si, t0, n, h, 128, sample, hs0, hgp, hgs, 12, h)

        identF = CST[:, 0:128]
        onesF = CST[:, 128:256]
        blkS = CST[0:64, 256:320]
        lastS = CST[0:64, 320:336]
        colA = CST[:, 336:337]
        colB = CST[:, 337:338]
        CW = cols[:, 24:56].rearrange("p (c k) -> p c k", k=4)
        CB = cols[:, 56:64]
        DTB, AROW, DROW, SNW = cols2[:, 0:8], cols2[:, 8:16], cols2[:, 16:24], cols2[:, 24:28]
        MS = aTf[:, 11104:11616]
        MSB = Wob[:, 8448:8960]

        def mamba_setup():
            P.op("sp", lambda e: e.dma_start(out=CST[:], in_=cst_d), writes=[("cst",)], dma=True)
            for k in range(4):
                P.op("sp", lambda e, k=k: e.dma_start(out=CW[:, :, k], in_=conv_w[0, k].rearrange("(c p) -> p c", p=128)),
                     writes=[("cols", "cw", k)], dma=True)
            P.op("sp", lambda e: e.dma_start(out=CB, in_=conv_b[0].rearrange("(c p) -> p c", p=128)),
                 writes=[("cols", "cb")], dma=True)
            P.op("sp", lambda e: e.dma_start(out=DTB, in_=dt_bias[0].partition_broadcast(128)), writes=[("cols2", "dtb")], dma=True)
            P.op("sp", lambda e: e.dma_start(out=AROW, in_=a_log[0].partition_broadcast(128)), writes=[("cols2", "a")], dma=True)
            P.op("sp", lambda e: e.dma_start(out=DROW, in_=ssm_d[0].partition_broadcast(128)), writes=[("cols2", "d")], dma=True)
            P.op("sp", lambda e: e.dma_start(out=SNW, in_=ssm_norm[0].rearrange("(c p) -> p c", p=128)),
                 writes=[("cols2", "snw")], dma=True)
            P.op("act", lambda e: e.activation(out=AROW, in_=AROW, func=AF.Exp), reads=[("cols2", "a")], writes=[("cols2", "a")])
            P.op("dve", lambda e: e.tensor_scalar(out=AROW, in0=AROW, scalar1=-1.0, scalar2=None, op0=ALU.mult),
                 reads=[("cols2", "a")], writes=[("cols2", "a")])
            P.op("dve", lambda e: e.memset(TAIL[:, :, :], 0.0), writes=[("tail",)])

        def KM(*a):
            return ("aT", "mx", "m") + a

        def KWM(*a):
            return ("Wo", "mx", "m") + a

        def mamba_tile_group(si, t0, n):
            sample = (si == 4)
            W = ab_w_in[0]
            r = min(128, n)
            ntile = max(1, n // 128)
            barrier()
            if si == 0:
                P.op("dve", lambda e: e.memset(MS, 0.0), writes=[KM("MS")])
                P.op("dve", lambda e: e.memset(MSB, 0.0), writes=[KWM("MSB")])
            fa = [0]

            def takeF(k, lo=None):
                o = fa[0]
                fa[0] += k
                assert fa[0] <= 5632
                return aTf[:, o:o + k]
            CONVX = takeF(2048).rearrange("p (c n) -> p c n", c=4)
            A1 = takeF(512)
            A2 = takeF(512)
            ZS = takeF(2048).rearrange("p (t f) -> p t f", t=4)
            XTOK = takeF(512)
            ZSf = ZS.rearrange("p t f -> p (t f)")
            CS0T = ZSf[0:48, 0:1024]
            CVOUT = ZSf[0:48, 1024:2048]
            fb = [6656]

            def takeG(k):
                o = fb[0]
                fb[0] += k
                assert fb[0] <= 11104
                return aTf[:, o:o + k]
            XB = takeG(8 * 520).rearrange("p (c n) -> p c n", c=8)
            wa = [0]

            def takeW(k):
                o = wa[0]
                wa[0] += k
                assert wa[0] <= 3328
                return Wob[:, o:o + k]
            BT = takeW(2 * n).rearrange("p (g n) -> p g n", g=2)
            CT = takeW(2 * n).rearrange("p (g n) -> p g n", g=2)
            BTOK = takeW(256)
            XDT = takeW(512)
            XEND = takeW(512)
            wb = [8960]

            def takeW2(k):
                o = wb[0]
                wb[0] += k
                assert wb[0] <= 11264
                return Wob[:, o:o + k]
            _mm = takeW2(512)
            MM4 = [_mm, _mm]
            YN = takeW2(512)
            CTm = takeW2(512).rearrange("p (g c i) -> p g c i", g=2, c=2)

            for c in range(8):
                b = proj_fm(W, 2560 + c * 128, t0, n)
                if not sample:
                    P.op("dve", lambda e, c=c: e.tensor_copy(out=XB[:, c, 0:3], in_=TAIL[:, c, 0:3]),
                         reads=[("tail", c)], writes=[KM("XB", c)])
                    P.op("act", lambda e, c=c, b=b: e.activation(out=XB[:, c, 3:3 + n], in_=PS[b][:, 0:n], func=AF.Copy),
                         reads=[("ps", b)], writes=[KM("XB", c)])
                    P.op("dve", lambda e, c=c: e.tensor_copy(out=TAIL[:, c, 0:3], in_=XB[:, c, n:n + 3]),
                         reads=[KM("XB", c)], writes=[("tail", c)])
                    if si == 3:
                        P.op("sp", lambda e, c=c: e.dma_start(out=cvp[:, c * 128:(c + 1) * 128].rearrange("j f -> f j"),
                                                              in_=XB[:, c, n:n + 3]), reads=[KM("XB", c)], dma=True)
                    xin = lambda k, c=c: XB[:, c, k:k + n]
                    AC, ack = (A1, KM("A1")) if c % 2 == 0 else (A2, KM("A2"))
                    acc = AC[:, 0:n]
                else:
                    xb3 = XB[:, c, 0:112].rearrange("p (s t) -> p s t", t=7)
                    if c == 0:
                        P.op("sp", lambda e: e.dma_start(out=CS0T, in_=cs0.rearrange("s j f -> (s j) f")),
                             writes=[KM("ZS")], dma=True)
                    bc = nb()
                    P.op("pe", lambda e, c=c, bc=bc: e.transpose(out=PS[bc][:, 0:48], in_=CS0T[:, c * 128:(c + 1) * 128],
                                                                 identity=identF[0:48, 0:48]),
                         reads=[KM("ZS"), ("cst",)], writes=[("ps", bc)])
                    P.op("act", lambda e, bc=bc, xb3=xb3: e.activation(
                        out=xb3[:, :, 0:3], in_=PS[bc][:, 0:48].rearrange("p (s t) -> p s t", t=3), func=AF.Copy),
                        reads=[("ps", bc)], writes=[KM("XB", c)])
                    P.op("act", lambda e, c=c, b=b, xb3=xb3: e.activation(
                        out=xb3[:, :, 3:7], in_=PS[b][:, 0:64].rearrange("p (s t) -> p s t", t=4), func=AF.Copy),
                        reads=[("ps", b)], writes=[KM("XB", c)])
                    P.op("dve", lambda e, xb3=xb3: e.tensor_copy(out=A2[:, 0:48].rearrange("p (s t) -> p s t", t=3), in_=xb3[:, :, 4:7]),
                         reads=[KM("XB", c)], writes=[KM("A2")])
                    P.op("pe", lambda e, c=c: e.transpose(out=PS[6 + c // 4][0:48, (c % 4) * 128:(c % 4) * 128 + 128], in_=A2[:, 0:48],
                                                          identity=identF),
                         reads=[KM("A2"), ("cst",)], writes=[("ps", 6 + c // 4)])
                    if c == 7:
                        for hb_ in range(2):
                            P.op("act", lambda e, hb_=hb_: e.activation(out=CVOUT[:, hb_ * 512:(hb_ + 1) * 512],
                                                                        in_=PS[6 + hb_][0:48, :], func=AF.Copy),
                                 reads=[("ps", 6 + hb_)], writes=[KM("ZS")])
                        P.op("sp", lambda e: e.dma_start(out=cvs.rearrange("s j f -> (s j) f"), in_=CVOUT),
                             reads=[KM("ZS")], dma=True)
                    xin = lambda k, xb3=xb3: xb3[:, :, k:k + 4]
                    AC, ack = A1, KM("A1")
                    acc = A1[:, 0:64].rearrange("p (s t) -> p s t", t=4)
                P.op("dve", lambda e, c=c, xin=xin, acc=acc: e.tensor_scalar(
                    out=acc, in0=xin(0), scalar1=CW[:, c, 0:1], scalar2=CB[:, c:c + 1], op0=ALU.mult, op1=ALU.add),
                    reads=[KM("XB", c), ("cols",)], writes=[ack])
                for k in range(1, 4):
                    P.op("dve", lambda e, c=c, k=k, xin=xin, acc=acc: e.scalar_tensor_tensor(
                        out=acc, in0=xin(k), scalar=CW[:, c, k:k + 1], op0=ALU.mult, in1=acc, op1=ALU.add),
                        reads=[KM("XB", c), ack, ("cols",)], writes=[ack])
                if c < 4:
                    dst, dk = CONVX[:, c, 0:n], KM("CONVX", c)
                elif c < 6:
                    dst, dk = BT[:, c - 4, 0:n], KWM("BT", c - 4)
                else:
                    dst, dk = CT[:, c - 6, 0:n], KWM("CT", c - 6)
                P.op("act", lambda e, dst=dst, AC=AC: e.activation(out=dst, in_=AC[:, 0:n], func=AF.Silu),
                     reads=[ack], writes=[dk])
            for zc in range(4):
                proj_tm(W, 2048 + zc * 128, t0, n, lambda ti, rr, zc=zc: ZS[0:rr, ti, zc * 128:(zc + 1) * 128],
                        KM("ZS"), func=AF.Silu)
            sDT = wload(W, 3584, 8)
            barrier()
            fb[0] = 6656
            CBm = [takeG(128), takeG(128)]
            DG4 = [takeG(4 * r), takeG(4 * r)]
            DM4 = [takeG(4 * r), takeG(4 * r)]
            if sample:
                SNAT = [takeG(512).rearrange("p (a n) -> p a n", a=4) for _ in range(2)]
                SNEW = [takeG(512).rearrange("p (a n) -> p a n", a=4) for _ in range(2)]
                ETR = takeG(512)
                DECP = takeG(64).rearrange("p (a s) -> p a s", a=4)
                XENDm = takeW(512)
                S0T = [takeW(512), MSB]
            mask = maskS if sample else maskP
            mkey = ("maskS",) if sample else ("maskP",)

            if sample:
                XTOKs, A1s, XENDs, BTOKs = [XTOK, XTOK], [A1, A1], [XEND, XEND], [BTOK, BTOK]
            else:
                XTOKs, A1s = [XTOK, takeG(512)], [A1, takeG(512)]
                XENDs, BTOKs = [XEND, takeW2(512)], [BTOK, takeW2(256)]
            Wd = ntile * 8
            DTw, DTAw, CUMw, TOTw, ECUMw, EENDw, TAw, TBw, DECAw, DECBw, DTMw = [takeG(Wd) for _ in range(11)]
            SSQ, RSQ = takeG(8), takeG(8)
            v3 = lambda a: a[0:r, 0:Wd].rearrange("p (t h) -> p t h", h=8)
            b = nb()
            for ti in range(ntile):
                tt = t0 // 128 + ti
                for c in range(8):
                    P.op("pe", lambda e, c=c, b=b, tt=tt, ti=ti: e.matmul(
                        PS[b][0:r, ti * 8:(ti + 1) * 8], lhsT=hT[:, c, tt * 128:tt * 128 + r], rhs=Wr[sDT][:, c, 0:8],
                        start=(c == 0 and ti == 0), stop=(c == 7), skip_group_check=True),
                        reads=[("Wr", sDT), ("hT", tt)], writes=[("ps", b)])
            P.op("dve", lambda e, b=b: e.tensor_tensor(out=v3(DTw), in0=PS[b][0:r, 0:Wd].rearrange("p (t h) -> p t h", h=8),
                                                       in1=DTB[0:r].unsqueeze(1).to_broadcast([r, ntile, 8]), op=ALU.add),
                 reads=[("ps", b), ("cols2",)], writes=[KM("DT")])
            P.op("act", lambda e: e.activation(out=DTw[0:r, 0:Wd], in_=DTw[0:r, 0:Wd], func=AF.Exp), reads=[KM("DT")], writes=[KM("DT")])
            P.op("dve", lambda e: e.tensor_scalar(out=DTw[0:r, 0:Wd], in0=DTw[0:r, 0:Wd], scalar1=1.0, scalar2=None, op0=ALU.add),
                 reads=[KM("DT")], writes=[KM("DT")])
            P.op("act", lambda e: e.activation(out=DTw[0:r, 0:Wd], in_=DTw[0:r, 0:Wd], func=AF.Ln), reads=[KM("DT")], writes=[KM("DT")])
            P.op("dve", lambda e: e.tensor_tensor(out=v3(DTAw), in0=v3(DTw), in1=AROW[0:r].unsqueeze(1).to_broadcast([r, ntile, 8]),
                                                  op=ALU.mult), reads=[KM("DT"), ("cols2",)], writes=[KM("DTA")])
            b = nb()
            P.op("pe", lambda e, b=b: e.matmul(PS[b][0:r, 0:Wd], lhsT=mask[0:r, 0:r], rhs=DTAw[0:r, 0:Wd], start=True, stop=True),
                 reads=[mkey, KM("DTA")], writes=[("ps", b)])
            P.op("act", lambda e, b=b: e.activation(out=CUMw[0:r, 0:Wd], in_=PS[b][0:r, 0:Wd], func=AF.Copy),
                 reads=[("ps", b)], writes=[KM("CUM")])
            if not sample:
                for (cm, TX, DECX) in ((colA, TAw, DECAw), (colB, TBw, DECBw)):
                    P.op("dve", lambda e, cm=cm: e.tensor_scalar(out=DTMw[:, 0:Wd], in0=DTAw[:, 0:Wd], scalar1=cm, scalar2=None, op0=ALU.mult),
                         reads=[KM("DTA"), ("cst",)], writes=[KM("DTM")])
                    b = nb()
                    P.op("pe", lambda e, b=b: e.matmul(PS[b][:, 0:Wd], lhsT=onesF, rhs=DTMw[:, 0:Wd], start=True, stop=True),
                         reads=[("cst",), KM("DTM")], writes=[("ps", b)])
                    P.op("act", lambda e, b=b, TX=TX: e.activation(out=TX[:, 0:Wd], in_=PS[b][:, 0:Wd], func=AF.Copy),
                         reads=[("ps", b)], writes=[KM("TX")])
                    P.op("act", lambda e, TX=TX, DECX=DECX: e.activation(out=DECX[:, 0:Wd], in_=TX[:, 0:Wd], func=AF.Exp),
                         reads=[KM("TX")], writes=[KM("DEC")])
                P.op("dve", lambda e: e.tensor_scalar(out=TOTw[:, 0:Wd], in0=TAw[:, 0:Wd], scalar1=colA, scalar2=None, op0=ALU.mult),
                     reads=[KM("TX"), ("cst",)], writes=[KM("TOT")])
                P.op("dve", lambda e: e.scalar_tensor_tensor(out=TOTw[:, 0:Wd], in0=TBw[:, 0:Wd], scalar=colB, op0=ALU.mult,
                                                             in1=TOTw[:, 0:Wd], op1=ALU.add),
                     reads=[KM("TX"), KM("TOT"), ("cst",)], writes=[KM("TOT")])
            else:
                b = nb()
                P.op("pe", lambda e, b=b: e.matmul(PS[b][0:64, 0:8], lhsT=blkS, rhs=DTAw[0:64, 0:8], start=True, stop=True),
                     reads=[("cst",), KM("DTA")], writes=[("ps", b)])
                P.op("act", lambda e, b=b: e.activation(out=TOTw[0:64, 0:8], in_=PS[b][0:64, 0:8], func=AF.Copy),
                     reads=[("ps", b)], writes=[KM("TOT")])
            P.op("act", lambda e: e.activation(out=ECUMw[0:r, 0:Wd], in_=CUMw[0:r, 0:Wd], func=AF.Exp), reads=[KM("CUM")], writes=[KM("ECUM")])
            P.op("dve", lambda e: e.tensor_tensor(out=EENDw[0:r, 0:Wd], in0=TOTw[0:r, 0:Wd], in1=CUMw[0:r, 0:Wd], op=ALU.subtract),
                 reads=[KM("TOT"), KM("CUM")], writes=[KM("EEND")])
            P.op("act", lambda e: e.activation(out=EENDw[0:r, 0:Wd], in_=EENDw[0:r, 0:Wd], func=AF.Exp), reads=[KM("EEND")], writes=[KM("EEND")])

            def _tile(ti, part):
                tt = t0 // 128 + ti
                c0 = ti * 128
                DT, DTA, CUM, TOT, ECUM, EEND, TA, TB, DECA, DECB = [a_[:, ti * 8:(ti + 1) * 8] for a_ in
                                                                     (DTw, DTAw, CUMw, TOTw, ECUMw, EENDw, TAw, TBw, DECAw, DECBw)]
                pp = ti % 2
                XTOK, A1, XEND, BTOK = XTOKs[pp], A1s[pp], XENDs[pp], BTOKs[pp]
                x3 = XTOK[0:r].rearrange("p (h q) -> p h q", q=64)
                if part == 0:
                    b = nb()
                    for c in range(4):
                        P.op("pe", lambda e, c=c, b=b, c0=c0: e.transpose(out=PS[b][0:r, c * 128:(c + 1) * 128],
                                                                         in_=CONVX[:, c, c0:c0 + r], identity=identF),
                             reads=[KM("CONVX", c), ("cst",)], writes=[("ps", b)])
                    P.op("act", lambda e, b=b: e.activation(out=XTOK[0:r], in_=PS[b][0:r, :], func=AF.Copy),
                         reads=[("ps", b)], writes=[KM("XTOK", pp)])
                    P.op("dve", lambda e, x3=x3: e.tensor_tensor(
                        out=XDT[0:r].rearrange("p (h q) -> p h q", q=64), in0=x3,
                        in1=DT[0:r].unsqueeze(2).to_broadcast([r, 8, 64]), op=ALU.mult),
                        reads=[KM("XTOK", pp), KM("DT")], writes=[KWM("XDT")])
                    P.op("dve", lambda e: e.tensor_tensor(
                        out=XEND[0:r].rearrange("p (h q) -> p h q", q=64), in0=XDT[0:r].rearrange("p (h q) -> p h q", q=64),
                        in1=EEND[0:r].unsqueeze(2).to_broadcast([r, 8, 64]), op=ALU.mult),
                        reads=[KWM("XDT"), KM("EEND")], writes=[KWM("XEND", pp)])
                    b = nb()
                    pvb = PS[b][:].bitcast(BF16)
                    for g in range(2):
                        P.op("pe", lambda e, g=g, pvb=pvb, c0=c0: e.transpose(out=pvb[0:r, g * 128:(g + 1) * 128],
                                                                             in_=BT[:, g, c0:c0 + r], identity=ident[:, :]),
                             reads=[KWM("BT", g), ("ident",)], writes=[("ps", b)])
                    P.op("act", lambda e, pvb=pvb, b=b: e.activation(out=BTOK[0:r], in_=pvb[0:r, 0:256], func=AF.Copy),
                         reads=[("ps", b)], writes=[KWM("BTOK", pp)])
                    byi = 6
                    for g in range(2):
                        b = nb()
                        P.op("pe", lambda e, g=g, b=b, c0=c0: e.matmul(PS[b][0:r, 0:r], lhsT=BT[:, g, c0:c0 + r],
                                                                      rhs=CT[:, g, c0:c0 + r], start=True, stop=True),
                             reads=[KWM("BT", g), KWM("CT", g)], writes=[("ps", b)])
                        P.op("dve", lambda e, g=g, b=b: e.tensor_tensor(out=CBm[g][0:r, 0:r], in0=PS[b][0:r, 0:r],
                                                                        in1=mask[0:r, 0:r], op=ALU.mult),
                             reads=[("ps", b), mkey], writes=[KM("CBm", g)])
                    for g in range(2):
                        dg, dm, mm4 = DG4[g], DM4[g], MM4[g]
                        cum4 = CUM[0:r, 4 * g:4 * g + 4]
                        P.op("dve", lambda e, dg=dg, cum4=cum4: e.tensor_tensor(
                            out=dg[0:r, :].rearrange("p (h i) -> p h i", h=4),
                            in0=identF[0:r, 0:r].unsqueeze(1).to_broadcast([r, 4, r]),
                            in1=cum4.unsqueeze(2).to_broadcast([r, 4, r]), op=ALU.mult),
                            reads=[("cst",), KM("CUM")], writes=[KM("DG", g)])
                        b = nb()
                        P.op("pe", lambda e, b=b, dg=dg: e.matmul(PS[b][0:r, 0:4 * r], lhsT=onesF[0:r, 0:r], rhs=dg[0:r, 0:4 * r],
                                                                  start=True, stop=True),
                             reads=[("cst",), KM("DG", g)], writes=[("ps", b)])
                        P.op("dve", lambda e, b=b, dm=dm, cum4=cum4: e.tensor_tensor(
                            out=dm[0:r, :].rearrange("p (h i) -> p h i", h=4),
                            in0=PS[b][0:r, 0:4 * r].rearrange("p (h i) -> p h i", h=4),
                            in1=cum4.unsqueeze(2).to_broadcast([r, 4, r]), op=ALU.subtract),
                            reads=[("ps", b), KM("CUM")], writes=[KM("DM", g)])
                        P.op("act", lambda e, dm=dm: e.activation(out=dm[0:r, 0:4 * r], in_=dm[0:r, 0:4 * r], func=AF.Exp),
                             reads=[KM("DM", g)], writes=[KM("DM", g)])
                        P.op("dve", lambda e, dm=dm, mm4=mm4, g=g: e.scalar_tensor_tensor(
                            out=mm4[0:r, 0:4 * r].rearrange("p (h i) -> p h i", h=4),
                            in0=dm[0:r, :].rearrange("p (h i) -> p h i", h=4), scalar=1.0, op0=ALU.min,
                            in1=CBm[g][0:r, 0:r].unsqueeze(1).to_broadcast([r, 4, r]), op1=ALU.mult),
                            reads=[KM("DM", g), KM("CBm", g)], writes=[KWM("MM", 0)])
                        for hh in range(4):
                            h = 4 * g + hh
                            P.op("pe", lambda e, h=h, hh=hh, mm4=mm4, byi=byi: e.matmul(
                                PS[byi][0:r, h * 64:(h + 1) * 64], lhsT=mm4[0:r, hh * r:(hh + 1) * r], rhs=XDT[0:r, h * 64:(h + 1) * 64],
                                start=(h == 0), stop=(h == 7), skip_group_check=True),
                                reads=[KWM("MM", 0), KWM("XDT")], writes=[("ps", byi)])
                    P.op("act", lambda e, byi=byi: e.activation(out=A1[0:r, :], in_=PS[byi][0:r, :], func=AF.Copy),
                         reads=[("ps", byi)], writes=[KM("A1", pp)])
                    return
                byx = 7
                if not sample:
                    for cc in range(2):
                        P.op("dve", lambda e, cc=cc, c0=c0: e.tensor_copy(out=CTm[:, :, cc, cc * 64:cc * 64 + 64],
                                                                         in_=CT[:, :, c0 + cc * 64:c0 + cc * 64 + 64]),
                             reads=[KWM("CT")], writes=[KWM("CTm", cc)])
                        P.op("dve", lambda e, cc=cc: e.memset(CTm[:, :, cc, (1 - cc) * 64:(1 - cc) * 64 + 64], 0.0),
                             writes=[KWM("CTm", cc)])
                    for cc in range(2):
                        for g in range(2):
                            P.op("pe", lambda e, cc=cc, g=g, byx=byx: e.matmul(
                                PS[byx][:, g * 256:(g + 1) * 256], lhsT=CTm[:, g, cc, :], rhs=MSB[:, g * 256:(g + 1) * 256],
                                start=(cc == 0 and g == 0), stop=(cc == 1 and g == 1), skip_group_check=True),
                                reads=[KWM("CTm", cc), KWM("MSB")], writes=[("ps", byx)])
                        bu = nb()
                        for g in range(2):
                            P.op("pe", lambda e, cc=cc, g=g, bu=bu: e.matmul(
                                PS[bu][:, g * 256:(g + 1) * 256], lhsT=BTOK[cc * 64:cc * 64 + 64, g * 128:(g + 1) * 128],
                                rhs=XEND[cc * 64:cc * 64 + 64, g * 256:(g + 1) * 256], start=(g == 0), stop=(g == 1),
                                skip_group_check=True),
                                reads=[KWM("BTOK", pp), KWM("XEND", pp)], writes=[("ps", bu)])
                        DECX = DECA if cc == 0 else DECB
                        P.op("dve", lambda e, DECX=DECX: e.tensor_tensor(
                            out=MS.rearrange("p (h q) -> p h q", q=64), in0=MS.rearrange("p (h q) -> p h q", q=64),
                            in1=DECX.unsqueeze(2).to_broadcast([128, 8, 64]), op=ALU.mult),
                            reads=[KM("MS"), KM("DEC")], writes=[KM("MS")])
                        P.op("dve", lambda e, bu=bu: e.tensor_tensor(out=MS, in0=MS, in1=PS[bu][:, :], op=ALU.add),
                             reads=[KM("MS"), ("ps", bu)], writes=[KM("MS")])
                        P.op("act", lambda e: e.activation(out=MSB, in_=MS, func=AF.Copy), reads=[KM("MS")], writes=[KWM("MSB")])
                else:
                    P.op("act", lambda e: e.activation(out=TA[0:64], in_=TOT[0:64], func=AF.Exp), reads=[KM("TOT")], writes=[KM("TX")])
                    P.op("dve", lambda e: e.tensor_copy(out=ETR[0:64].rearrange("p (h q) -> p h q", q=64),
                                                        in_=TA[0:64].unsqueeze(2).to_broadcast([64, 8, 64])),
                         reads=[KM("TX")], writes=[KM("ETR")])
                    bd = nb()
                    for a in range(4):
                        P.op("pe", lambda e, a=a, bd=bd: e.matmul(PS[bd][:, a * 16:(a + 1) * 16], lhsT=ETR[0:64, a * 128:(a + 1) * 128],
                                                                  rhs=lastS, start=(a == 0), stop=(a == 3), skip_group_check=True),
                             reads=[KM("ETR"), ("cst",)], writes=[("ps", bd)])
                    P.op("act", lambda e, bd=bd: e.activation(out=DECP, in_=PS[bd][:, 0:64].rearrange("p (a s) -> p a s", a=4),
                                                              func=AF.Copy), reads=[("ps", bd)], writes=[KM("DECP")])
                    for sq in range(16):
                        sn, snew, s0t = SNAT[sq % 2], SNEW[sq % 2], S0T[sq % 2]
                        P.op("sp", lambda e, sq=sq, sn=sn: e.dma_start(
                            out=sn, in_=ss0[sq].rearrange("(a b) p n -> (b p) a n", b=2)), writes=[KM("SNAT", sq % 2)], dma=True)
                        bt_ = nb()
                        for a in range(4):
                            P.op("pe", lambda e, a=a, bt_=bt_, sn=sn: e.transpose(out=PS[bt_][:, a * 128:(a + 1) * 128],
                                                                               in_=sn[:, a, :], identity=identF),
                                 reads=[KM("SNAT", sq % 2), ("cst",)], writes=[("ps", bt_)])
                        P.op("act", lambda e, bt_=bt_, s0t=s0t: e.activation(out=s0t, in_=PS[bt_][:, :], func=AF.Copy),
                             reads=[("ps", bt_)], writes=[KWM("S0T", sq % 2)])
                        P.op("dve", lambda e: e.memset(CTm[:, :, 0, 0:64], 0.0), writes=[KWM("CTm", 0)])
                        P.op("dve", lambda e, sq=sq: e.tensor_copy(out=CTm[:, :, 0, sq * 4:sq * 4 + 4], in_=CT[:, :, sq * 4:sq * 4 + 4]),
                             reads=[KWM("CT")], writes=[KWM("CTm", 0)])
                        for g in range(2):
                            P.op("pe", lambda e, g=g, sq=sq, s0t=s0t, byx=byx: e.matmul(
                                PS[byx][0:64, g * 256:(g + 1) * 256], lhsT=CTm[:, g, 0, 0:64], rhs=s0t[:, g * 256:(g + 1) * 256],
                                start=(sq == 0 and g == 0), stop=(sq == 15 and g == 1), skip_group_check=True),
                                reads=[KWM("CTm", 0), KWM("S0T", sq % 2)], writes=[("ps", byx)])
                        P.op("dve", lambda e, sq=sq: e.tensor_scalar(out=XENDm[0:64], in0=XEND[0:64],
                                                                     scalar1=maskS[0:64, 64 + sq:65 + sq], scalar2=None, op0=ALU.mult),
                             reads=[KWM("XEND", pp), ("maskS",)], writes=[KWM("XENDm")])
                        bn = nb()
                        for a in range(4):
                            P.op("pe", lambda e, a=a, bn=bn: e.matmul(
                                PS[bn][:, a * 128:(a + 1) * 128], lhsT=XENDm[0:64, a * 128:(a + 1) * 128],
                                rhs=BTOK[0:64, (a // 2) * 128:(a // 2) * 128 + 128], start=(a == 0), stop=(a == 3),
                                skip_group_check=True),
                                reads=[KWM("XENDm"), KWM("BTOK", pp)], writes=[("ps", bn)])
                        for a in range(4):
                            P.op("dve", lambda e, a=a, sq=sq, bn=bn, sn=sn, snew=snew: e.scalar_tensor_tensor(
                                out=snew[:, a, :], in0=sn[:, a, :], scalar=DECP[:, a, sq:sq + 1], op0=ALU.mult,
                                in1=PS[bn][:, a * 128:(a + 1) * 128], op1=ALU.add),
                                reads=[KM("SNAT", sq % 2), KM("DECP"), ("ps", bn)], writes=[KM("SNEW", sq % 2)])
                        P.op("sp", lambda e, sq=sq, snew=snew: e.dma_start(
                            out=sss[sq].rearrange("(a b) p n -> (b p) a n", b=2), in_=snew), reads=[KM("SNEW", sq % 2)], dma=True)
                P.op("dve", lambda e, byx=byx: e.tensor_tensor(
                    out=A2[0:r, :].rearrange("p (h q) -> p h q", q=64), in0=PS[byx][0:r, :].rearrange("p (h q) -> p h q", q=64),
                    in1=ECUM[0:r].unsqueeze(2).to_broadcast([r, 8, 64]), op=ALU.mult),
                    reads=[("ps", byx), KM("ECUM")], writes=[KM("A2")])
                P.op("dve", lambda e: e.tensor_tensor(out=A2[0:r, :], in0=A2[0:r, :], in1=A1[0:r, :], op=ALU.add),
                     reads=[KM("A2"), KM("A1", pp)], writes=[KM("A2")])
                P.op("dve", lambda e, x3=x3: e.tensor_tensor(out=A1[0:r, :].rearrange("p (h q) -> p h q", q=64), in0=x3,
                                                             in1=DROW[0:r].unsqueeze(2).to_broadcast([r, 8, 64]), op=ALU.mult),
                     reads=[KM("XTOK", pp), ("cols2",)], writes=[KM("A1", pp)])
                P.op("dve", lambda e: e.tensor_tensor(out=A2[0:r, :], in0=A2[0:r, :], in1=A1[0:r, :], op=ALU.add),
                     reads=[KM("A2"), KM("A1", pp)], writes=[KM("A2")])
                P.op("dve", lambda e, ti=ti: e.tensor_tensor(out=A2[0:r, :], in0=A2[0:r, :], in1=ZS[0:r, ti, :], op=ALU.mult),
                     reads=[KM("A2"), KM("ZS")], writes=[KM("A2")])
                for g in range(2):
                    P.op("act", lambda e, g=g: e.activation(out=A1[0:r, g * 256:(g + 1) * 256], in_=A2[0:r, g * 256:(g + 1) * 256],
                                                            func=AF.Square, accum_out=SSQ[0:r, g:g + 1]),
                         reads=[KM("A2")], writes=[KM("A1", pp), KM("SSQ")])
                P.op("dve", lambda e: e.tensor_scalar(out=RSQ[0:r, 0:2], in0=SSQ[0:r, 0:2], scalar1=1.0 / 256, scalar2=EPS,
                                                      op0=ALU.mult, op1=ALU.add), reads=[KM("SSQ")], writes=[KM("RSQ")])
                P.op("act", lambda e: e.activation(out=RSQ[0:r, 0:2], in_=RSQ[0:r, 0:2], func=AF.Ln),
                     reads=[KM("RSQ")], writes=[KM("RSQ")])
                P.op("act", lambda e: e.activation(out=RSQ[0:r, 0:2], in_=RSQ[0:r, 0:2], func=AF.Exp, scale=-0.5),
                     reads=[KM("RSQ")], writes=[KM("RSQ")])
                for g in range(2):
                    P.op("dve", lambda e, g=g: e.tensor_scalar(out=YN[0:r, g * 256:(g + 1) * 256], in0=A2[0:r, g * 256:(g + 1) * 256],
                                                               scalar1=RSQ[0:r, g:g + 1], scalar2=None, op0=ALU.mult),
                         reads=[KM("A2"), KM("RSQ")], writes=[KWM("YN")])
                b = nb()
                pvy = PS[b][:].bitcast(BF16).rearrange("p (c t) -> p c t", c=8)
                for c in range(4):
                    P.op("pe", lambda e, c=c, pvy=pvy: e.transpose(out=pvy[:, c, 0:r], in_=YN[0:r, c * 128:(c + 1) * 128],
                                                                   identity=ident[0:r, 0:r]),
                         reads=[KWM("YN"), ("ident",)], writes=[("ps", b)])
                for c in range(4):
                    P.op("act", lambda e, c=c, pvy=pvy, c0=c0: e.activation(out=OAT[:, 4 + c, c0:c0 + r], in_=pvy[:, c, 0:r],
                                                                           func=AF.Copy, scale=SNW[:, c:c + 1]),
                         reads=[("ps", b), ("cols2",)], writes=[KW("OAT", 4 + c)])
            _tile(0, 0)
            for ti in range(ntile):
                if ti + 1 < ntile:
                    _tile(ti + 1, 0)
                _tile(ti, 1)
            if si == 3:
                b = nb()
                for a in range(4):
                    P.op("pe", lambda e, a=a, b=b: e.transpose(out=PS[b][:, a * 128:(a + 1) * 128], in_=MS[:, a * 128:(a + 1) * 128],
                                                               identity=identF), reads=[KM("MS"), ("cst",)], writes=[("ps", b)])
                P.op("act", lambda e, b=b: e.activation(out=A1[:, :], in_=PS[b][:, :], func=AF.Copy),
                     reads=[("ps", b)], writes=[KM("A1")])
                P.op("sp", lambda e: e.dma_start(out=ssp.rearrange("(a b) p n -> (b p) a n", b=2),
                                                 in_=A1[:, :].rearrange("p (a n) -> p a n", a=4)), reads=[KM("A1")], dma=True)
            barrier()

        def hgrn_mixer(norm_gi, next_gi):
            pend = list(range(len(STILES)))

            def norm_upto(idx):
                while pend and pend[0] <= idx:
                    gi_ = pend.pop(0)
                    norm_group(norm_gi, *STILES[gi_])
            mamba_setup()
            barrier()
            P.op("dve", lambda e: e.memset(SST[:, :, :], 0.0), writes=[K("SST")])
            P.op("dve", lambda e: e.memset(SBF[:, :, :], 0.0), writes=[KW("SBF")])
            for si, (t0, n) in enumerate(STILES):
                norm_upto(si + 1)
                if si == 4:
                    barrier()
                for h in range(4):
                    hgrn_head(si, t0, n, h, si == 4)
                mamba_tile_group(si, t0, n)
                out_proj(ab_w_out[0], t0, n, 8)
                norm_group(next_gi, t0, n)
                if si == 3:
                    for h in range(4):
                        P.op("sp", lambda e, h=h: e.dma_start(out=hgp[h], in_=SST[:, h, 0:128]),
                             reads=[K("SST", 0, h)], dma=True)
            barrier()

        def gla_head(si, t0, n, h, sample):
            W = gla_w_in[0]
            for vc in range(2):
                b2 = proj_fm(W, 2048 + h * 256 + vc * 128, t0, n)
                evac(AF.Silu, (GTa, GTb)[vc][:, 0:n], K("GT", vc), b2, n)
            b = proj_fm(W, h * 128, t0, n)
            P.op("dve", lambda e, b=b: e.tensor_scalar(out=QV[:, 0:n], in0=PS[b][:, 0:n], scalar1=float(128 ** -0.5),
                                                       scalar2=None, op0=ALU.mult),
                 reads=[("ps", b)], writes=[K("QV")])
            b = proj_fm(W, 512 + h * 128, t0, n)
            evac(AF.Copy, KV[:, 0:n], K("KV"), b, n)
            b = nb()
            P.op("pe", lambda e, b=b: e.matmul(PS[b][:, 0:n], lhsT=WGK[:, h * 128:(h + 1) * 128],
                                               rhs=GKL[:, 0:n], start=True, stop=True),
                 reads=[KW("WGK"), KW("GKL")], writes=[("ps", b)])
            P.op("dve", lambda e, b=b: e.tensor_scalar(out=T1[:, 0:n], in0=PS[b][:, 0:n], scalar1=cols[:, 16 + h:17 + h],
                                                       scalar2=None, op0=ALU.subtract),
                 reads=[("ps", b), ("cols",)], writes=[K("T1")])
            P.op("act", lambda e: e.activation(out=T1[:, 0:n], in_=T1[:, 0:n], func=AF.Exp, scale=-1.0),
                 reads=[K("T1")], writes=[K("T1")])
            P.op("dve", lambda e: e.tensor_scalar(out=T1[:, 0:n], in0=T1[:, 0:n], scalar1=1.0, scalar2=None, op0=ALU.add),
                 reads=[K("T1")], writes=[K("T1")])
            P.op("act", lambda e: e.activation(out=T1[:, 0:n], in_=T1[:, 0:n], func=AF.Ln),
                 reads=[K("T1")], writes=[K("T1")])
            P.op("dve", lambda e: e.tensor_scalar(out=LF[:, 0:n], in0=T1[:, 0:n], scalar1=-1.0 / 16.0,
                                                  scalar2=None, op0=ALU.mult),
                 reads=[K("T1")], writes=[K("LF")])
            for vc in range(2):
                proj_tm(W, 1024 + h * 256 + vc * 128, t0, n,
                        lambda ti, r, vc=vc: VT[0:r, ti, vc * 128:(vc + 1) * 128], KW("VT"))
            gla_core(si, t0, n, h, 256, sample, gs0, glp, gls, 20, 2 * h)

        def gla_gk(t0, n):
            b = proj_fm(gla_w_in[0], 3072, t0, n, ncols=16)
            evac(AF.Copy, GKL[:, 0:n], KW("GKL"), b, n, rows_=16)

        def gla_mixer(norm_gi, next_gi):
            pend = list(range(len(STILES)))

            def norm_upto(idx):
                while pend and pend[0] <= idx:
                    gi_ = pend.pop(0)
                    norm_group(norm_gi, *STILES[gi_])
            barrier()
            P.op("dve", lambda e: e.memset(SST[:, :, :], 0.0), writes=[K("SST")])
            P.op("dve", lambda e: e.memset(SBF[:, :, :], 0.0), writes=[KW("SBF")])
            P.op("pool", lambda e: e.dma_start(out=WGK, in_=gla_w_gk[0]), writes=[KW("WGK")], dma=True)
            for si, (t0, n) in enumerate(STILES):
                norm_upto(si + 1)
                if si == 4:
                    barrier()
                gla_gk(t0, n)
                for h in range(4):
                    gla_head(si, t0, n, h, si == 4)
                out_proj(gla_w_out[0], t0, n, 8)
                norm_group(next_gi, t0, n)
                if si == 3:
                    for h in range(4):
                        P.op("sp", lambda e, h=h: e.dma_start(out=glp[h], in_=SST[:, h, 0:256]),
                             reads=[K("SST", 0, h)], dma=True)
            barrier()

        for layer in range(2):
            norm_to_hT(3 * layer + 0)
            ffn(ffn_w_in[0][layer], ffn_w_out[0][layer])
            if layer == 0:
                hgrn_mixer(3 * layer + 1, 3 * layer + 2)
            else:
                gla_mixer(3 * layer + 1, 3 * layer + 2)
            ffn(ffn_w_in[1][layer], ffn_w_out[1][layer], tile_epilogue=(final_tile if layer == 1 else None))

        with nc.allow_non_contiguous_dma(reason="small strided parameter/state transfers"):
            P.emit()
    return nc


_CACHE = {}


def kernel(**inputs):
    f32 = lambda a: np.ascontiguousarray(np.asarray(a, dtype=np.float32))
    x_prompt = f32(inputs["x_prompt"])
    x_sample = f32(inputs["x_sample"]).reshape(128 * 4, D)
    shared = {k: f32(inputs[k]) for k in ("norm_ffn1", "norm_mix", "norm_ffn2", "norm_final",
                                          "ffn1_w_in", "ffn1_w_out", "ffn2_w_in", "ffn2_w_out")}
    for k in ("ab_w_in", "ab_w_out", "hgrn_lb_logits", "hgrn_norm", "gla_w_in", "gla_w_gk", "gla_b_gk",
              "gla_norm", "gla_w_out", "ssm_conv_w", "ssm_conv_b", "ssm_dt_bias", "ssm_a_log", "ssm_d", "ssm_norm"):
        shared[k] = f32(inputs[k])
    shared["ident"] = np.eye(128, dtype=np.float32).astype(ml_dtypes.bfloat16)
    shared["ones_bf"] = np.ones((128, 128), dtype=np.float32).astype(ml_dtypes.bfloat16)
    jj, ii = np.meshgrid(np.arange(128), np.arange(128), indexing="ij")
    shared["maskP"] = ((jj // 64 == ii // 64) & (jj <= ii)).astype(np.float32)
    mS = np.zeros((128, 128), np.float32)
    mS[:64, :64] = ((jj // 4 == ii // 4) & (jj <= ii))[:64, :64]
    mS[:64, 64:80] = (np.arange(64)[:, None] // 4 == np.arange(16)[None, :])
    shared["maskS"] = mS
    shared["resetP"] = np.broadcast_to((np.arange(512) % 64 != 0).astype(np.float32), (128, 512)).copy()
    shared["resetS"] = np.broadcast_to((np.arange(512) % 4 != 0).astype(np.float32), (128, 512)).copy()
    cst = np.zeros((128, 384), np.float32)
    cst[:, 0:128] = np.eye(128)
    cst[:, 128:256] = 1.0
    cst[:64, 256:320] = (np.arange(64)[:, None] // 4 == np.arange(64)[None, :] // 4)
    cst[:64, 320:336] = (np.arange(64)[:, None] == 4 * np.arange(16)[None, :] + 3)
    cst[:64, 336] = 1.0
    cst[64:, 337] = 1.0
    shared["cst"] = cst
    st_s = f32(inputs["state_ssm"])[0]
    st_c = f32(inputs["state_conv"])[0]
    st_h = f32(inputs["state_hgrn"])[0]
    st_g = f32(inputs["state_gla"])[0]
    if "nc" not in _CACHE:
        _CACHE["nc"] = build_program()
    nc = _CACHE["nc"]
    in_maps = []
    for c in range(N_CORES):
        m = dict(shared)
        m["xp"] = x_prompt[c]
        m["xs"] = x_sample[c * 64:(c + 1) * 64]
        m["hs0"] = st_h[c * 16:(c + 1) * 16]
        m["gs0"] = st_g[c * 16:(c + 1) * 16]
        m["ss0"] = st_s[c * 16:(c + 1) * 16]
        m["cs0"] = st_c[c * 16:(c + 1) * 16]
        in_maps.append(m)
    res = run_bass_kernel_spmd(nc, in_maps, core_ids=list(range(N_CORES)))
    outs = res.results
    y_prompt = np.stack([outs[c]["yp"] for c in range(N_CORES)], axis=0)
    y_sample = np.concatenate([outs[c]["ys"] for c in range(N_CORES)], axis=0).reshape(128, 4, D)
    cat = lambda k: np.concatenate([outs[c][k] for c in range(N_CORES)], axis=0)
    stk = lambda k: np.stack([outs[c][k] for c in range(N_CORES)], axis=0)
    hgrn_p = stk("hgp")[None]
    hgrn_s = cat("hgs")[None]
    gla_p = stk("glp")[None]
    gla_s = cat("gls")[None]
    z = lambda *sh: np.zeros(sh, np.float32)
    return (y_prompt, y_sample, hgrn_p, hgrn_s, stk("ssp")[None], cat("sss")[None],
            stk("cvp")[None], cat("cvs")[None], gla_p, gla_s)
```

```python
import contextlib
import numpy as np
import ml_dtypes
import concourse.bass as bass
import concourse.mybir as mybir
from concourse.bass_utils import run_bass_kernel_spmd

F32 = mybir.dt.float32
BF16 = mybir.dt.bfloat16
AF = mybir.ActivationFunctionType
ALU = mybir.AluOpType

N_CORES = 8
D = 1024
DFF = 2816
NFC = 22
SEQ = 2048
NT = 17
TTOK = 2112
EPS = 1e-6
STILES = [(0, 512), (512, 512), (1024, 512), (1536, 512), (2048, 64)]
FSTILES = [(0, 448), (448, 448), (896, 448), (1344, 448), (1792, 320)]

ENGS = ("pe", "act", "dve", "pool", "sp")
NLANES = {"sp": 12, "pool": 12, "act": 6}

DEBUG = {"mixers": True, "strict": True, "hgrn": True, "gla": True}


class _Op:
    __slots__ = ("eng", "fn", "waits", "signal", "ticket", "dma", "lane", "lane_ticket", "idx")


class Prog:
    def __init__(self, nc, strict_same_engine=True):
        self.nc = nc
        self.ops = {e: [] for e in ENGS}
        self.last_w = {}
        self.readers = {}
        self.children = {}
        self.dma_count = {"sp": 0, "pool": 0, "act": 0}
        self.strict = strict_same_engine

    def _conflicts(self, key):
        out = [key[:i] for i in range(1, len(key) + 1)]
        out.extend(self.children.get(key, ()))
        return out

    def _register(self, key):
        for i in range(1, len(key)):
            self.children.setdefault(key[:i], set()).add(key)

    def op(self, eng, fn, reads=(), writes=(), dma=False):
        o = _Op()
        o.eng, o.fn, o.dma, o.signal, o.ticket = eng, fn, dma, False, None
        o.idx = len(self.ops[eng])
        deps = []
        for k in reads:
            for c in self._conflicts(k):
                t = self.last_w.get(c)
                if t is not None:
                    deps.append((t, "raw"))
        for k in writes:
            for c in self._conflicts(k):
                t = self.last_w.get(c)
                if t is not None:
                    deps.append((t, "waw"))
                for t in self.readers.get(c, {}).values():
                    deps.append((t, "war"))
        if dma:
            n = self.dma_count[eng]
            self.dma_count[eng] = n + 1
            nl = NLANES[eng]
            o.lane = n % nl
            o.lane_ticket = 16 * (n // nl + 1)
            tok = ("d", eng, o.lane, o.lane_ticket, o.idx)
            if n >= nl:
                deps.append((("d", eng, o.lane, o.lane_ticket - 16, -1), "lane"))
        else:
            tok = ("c", eng, o.idx)
        o.waits = []
        for t, kind in deps:
            if t[0] == "c" and t[1] == eng and not dma:
                if eng == "pe" or not self.strict:
                    continue
            o.waits.append(t)
        for k in writes:
            self._register(k)
            self.last_w[k] = tok
            self.readers[k] = {}
            for ch in list(self.children.get(k, ())):
                self.last_w.pop(ch, None)
                self.readers.pop(ch, None)
        for k in reads:
            self._register(k)
            src = (tok[0], tok[1], tok[2] if tok[0] == "d" else 0)
            self.readers.setdefault(k, {})[src] = tok
        self.ops[eng].append(o)
        return o

    def emit(self, final_wait_eng="sp"):
        nc = self.nc
        for e in ENGS:
            for o in self.ops[e]:
                for t in o.waits:
                    if t[0] == "c":
                        self.ops[t[1]][t[2]].signal = True
        for e in ENGS:
            n = 0
            for o in self.ops[e]:
                if o.signal and not o.dma:
                    n += 1
                    o.ticket = n
        with contextlib.ExitStack() as st:
            csem = {e: st.enter_context(nc.semaphore("c_" + e)) for e in ENGS}
            lsem = {}
            for q, nl in NLANES.items():
                for l in range(nl):
                    lsem[(q, l)] = st.enter_context(nc.semaphore("l_%s_%d" % (q, l)))
            block = st.enter_context(nc.Block())
            engobj = {"pe": "tensor", "act": "scalar", "dve": "vector", "pool": "gpsimd", "sp": "sync"}

            def emit_engine(e, eng):
                seen = {}
                for o in self.ops[e]:
                    need = {}
                    for t in o.waits:
                        if t[0] == "c":
                            key = ("c", t[1])
                            val = self.ops[t[1]][t[2]].ticket
                        else:
                            key = ("d", t[1], t[2])
                            val = t[3]
                        if seen.get(key, 0) >= val:
                            continue
                        if need.get(key, 0) < val:
                            need[key] = val
                    for key, val in need.items():
                        seen[key] = val
                        sem = csem[key[1]] if key[0] == "c" else lsem[(key[1], key[2])]
                        eng.wait_ge(sem, val)
                    inst = o.fn(eng)
                    if o.dma:
                        inst.then_inc(lsem[(e, o.lane)], 16)
                    elif o.signal:
                        inst.then_inc(csem[e], 1)
                if e == final_wait_eng:
                    for q, nl in NLANES.items():
                        n = self.dma_count[q]
                        for l in range(nl):
                            cnt = (n - l + nl - 1) // nl if n > l else 0
                            if cnt > 0:
                                eng.wait_ge(lsem[(q, l)], 16 * cnt)

            for e in ENGS:
                if not self.ops[e] and e != final_wait_eng:
                    continue

                def body(eng, e=e):
                    emit_engine(e, eng)
                getattr(block, engobj[e])(body)


def build_program():
    nc = bass.Bass("TRN2", target_bir_lowering=False)

    def din(name, shape, dt=F32):
        return nc.dram_tensor(name, list(shape), dt, kind="ExternalInput").ap()

    def dout(name, shape):
        return nc.dram_tensor(name, list(shape), F32, kind="ExternalOutput").ap()

    xp = din("xp", [SEQ, D])
    xs = din("xs", [64, D])
    norm_ffn1 = din("norm_ffn1", [2, D])
    norm_mix = din("norm_mix", [2, D])
    norm_ffn2 = din("norm_ffn2", [2, D])
    norm_final = din("norm_final", [D])
    ffn_w_in = [din("ffn1_w_in", [2, D, 2 * DFF]), din("ffn2_w_in", [2, D, 2 * DFF])]
    ffn_w_out = [din("ffn1_w_out", [2, DFF, D]), din("ffn2_w_out", [2, DFF, D])]
    ident_d = din("ident", [128, 128], BF16)
    ones_d = din("ones_bf", [128, 128], BF16)
    maskP_d = din("maskP", [128, 128])
    maskS_d = din("maskS", [128, 128])
    resetP_d = din("resetP", [128, 512])
    resetS_d = din("resetS", [128, 512])
    ab_w_in = din("ab_w_in", [1, D, 3592])
    ab_w_out = din("ab_w_out", [1, D, D])
    lb_logits = din("hgrn_lb_logits", [2, 512])
    hgrn_norm = din("hgrn_norm", [1, 128])
    gla_w_in = din("gla_w_in", [1, D, 3088])
    gla_w_gk = din("gla_w_gk", [1, 16, 512])
    gla_b_gk = din("gla_b_gk", [1, 512])
    gla_norm = din("gla_norm", [1, 256])
    gla_w_out = din("gla_w_out", [1, D, D])
    hs0 = din("hs0", [16, 4, 128, 128])
    gs0 = din("gs0", [16, 4, 128, 256])
    hgp = dout("hgp", [4, 128, 128])
    hgs = dout("hgs", [16, 4, 128, 128])
    glp = dout("glp", [4, 128, 256])
    gls = dout("gls", [16, 4, 128, 256])
    cst_d = din("cst", [128, 384])
    conv_w = din("ssm_conv_w", [1, 4, 1024])
    conv_b = din("ssm_conv_b", [1, 1024])
    dt_bias = din("ssm_dt_bias", [1, 8])
    a_log = din("ssm_a_log", [1, 8])
    ssm_d = din("ssm_d", [1, 8])
    ssm_norm = din("ssm_norm", [1, 512])
    ss0 = din("ss0", [16, 8, 64, 128])
    cs0 = din("cs0", [16, 3, 1024])
    ssp = dout("ssp", [8, 64, 128])
    sss = dout("sss", [16, 8, 64, 128])
    cvp = dout("cvp", [3, 1024])
    cvs = dout("cvs", [16, 3, 1024])

    yp = dout("yp", [SEQ, D])
    ys = dout("ys", [64, D])

    with contextlib.ExitStack() as st:
        def sb(name, shape, dt=F32):
            return st.enter_context(nc.sbuf_tensor(name, list(shape), dt))

        X = sb("X", [128, NT, D])
        hT = sb("hT", [128, 8, TTOK], BF16)
        aT = sb("aT", [128, 11, TTOK], BF16)
        Wo = sb("Wo", [128, 11, D], BF16)
        Wr = [sb("Wr%d" % i, [128, 8, 128], BF16) for i in range(8)]
        sg = [sb("sg%d" % i, [128, 512]) for i in range(2)]
        hn = [sb("hn%d" % i, [128, D], BF16) for i in range(2)]
        junk = sb("junk", [128, D], BF16)
        ident = sb("ident_sb", [128, 128], BF16)
        gcol = sb("gcol", [128, 6, 8])
        gfin = sb("gfin", [128, D])
        ss = sb("ss", [128, 32])
        rs = sb("rs", [128, 32])
        ones_bf = sb("ones_sb", [128, 128], BF16)
        maskP = sb("maskP_sb", [128, 128])
        maskS = sb("maskS_sb", [128, 128])
        resetP = sb("resetP_sb", [128, 512])
        resetS = sb("resetS_sb", [128, 512])
        cols = sb("cols", [128, 64])
        dummy = sb("dummy_sb", [128, 8])
        CST = sb("cst_sb", [128, 384])
        cols2 = sb("cols2", [128, 64])
        TAIL = sb("tail_sb", [128, 8, 4])
        PS = [st.enter_context(nc.psum_tensor("ps%d" % i, [128, 512], F32)) for i in range(8)]

        P = Prog(nc, strict_same_engine=DEBUG["strict"])

        P.op("act", lambda e: e.dma_start(out=ident[:], in_=ident_d), writes=[("ident",)], dma=True)
        gains = [norm_ffn1[0], norm_mix[0], norm_ffn2[0], norm_ffn1[1], norm_mix[1], norm_ffn2[1]]
        for i, g in enumerate(gains):
            P.op("act", lambda e, i=i, g=g: e.dma_start(out=gcol[:, i, :], in_=g.rearrange("(c p) -> p c", p=128)),
                 writes=[("gcol", i)], dma=True)
        P.op("act", lambda e: e.dma_start(out=gfin[:], in_=norm_final.partition_broadcast(128)),
             writes=[("gfin",)], dma=True)
        for tt in range(16):
            P.op("sp", lambda e, tt=tt: e.dma_start(out=X[:, tt, :], in_=xp[tt * 128:(tt + 1) * 128, :]),
                 writes=[("X", tt)], dma=True)
        P.op("sp", lambda e: e.dma_start(out=X[0:64, 16, :], in_=xs), writes=[("X", 16)], dma=True)

        def rows(tt):
            return 64 if tt == 16 else 128

        nrm_ctr = [0]

        def norm_to_hT(gi):
            for (t0, n) in STILES:
                norm_group(gi, t0, n)

        def norm_group(gi, t0, n, only_ti=None, evac=True):
            if True:
                ntile = max(1, n // 128)
                r = min(128, n)
                pvs = [PS[bk][:].bitcast(BF16) for bk in range(4)]
                for ti in range(ntile):
                    if only_ti is not None and ti != only_ti:
                        continue
                    tt = t0 // 128 + ti
                    k = nrm_ctr[0] % 32
                    nrm_ctr[0] += 1
                    P.op("act", lambda e, tt=tt, k=k, r=r: e.activation(
                        out=junk[0:r, :], in_=X[0:r, tt, :], func=AF.Square, accum_out=ss[0:r, k:k + 1]),
                        reads=[("X", tt)], writes=[("junk",), ("ss", k)])
                    P.op("act", lambda e, k=k, r=r: e.activation(
                        out=rs[0:r, k:k + 1], in_=ss[0:r, k:k + 1], func=AF.Sqrt, scale=1.0 / D, bias=EPS),
                        reads=[("ss", k)], writes=[("rs", k)])
                    P.op("dve", lambda e, k=k, r=r: e.reciprocal(out=rs[0:r, k:k + 1], in_=rs[0:r, k:k + 1]),
                         reads=[("rs", k)], writes=[("rs", k)])
                    hb = tt % 2
                    P.op("dve", lambda e, tt=tt, k=k, hb=hb, r=r: e.tensor_scalar(
                        out=hn[hb][0:r, :], in0=X[0:r, tt, :], scalar1=rs[0:r, k:k + 1], scalar2=None, op0=ALU.mult),
                        reads=[("X", tt), ("rs", k)], writes=[("hn", hb)])
                    for c in range(8):
                        o0 = (c % 2) * 512 + ti * 128
                        P.op("pe", lambda e, c=c, hb=hb, o0=o0, r=r: e.transpose(
                            out=pvs[c // 2][:, o0:o0 + r], in_=hn[hb][0:r, c * 128:(c + 1) * 128], identity=ident[0:r, 0:r]),
                            reads=[("hn", hb), ("ident",)], writes=[("ps", c // 2)])
                if not evac:
                    return
                tkeys = [("hT", t0 // 128 + ti) for ti in range(ntile)]
                for c in range(8):
                    src = pvs[c // 2][:, (c % 2) * 512:(c % 2) * 512 + n]
                    if c % 2 == 0:
                        P.op("act", lambda e, c=c, src=src, t0=t0, n=n: e.activation(
                            out=hT[:, c, t0:t0 + n], in_=src, func=AF.Copy, scale=gcol[:, gi, c:c + 1]),
                            reads=[("ps", c // 2), ("gcol", gi)], writes=tkeys)
                    else:
                        P.op("dve", lambda e, c=c, src=src, t0=t0, n=n: e.tensor_scalar(
                            out=hT[:, c, t0:t0 + n], in0=src, scalar1=gcol[:, gi, c:c + 1], scalar2=None, op0=ALU.mult),
                            reads=[("ps", c // 2), ("gcol", gi)], writes=tkeys)

        def final_tile(tt):
            r = rows(tt)
            k = nrm_ctr[0] % 32
            nrm_ctr[0] += 1
            P.op("act", lambda e, tt=tt, r=r, k=k: e.activation(
                out=junk[0:r, :], in_=X[0:r, tt, :], func=AF.Square, accum_out=ss[0:r, k:k + 1]),
                reads=[("X", tt)], writes=[("junk",), ("ss", k)])
            P.op("act", lambda e, r=r, k=k: e.activation(
                out=rs[0:r, k:k + 1], in_=ss[0:r, k:k + 1], func=AF.Sqrt, scale=1.0 / D, bias=EPS),
                reads=[("ss", k)], writes=[("rs", k)])
            P.op("dve", lambda e, r=r, k=k: e.reciprocal(out=rs[0:r, k:k + 1], in_=rs[0:r, k:k + 1]),
                 reads=[("rs", k)], writes=[("rs", k)])
            P.op("dve", lambda e, tt=tt, r=r, k=k: e.scalar_tensor_tensor(
                out=X[0:r, tt, :], in0=X[0:r, tt, :], scalar=rs[0:r, k:k + 1], op0=ALU.mult,
                in1=gfin[0:r, :], op1=ALU.mult),
                reads=[("X", tt), ("rs", k), ("gfin",)], writes=[("X", tt)])
            if tt < 16:
                P.op("sp", lambda e, tt=tt: e.dma_start(out=yp[tt * 128:(tt + 1) * 128, :], in_=X[:, tt, :]),
                     reads=[("X", tt)], dma=True)
            else:
                P.op("sp", lambda e: e.dma_start(out=ys, in_=X[0:64, 16, :]), reads=[("X", 16)], dma=True)


        def norm_epilogue(gi):
            def one(t_):
                si_ = min(t_ // 4, 4)
                t0_, n_ = STILES[si_]
                ti_ = t_ - t0_ // 128
                norm_group(gi, t0_, n_, only_ti=ti_, evac=False)
                if ti_ == max(1, n_ // 128) - 1:
                    norm_group(gi, t0_, n_, only_ti=-1, evac=True)

            def ep(tt):
                if tt >= 1:
                    one(tt - 1)
                if tt == NT - 1:
                    one(tt)
            return ep

        wslot = [0]
        psrot = [0]

        def wload(wmat, col0, ncols=128):
            s = wslot[0] % 8
            wslot[0] += 1
            P.op("pool", lambda e, s=s: e.dma_start(
                out=Wr[s][:, :, 0:ncols], in_=wmat[:, col0:col0 + ncols].rearrange("(c p) n -> p c n", p=128)),
                writes=[("Wr", s)], dma=True)
            return s

        def ffn(w_in, w_out, norm_gi=None, tile_epilogue=None):
            pend = list(range(len(STILES))) if norm_gi is not None else []

            def norm_upto(idx):
                while pend and pend[0] <= idx:
                    gi_ = pend.pop(0)
                    norm_group(norm_gi, *STILES[gi_])
            for g in range(2):
                j0 = g * 11
                for jl in range(11):
                    j = j0 + jl
                    if jl == 4:
                        P.op("pool", lambda e, j0=j0: e.dma_start(
                            out=Wo[:], in_=w_out[j0 * 128:(j0 + 11) * 128, :].rearrange("(j p) d -> p j d", p=128)),
                            writes=[("Wo",)], dma=True)
                    sG = wload(w_in, j * 128)
                    sU = wload(w_in, DFF + j * 128)
                    for (t0, n) in FSTILES:
                        norm_upto(min(4, (t0 + n - 1) // 512) + 1)
                        bg = 4 + (psrot[0] % 2) * 2
                        bu = bg + 1
                        sgi = psrot[0] % 2
                        psrot[0] += 1
                        tiles = [("hT", tt) for tt in range(t0 // 128, (t0 + n - 1) // 128 + 1)]
                        for c in range(8):
                            P.op("pe", lambda e, c=c, s=sG, t0=t0, n=n, bg=bg: e.matmul(
                                PS[bg][:, 0:n], lhsT=Wr[s][:, c, :], rhs=hT[:, c, t0:t0 + n],
                                start=(c == 0), stop=(c == 7)),
                                reads=[("Wr", sG)] + tiles, writes=[("ps", bg)])
                        for c in range(8):
                            P.op("pe", lambda e, c=c, s=sU, t0=t0, n=n, bu=bu: e.matmul(
                                PS[bu][:, 0:n], lhsT=Wr[s][:, c, :], rhs=hT[:, c, t0:t0 + n],
                                start=(c == 0), stop=(c == 7)),
                                reads=[("Wr", sU)] + tiles, writes=[("ps", bu)])
                        P.op("act", lambda e, n=n, bg=bg, sgi=sgi: e.activation(
                            out=sg[sgi][:, 0:n], in_=PS[bg][:, 0:n], func=AF.Silu),
                            reads=[("ps", bg)], writes=[("sg", sgi)])
                        P.op("dve", lambda e, n=n, bu=bu, sgi=sgi, jl=jl, t0=t0: e.tensor_tensor(
                            out=aT[:, jl, t0:t0 + n], in0=sg[sgi][:, 0:n], in1=PS[bu][:, 0:n], op=ALU.mult),
                            reads=[("sg", sgi), ("ps", bu)], writes=[("aT", jl, t0)])
                for tt in range(NT):
                    r = rows(tt)
                    akeys = [(f0) for (f0, fn) in FSTILES if f0 < tt * 128 + r and f0 + fn > tt * 128]
                    for dh in range(2):
                        bo = dh
                        bo = (tt % 2) * 2 + dh + (4 if g == 1 else 0)
                        for jl in range(11):
                            P.op("pe", lambda e, jl=jl, tt=tt, r=r, dh=dh, bo=bo: e.matmul(
                                PS[bo][0:r, :], lhsT=aT[:, jl, tt * 128:tt * 128 + r],
                                rhs=Wo[:, jl, dh * 512:(dh + 1) * 512], start=(jl == 0), stop=(jl == 10)),
                                reads=[("aT", jl, f0) for f0 in akeys] + [("Wo",)], writes=[("ps", bo)])
                        P.op("dve", lambda e, tt=tt, r=r, dh=dh, bo=bo: e.scalar_tensor_tensor(
                            out=X[0:r, tt, dh * 512:(dh + 1) * 512], in0=PS[bo][0:r, :], scalar=0.5, op0=ALU.mult,
                            in1=X[0:r, tt, dh * 512:(dh + 1) * 512], op1=ALU.add),
                            reads=[("ps", bo), ("X", tt)], writes=[("X", tt)])
                    if g == 1 and tile_epilogue is not None:
                        tile_epilogue(tt)

        for nm, t_sb, t_d in (("ones", ones_bf, ones_d), ("maskP", maskP, maskP_d), ("maskS", maskS, maskS_d),
                              ("resetP", resetP, resetP_d), ("resetS", resetS, resetS_d)):
            P.op("sp", lambda e, t_sb=t_sb, t_d=t_d: e.dma_start(out=t_sb[:], in_=t_d), writes=[(nm,)], dma=True)
        P.op("sp", lambda e: e.dma_start(out=cols[:, 0:4], in_=lb_logits[0].rearrange("(h p) -> p h", p=128)),
             writes=[("cols", "lb")], dma=True)
        P.op("sp", lambda e: e.dma_start(out=cols[:, 4:8], in_=lb_logits[1].rearrange("(h p) -> p h", p=128)),
             writes=[("cols", "l1")], dma=True)
        P.op("sp", lambda e: e.dma_start(out=cols[:, 12:13], in_=hgrn_norm[0].rearrange("(h p) -> p h", p=128)),
             writes=[("cols", "hn")], dma=True)
        P.op("sp", lambda e: e.dma_start(out=cols[:, 16:20], in_=gla_b_gk[0].rearrange("(h p) -> p h", p=128)),
             writes=[("cols", "bgk")], dma=True)
        P.op("sp", lambda e: e.dma_start(out=cols[:, 20:22], in_=gla_norm[0].rearrange("(h p) -> p h", p=128)),
             writes=[("cols", "gn")], dma=True)
        P.op("dve", lambda e: e.tensor_tensor(out=cols[:, 0:4], in0=cols[:, 0:4], in1=cols[:, 4:8], op=ALU.subtract),
             reads=[("cols", "lb"), ("cols", "l1")], writes=[("cols", "lb")])
        P.op("act", lambda e: e.activation(out=cols[:, 0:4], in_=cols[:, 0:4], func=AF.Sigmoid),
             reads=[("cols", "lb")], writes=[("cols", "lb")])
        P.op("dve", lambda e: e.tensor_scalar(out=cols[:, 8:12], in0=cols[:, 0:4], scalar1=-1.0, scalar2=1.0,
                                              op0=ALU.mult, op1=ALU.add),
             reads=[("cols", "lb")], writes=[("cols", "oml")])
        P.op("dve", lambda e: e.tensor_scalar(out=cols[:, 16:20], in0=cols[:, 16:20], scalar1=-1.0, scalar2=None,
                                              op0=ALU.mult),
             reads=[("cols", "bgk")], writes=[("cols", "bgk")])

        aTf = aT[:].rearrange("p a b -> p (a b)").bitcast(F32)
        Wob = Wo[:].rearrange("p a b -> p (a b)")

        def fA(i, w=512):
            return aTf[:, i * 512:i * 512 + w]
        QV, KV, LF, GG, T1, T2, O32a, O32b, GTa, GTb, RSTD = [fA(i) for i in range(11)]
        SST = aTf[:, 5632:6656].rearrange("p (h v) -> p h v", h=4)
        S0F = aTf[:, 6656:10752].rearrange("p (s v) -> p s v", s=16)
        QD = Wob[:, 0:512]
        KI = Wob[:, 512:1024]
        KE = Wob[:, 1024:1536]
        STm = [Wob[:, 1536:1664], Wob[:, 1664:1792], Wob[:, 11008:11136], Wob[:, 11136:11264]]
        SBFp = [Wob[:, 7424:7680], Wob[:, 7680:7936]]
        SST2 = [SST, aTf[:, 6656:7680].rearrange("p (h v) -> p h v", h=4)]
        VT = Wob[:, 1792:2816].rearrange("p (t v) -> p t v", t=4)
        KEt = Wob[:, 2816:3328].rearrange("p (t k) -> p t k", t=4)
        OAT = Wob[:, 3328:7424].rearrange("p (c n) -> p c n", c=8)
        SBF = Wob[:, 7424:8448].rearrange("p (h v) -> p h v", h=4)
        S0C = [Wob[:, 8448:8704], Wob[:, 8704:8960]]
        GKL = Wob[0:16, 8960:9472]
        WGK = Wob[0:16, 9472:9984]
        VBLK = Wob[0:64, 9984:11008].rearrange("p (s v) -> p s v", s=4)
        bankrr = [0]

        def nb():
            b = bankrr[0] % 6
            bankrr[0] += 1
            return b

        def K(*a):
            return ("aT", "mx") + a

        def KW(*a):
            return ("Wo", "mx") + a

        def barrier():
            P.op("dve", lambda e: e.memset(dummy[:, 0:1], 0.0), writes=[("aT",), ("Wo",), ("dummy",)])

        def proj_fm(wmat, col0, t0, n, ncols=128):
            sW = wload(wmat, col0, ncols)
            b = nb()
            tiles = [("hT", tt) for tt in range(t0 // 128, t0 // 128 + max(1, n // 128))]
            for c in range(8):
                P.op("pe", lambda e, c=c, b=b, sW=sW: e.matmul(
                    PS[b][0:ncols, 0:n], lhsT=Wr[sW][:, c, 0:ncols], rhs=hT[:, c, t0:t0 + n],
                    start=(c == 0), stop=(c == 7)), reads=[("Wr", sW)] + tiles, writes=[("ps", b)])
            return b

        def proj_tm(wmat, col0, t0, n, dst, dkey, func=AF.Copy):
            sW = wload(wmat, col0, 128)
            for ti in range(max(1, n // 128)):
                r = min(128, n)
                tt = t0 // 128 + ti
                b = nb()
                for c in range(8):
                    P.op("pe", lambda e, c=c, b=b, sW=sW, tt=tt, r=r: e.matmul(
                        PS[b][0:r, 0:128], lhsT=hT[:, c, tt * 128:tt * 128 + r], rhs=Wr[sW][:, c, :],
                        start=(c == 0), stop=(c == 7)), reads=[("Wr", sW), ("hT", tt)], writes=[("ps", b)])
                P.op("act", lambda e, b=b, ti=ti, r=r: e.activation(out=dst(ti, r), in_=PS[b][0:r, 0:128], func=func),
                     reads=[("ps", b)], writes=[dkey])

        def gla_core(si, t0, n, h, V, sample, s0_d, sp_d, ss_d, normcol0, oc0):
            nvc = V // 128
            reset = resetS if sample else resetP
            C = 4 if sample else 64
            nch = n // C
            P.op("dve", lambda e: e.tensor_tensor_scan(out=GG[:, 0:n], data0=reset[:, 0:n], data1=LF[:, 0:n],
                                                       initial=0.0, op0=ALU.mult, op1=ALU.add),
                 reads=[K("LF"), ("resetS" if sample else "resetP",)], writes=[K("GG")])
            P.op("act", lambda e: e.activation(out=T1[:, 0:n], in_=GG[:, 0:n], func=AF.Exp),
                 reads=[K("GG")], writes=[K("T1")])
            P.op("dve", lambda e: e.tensor_tensor(out=QD[:, 0:n], in0=QV[:, 0:n], in1=T1[:, 0:n], op=ALU.mult),
                 reads=[K("QV"), K("T1")], writes=[KW("QD")])
            P.op("act", lambda e: e.activation(out=T2[:, 0:n], in_=GG[:, 0:n], func=AF.Exp, scale=-1.0),
                 reads=[K("GG")], writes=[K("T2")])
            P.op("dve", lambda e: e.tensor_tensor(out=KI[:, 0:n], in0=KV[:, 0:n], in1=T2[:, 0:n], op=ALU.mult),
                 reads=[K("KV"), K("T2")], writes=[KW("KI")])
            g3 = GG[:, 0:n].rearrange("p (c j) -> p c j", j=C)
            P.op("dve", lambda e: e.tensor_tensor(out=T2[:, 0:n].rearrange("p (c j) -> p c j", j=C),
                                                  in0=g3[:, :, C - 1:C].to_broadcast([128, nch, C]), in1=g3,
                                                  op=ALU.subtract),
                 reads=[K("GG")], writes=[K("T2")])
            P.op("act", lambda e: e.activation(out=T2[:, 0:n], in_=T2[:, 0:n], func=AF.Exp),
                 reads=[K("T2")], writes=[K("T2")])
            P.op("dve", lambda e: e.tensor_tensor(out=KE[:, 0:n], in0=KV[:, 0:n], in1=T2[:, 0:n], op=ALU.mult),
                 reads=[K("KV"), K("T2")], writes=[KW("KE")])
            ntile = max(1, n // 128)
            r = min(128, n)
            for ti in range(ntile):
                b = nb()
                pv = PS[b][:].bitcast(BF16)
                P.op("pe", lambda e, ti=ti, pv=pv: e.transpose(out=pv[0:r, 0:128], in_=KE[:, ti * 128:ti * 128 + r],
                                                               identity=ident[:, :]),
                     reads=[KW("KE"), ("ident",)], writes=[("ps", b)])
                P.op("act", lambda e, ti=ti, pv=pv: e.activation(out=KEt[0:r, ti, :], in_=pv[0:r, 0:128], func=AF.Copy),
                     reads=[("ps", b)], writes=[KW("KEt", ti)])
            mask = maskS if sample else maskP
            if sample:
                P.op("sp", lambda e: e.dma_start(out=S0F[:, :, 0:V], in_=s0_d[:, h].rearrange("s k v -> k s v")),
                     writes=[K("S0F")], dma=True)
            if False:
                for ti in range(ntile):
                    c0 = ti * 128
                    b = nb()
                    P.op("pe", lambda e, b=b, c0=c0: e.matmul(PS[b][:, 0:128], lhsT=KI[:, c0:c0 + 128], rhs=QD[:, c0:c0 + 128],
                                                              start=True, stop=True),
                         reads=[KW("KI"), KW("QD")], writes=[("ps", b)])
                    P.op("dve", lambda e, b=b, ti=ti: e.tensor_tensor(out=STm[ti][:, 0:128], in0=PS[b][:, 0:128],
                                                                     in1=maskP[:, :], op=ALU.mult),
                         reads=[("ps", b), ("maskP",)], writes=[KW("ST", ti)])
                per = 512 // V
                ub = {}
                for c in range(2 * ntile):
                    ti, cc = divmod(c, 2)
                    if c % per == 0:
                        bU = nb()
                    col = (c % per) * V
                    P.op("pe", lambda e, bU=bU, col=col, cc=cc, ti=ti, c=c: e.matmul(
                        PS[bU][:, col:col + V], lhsT=KEt[cc * 64:cc * 64 + 64, ti, :], rhs=VT[cc * 64:cc * 64 + 64, ti, 0:V],
                        start=(c % per == 0), stop=True, skip_group_check=True),
                        reads=[KW("KEt", ti), KW("VT", ti)], writes=[("ps", bU)])
                    ub[c] = (bU, col)
                P.op("act", lambda e: e.activation(out=SBFp[0][:, 0:V], in_=SST2[0][:, h, 0:V], func=AF.Copy),
                     reads=[K("SST", 0, h)], writes=[KW("SBFp", 0)])
                for c in range(2 * ntile):
                    ti, cc = divmod(c, 2)
                    q0 = c * 64
                    if cc == 0:
                        bo = nb()
                        for vc in range(nvc):
                            P.op("pe", lambda e, bo=bo, ti=ti, vc=vc: e.matmul(
                                PS[bo][:, vc * 128:vc * 128 + 128], lhsT=VT[:, ti, vc * 128:(vc + 1) * 128], rhs=STm[ti][:, 0:128],
                                start=(vc == 0), stop=False, skip_group_check=True),
                                reads=[KW("VT", ti), KW("ST", ti)], writes=[("ps", bo)])
                    for vc in range(nvc):
                        P.op("pe", lambda e, vc=vc, q0=q0, cc=cc, bo=bo, c=c: e.matmul(
                            PS[bo][:, vc * 128 + cc * 64:vc * 128 + cc * 64 + 64],
                            lhsT=SBFp[c % 2][:, vc * 128:(vc + 1) * 128], rhs=QD[:, q0:q0 + 64],
                            start=False, stop=True, skip_group_check=True),
                            reads=[KW("SBFp", c % 2), KW("QD")], writes=[("ps", bo)])
                    bU, col = ub[c]
                    P.op("dve", lambda e, bU=bU, col=col, q0=q0, c=c: e.scalar_tensor_tensor(
                        out=SST2[(c + 1) % 2][:, h, 0:V], in0=SST2[c % 2][:, h, 0:V], scalar=T1[:, q0 + 63:q0 + 64], op0=ALU.mult,
                        in1=PS[bU][:, col:col + V], op1=ALU.add),
                        reads=[K("SST", c % 2, h), K("T1"), ("ps", bU)], writes=[K("SST", (c + 1) % 2, h)])
                    if c < 2 * ntile - 1:
                        P.op("act", lambda e, c=c: e.activation(out=SBFp[(c + 1) % 2][:, 0:V], in_=SST2[(c + 1) % 2][:, h, 0:V],
                                                              func=AF.Copy),
                             reads=[K("SST", (c + 1) % 2, h)], writes=[KW("SBFp", (c + 1) % 2)])
                    if cc == 1:
                        c0 = ti * 128
                        for vc, o32 in zip(range(nvc), (O32a, O32b)):
                            P.op("act", lambda e, vc=vc, o32=o32, c0=c0, bo=bo: e.activation(
                                out=o32[:, c0:c0 + 128], in_=PS[bo][:, vc * 128:vc * 128 + 128], func=AF.Copy),
                                reads=[("ps", bo)], writes=[K("O32", vc, ti)])
            if not sample:
                for ti in range(ntile):
                    c0 = ti * 128
                    b = nb()
                    P.op("pe", lambda e, b=b, c0=c0: e.matmul(PS[b][:, 0:128], lhsT=KI[:, c0:c0 + 128], rhs=QD[:, c0:c0 + 128],
                                                              start=True, stop=True),
                         reads=[KW("KI"), KW("QD")], writes=[("ps", b)])
                    P.op("dve", lambda e, b=b, ti=ti: e.tensor_tensor(out=STm[ti][:, 0:128], in0=PS[b][:, 0:128],
                                                                     in1=maskP[:, :], op=ALU.mult),
                         reads=[("ps", b), ("maskP",)], writes=[KW("ST", ti)])
                per = 512 // V
                ub = {}
                ubank = {}
                for c in range(2 * ntile):
                    ti, cc = divmod(c, 2)
                    slot = (cc, ti // per)
                    if slot not in ubank:
                        ubank[slot] = nb()
                    bU = ubank[slot]
                    col = (ti % per) * V
                    P.op("pe", lambda e, bU=bU, col=col, cc=cc, ti=ti: e.matmul(
                        PS[bU][:, col:col + V], lhsT=KEt[cc * 64:cc * 64 + 64, ti, :], rhs=VT[cc * 64:cc * 64 + 64, ti, 0:V],
                        start=(ti % per == 0), stop=True, skip_group_check=True),
                        reads=[KW("KEt", ti), KW("VT", ti)], writes=[("ps", bU)])
                    ub[c] = (bU, col)
                P.op("act", lambda e: e.activation(out=SBFp[0][:, 0:V], in_=SST2[0][:, h, 0:V], func=AF.Copy),
                     reads=[K("SST", 0, h)], writes=[KW("SBFp", 0)])
                for c in range(2 * ntile):
                    ti, cc = divmod(c, 2)
                    q0 = c * 64
                    if cc == 0:
                        bo = nb()
                        for vc in range(nvc):
                            P.op("pe", lambda e, bo=bo, ti=ti, vc=vc: e.matmul(
                                PS[bo][:, vc * 128:vc * 128 + 128], lhsT=VT[:, ti, vc * 128:(vc + 1) * 128], rhs=STm[ti][:, 0:128],
                                start=(vc == 0), stop=False, skip_group_check=True),
                                reads=[KW("VT", ti), KW("ST", ti)], writes=[("ps", bo)])
                    for vc in range(nvc):
                        P.op("pe", lambda e, vc=vc, q0=q0, cc=cc, bo=bo, c=c: e.matmul(
                            PS[bo][:, vc * 128 + cc * 64:vc * 128 + cc * 64 + 64],
                            lhsT=SBFp[c % 2][:, vc * 128:(vc + 1) * 128], rhs=QD[:, q0:q0 + 64],
                            start=False, stop=True, skip_group_check=True),
                            reads=[KW("SBFp", c % 2), KW("QD")], writes=[("ps", bo)])
                    bU, col = ub[c]
                    P.op("dve", lambda e, bU=bU, col=col, q0=q0, c=c: e.scalar_tensor_tensor(
                        out=SST2[(c + 1) % 2][:, h, 0:V], in0=SST2[c % 2][:, h, 0:V], scalar=T1[:, q0 + 63:q0 + 64], op0=ALU.mult,
                        in1=PS[bU][:, col:col + V], op1=ALU.add),
                        reads=[K("SST", c % 2, h), K("T1"), ("ps", bU)], writes=[K("SST", (c + 1) % 2, h)])
                    if c < 2 * ntile - 1:
                        P.op("act", lambda e, c=c: e.activation(out=SBFp[(c + 1) % 2][:, 0:V], in_=SST2[(c + 1) % 2][:, h, 0:V],
                                                              func=AF.Copy),
                             reads=[K("SST", (c + 1) % 2, h)], writes=[KW("SBFp", (c + 1) % 2)])
                    if cc == 1:
                        c0 = ti * 128
                        for vc, o32 in zip(range(nvc), (O32a, O32b)):
                            P.op("act", lambda e, vc=vc, o32=o32, c0=c0, bo=bo: e.activation(
                                out=o32[:, c0:c0 + 128], in_=PS[bo][:, vc * 128:vc * 128 + 128], func=AF.Copy),
                                reads=[("ps", bo)], writes=[K("O32", vc, ti)])
            for ti in (range(ntile) if sample else ()):
                c0 = ti * 128
                b = nb()
                P.op("pe", lambda e, b=b, c0=c0: e.matmul(PS[b][0:r, 0:r], lhsT=KI[:, c0:c0 + r], rhs=QD[:, c0:c0 + r],
                                                          start=True, stop=True),
                     reads=[KW("KI"), KW("QD")], writes=[("ps", b)])
                sm = STm[ti % 2]
                P.op("dve", lambda e, b=b, sm=sm: e.tensor_tensor(out=sm[0:r, 0:r], in0=PS[b][0:r, 0:r],
                                                                 in1=mask[0:r, 0:r], op=ALU.mult),
                     reads=[("ps", b), ("maskS" if sample else "maskP",)], writes=[KW("ST", ti % 2)])
                bo = nb()
                for vc in range(nvc):
                    ov = PS[bo][:, vc * 128:vc * 128 + r]
                    P.op("pe", lambda e, ov=ov, ti=ti, vc=vc, sm=sm: e.matmul(
                        ov, lhsT=VT[0:r, ti, vc * 128:(vc + 1) * 128], rhs=sm[0:r, 0:r], start=(vc == 0), stop=False,
                        skip_group_check=True),
                        reads=[KW("VT", ti), KW("ST", ti % 2)], writes=[("ps", bo)])
                if not sample:
                    for cc in range(2):
                        q0 = c0 + cc * 64
                        for vc in range(nvc):
                            P.op("pe", lambda e, vc=vc, q0=q0, cc=cc, bo=bo: e.matmul(
                                PS[bo][:, vc * 128 + cc * 64:vc * 128 + cc * 64 + 64],
                                lhsT=SBF[:, h, vc * 128:(vc + 1) * 128], rhs=QD[:, q0:q0 + 64],
                                start=False, stop=True, skip_group_check=True),
                                reads=[KW("SBF", h), KW("QD")], writes=[("ps", bo)])
                        bs = nb()
                        P.op("pe", lambda e, bs=bs, cc=cc, ti=ti: e.matmul(
                            PS[bs][:, 0:V], lhsT=KEt[cc * 64:cc * 64 + 64, ti, :], rhs=VT[cc * 64:cc * 64 + 64, ti, 0:V],
                            start=True, stop=True), reads=[KW("KEt", ti), KW("VT", ti)], writes=[("ps", bs)])
                        P.op("dve", lambda e, bs=bs, q0=q0: e.scalar_tensor_tensor(
                            out=SST[:, h, 0:V], in0=SST[:, h, 0:V], scalar=T1[:, q0 + 63:q0 + 64], op0=ALU.mult,
                            in1=PS[bs][:, 0:V], op1=ALU.add),
                            reads=[K("SST", 0, h), K("T1"), ("ps", bs)], writes=[K("SST", 0, h)])
                        P.op("act", lambda e: e.activation(out=SBF[:, h, 0:V], in_=SST[:, h, 0:V], func=AF.Copy),
                             reads=[K("SST", 0, h)], writes=[KW("SBF", h)])
                else:
                    nper = 1024 // V
                    for q0_ in range(0, 16, nper):
                        hb_ = (q0_ // nper) % 2
                        P.op("act", lambda e, q0_=q0_, hb_=hb_: e.activation(
                            out=hn[hb_][:, 0:nper * V].rearrange("p (s v) -> p s v", v=V), in_=S0F[:, q0_:q0_ + nper, 0:V], func=AF.Copy),
                            reads=[K("S0F")], writes=[("hn", hb_)])
                        for sq in range(q0_, q0_ + nper):
                            for vc in range(nvc):
                                o_ = (sq - q0_) * V + vc * 128
                                P.op("pe", lambda e, vc=vc, sq=sq, hb_=hb_, o_=o_, bo=bo: e.matmul(
                                    PS[bo][:, vc * 128 + sq * 4:vc * 128 + sq * 4 + 4],
                                    lhsT=hn[hb_][:, o_:o_ + 128], rhs=QD[:, sq * 4:sq * 4 + 4],
                                    start=False, stop=True, skip_group_check=True),
                                    reads=[("hn", hb_), KW("QD")], writes=[("ps", bo)])
                for vc, o32 in zip(range(nvc), (O32a, O32b)):
                    P.op("act", lambda e, vc=vc, o32=o32, c0=c0, bo=bo: e.activation(
                        out=o32[:, c0:c0 + r], in_=PS[bo][:, vc * 128:vc * 128 + r], func=AF.Copy),
                        reads=[("ps", bo)], writes=[K("O32", vc, ti)])
            if sample:
                for q4 in range(4):
                    P.op("dve", lambda e, q4=q4: e.tensor_tensor(
                        out=VBLK[:, :, 0:V], in0=VT[0:64, 0:1, 0:V].to_broadcast([64, 4, V]),
                        in1=maskS[0:64, 64 + q4 * 4:64 + q4 * 4 + 4].unsqueeze(2).to_broadcast([64, 4, V]), op=ALU.mult),
                        reads=[KW("VT", 0), ("maskS",)], writes=[KW("VBLK")])
                    for s4 in range(4):
                        sq = q4 * 4 + s4
                        bs = nb()
                        P.op("pe", lambda e, bs=bs, s4=s4: e.matmul(
                            PS[bs][:, 0:V], lhsT=KEt[0:64, 0, :], rhs=VBLK[:, s4, 0:V], start=True, stop=True),
                            reads=[KW("KEt", 0), KW("VBLK")], writes=[("ps", bs)])
                        P.op("dve", lambda e, bs=bs, sq=sq: e.scalar_tensor_tensor(
                            out=S0F[:, sq, 0:V], in0=S0F[:, sq, 0:V], scalar=T1[:, sq * 4 + 3:sq * 4 + 4], op0=ALU.mult,
                            in1=PS[bs][:, 0:V], op1=ALU.add),
                            reads=[K("S0F"), K("T1"), ("ps", bs)], writes=[K("S0F")])
                P.op("sp", lambda e: e.dma_start(out=ss_d[:, h].rearrange("s k v -> k s v"), in_=S0F[:, :, 0:V]),
                     reads=[K("S0F")], dma=True)
            bq = nb()
            for vc, o32 in zip(range(nvc), (O32a, O32b)):
                sqb, sqk = (QD, KW("QD")) if vc == 0 else (KI, KW("KI"))
                P.op("act", lambda e, o32=o32, sqb=sqb: e.activation(out=sqb[:, 0:n], in_=o32[:, 0:n], func=AF.Square),
                     reads=[K("O32", vc)], writes=[sqk])
                P.op("pe", lambda e, vc=vc, sqb=sqb: e.matmul(PS[bq][:, 0:n], lhsT=ones_bf[:, :], rhs=sqb[:, 0:n],
                                                              start=(vc == 0), stop=(vc == nvc - 1)),
                     reads=[sqk, ("ones",)], writes=[("ps", bq)])
            P.op("dve", lambda e: e.tensor_scalar(out=RSTD[:, 0:n], in0=PS[bq][:, 0:n], scalar1=1.0 / V, scalar2=EPS,
                                                  op0=ALU.mult, op1=ALU.add),
                 reads=[("ps", bq)], writes=[K("RSTD")])
            P.op("act", lambda e: e.activation(out=RSTD[:, 0:n], in_=RSTD[:, 0:n], func=AF.Ln),
                 reads=[K("RSTD")], writes=[K("RSTD")])
            P.op("act", lambda e: e.activation(out=RSTD[:, 0:n], in_=RSTD[:, 0:n], func=AF.Exp, scale=-0.5),
                 reads=[K("RSTD")], writes=[K("RSTD")])
            for vc, o32, gt in zip(range(nvc), (O32a, O32b), (GTa, GTb)):
                P.op("dve", lambda e, o32=o32, vc=vc: e.scalar_tensor_tensor(
                    out=o32[:, 0:n], in0=o32[:, 0:n], scalar=cols[:, normcol0 + vc:normcol0 + vc + 1], op0=ALU.mult,
                    in1=RSTD[:, 0:n], op1=ALU.mult),
                    reads=[K("O32", vc), K("RSTD"), ("cols",)], writes=[K("O32", vc)])
                P.op("dve", lambda e, o32=o32, gt=gt, vc=vc: e.tensor_tensor(
                    out=OAT[:, oc0 + vc, 0:n], in0=o32[:, 0:n], in1=gt[:, 0:n], op=ALU.mult),
                    reads=[K("O32", vc), K("GT", vc)], writes=[KW("OAT", oc0 + vc)])

        def out_proj(w_out, t0, n, nfc):
            slots = []
            for fc in range(nfc):
                s_ = wslot[0] % 8
                wslot[0] += 1
                P.op("pool", lambda e, s_=s_, fc=fc: e.dma_start(
                    out=Wr[s_][:].rearrange("p c n -> p (c n)"), in_=w_out[fc * 128:(fc + 1) * 128, :]),
                    writes=[("Wr", s_)], dma=True)
                slots.append(s_)
            for ti in range(max(1, n // 128)):
                r = min(128, n)
                tt = t0 // 128 + ti
                for dh in range(2):
                    b = nb()
                    for fc in range(nfc):
                        P.op("pe", lambda e, fc=fc, b=b, ti=ti, dh=dh: e.matmul(
                            PS[b][0:r, :], lhsT=OAT[:, fc, ti * 128:ti * 128 + r],
                            rhs=Wr[slots[fc]][:].rearrange("p c n -> p (c n)")[:, dh * 512:(dh + 1) * 512],
                            start=(fc == 0), stop=(fc == nfc - 1)),
                            reads=[KW("OAT", fc), ("Wr", slots[fc])], writes=[("ps", b)])
                    P.op("dve", lambda e, b=b, tt=tt, dh=dh: e.tensor_tensor(
                        out=X[0:r, tt, dh * 512:(dh + 1) * 512], in0=PS[b][0:r, :],
                        in1=X[0:r, tt, dh * 512:(dh + 1) * 512], op=ALU.add),
                        reads=[("ps", b), ("X", tt)], writes=[("X", tt)])

        def evac(func, dst, dkey, b, n, rows_=128, **kw):
            P.op("act", lambda e: e.activation(out=dst, in_=PS[b][0:rows_, 0:n], func=func, **kw),
                 reads=[("ps", b)], writes=[dkey])

        def hgrn_head(si, t0, n, h, sample):
            W = ab_w_in[0]
            b = proj_fm(W, h * 128, t0, n)
            evac(AF.Silu, QV[:, 0:n], K("QV"), b, n)
            b = proj_fm(W, 1536 + h * 128, t0, n)
            evac(AF.Silu, GTa[:, 0:n], K("GT", 0), b, n)
            b = proj_fm(W, 512 + h * 128, t0, n)
            evac(AF.Exp, T1[:, 0:n], K("T1"), b, n, scale=-1.0)
            P.op("dve", lambda e: e.tensor_scalar(out=T1[:, 0:n], in0=T1[:, 0:n], scalar1=1.0, scalar2=None, op0=ALU.add),
                 reads=[K("T1")], writes=[K("T1")])
            P.op("act", lambda e: e.activation(out=T1[:, 0:n], in_=T1[:, 0:n], func=AF.Ln),
                 reads=[K("T1")], writes=[K("T1")])
            P.op("act", lambda e: e.activation(out=T1[:, 0:n], in_=T1[:, 0:n], func=AF.Exp, scale=-1.0),
                 reads=[K("T1")], writes=[K("T1")])
            P.op("dve", lambda e: e.tensor_scalar(out=T1[:, 0:n], in0=T1[:, 0:n], scalar1=cols[:, 8 + h:9 + h],
                                                  scalar2=cols[:, h:h + 1], op0=ALU.mult, op1=ALU.add),
                 reads=[K("T1"), ("cols",)], writes=[K("T1")])
            P.op("dve", lambda e: e.tensor_scalar(out=KV[:, 0:n], in0=T1[:, 0:n], scalar1=-1.0, scalar2=1.0,
                                                  op0=ALU.mult, op1=ALU.add),
                 reads=[K("T1")], writes=[K("KV")])
            P.op("act", lambda e: e.activation(out=LF[:, 0:n], in_=T1[:, 0:n], func=AF.Ln),
                 reads=[K("T1")], writes=[K("LF")])
            proj_tm(W, 1024 + h * 128, t0, n, lambda ti, r: VT[0:r, ti, 0:128], KW("VT"))
            gla_core(si, t0, n, h, 128, sample, hs0, hgp, hgs, 12, h)

        identF = CST[:, 0:128]
        onesF = CST[:, 128:256]
        blkS = CST[0:64, 256:320]
        lastS = CST[0:64, 320:336]
        colA = CST[:, 336:337]
        colB = CST[:, 337:338]
        CW = cols[:, 24:56].rearrange("p (c k) -> p c k", k=4)
        CB = cols[:, 56:64]
        DTB, AROW, DROW, SNW = cols2[:, 0:8], cols2[:, 8:16], cols2[:, 16:24], cols2[:, 24:28]
        MS = aTf[:, 11104:11616]
        MSB = Wob[:, 8448:8960]

        def mamba_setup():
            P.op("sp", lambda e: e.dma_start(out=CST[:], in_=cst_d), writes=[("cst",)], dma=True)
            for k in range(4):
                P.op("sp", lambda e, k=k: e.dma_start(out=CW[:, :, k], in_=conv_w[0, k].rearrange("(c p) -> p c", p=128)),
                     writes=[("cols", "cw", k)], dma=True)
            P.op("sp", lambda e: e.dma_start(out=CB, in_=conv_b[0].rearrange("(c p) -> p c", p=128)),
                 writes=[("cols", "cb")], dma=True)
            P.op("sp", lambda e: e.dma_start(out=DTB, in_=dt_bias[0].partition_broadcast(128)), writes=[("cols2", "dtb")], dma=True)
            P.op("sp", lambda e: e.dma_start(out=AROW, in_=a_log[0].partition_broadcast(128)), writes=[("cols2", "a")], dma=True)
            P.op("sp", lambda e: e.dma_start(out=DROW, in_=ssm_d[0].partition_broadcast(128)), writes=[("cols2", "d")], dma=True)
            P.op("sp", lambda e: e.dma_start(out=SNW, in_=ssm_norm[0].rearrange("(c p) -> p c", p=128)),
                 writes=[("cols2", "snw")], dma=True)
            P.op("act", lambda e: e.activation(out=AROW, in_=AROW, func=AF.Exp), reads=[("cols2", "a")], writes=[("cols2", "a")])
            P.op("dve", lambda e: e.tensor_scalar(out=AROW, in0=AROW, scalar1=-1.0, scalar2=None, op0=ALU.mult),
                 reads=[("cols2", "a")], writes=[("cols2", "a")])
            P.op("dve", lambda e: e.memset(TAIL[:, :, :], 0.0), writes=[("tail",)])

        def KM(*a):
            return ("aT", "mx", "m") + a

        def KWM(*a):
            return ("Wo", "mx", "m") + a

        def mamba_tile_group(si, t0, n):
            sample = (si == 4)
            W = ab_w_in[0]
            r = min(128, n)
            ntile = max(1, n // 128)
            barrier()
            if si == 0:
                P.op("dve", lambda e: e.memset(MS, 0.0), writes=[KM("MS")])
                P.op("dve", lambda e: e.memset(MSB, 0.0), writes=[KWM("MSB")])
            fa = [0]

            def takeF(k, lo=None):
                o = fa[0]
                fa[0] += k
                assert fa[0] <= 5632
                return aTf[:, o:o + k]
            CONVX = takeF(2048).rearrange("p (c n) -> p c n", c=4)
            A1 = takeF(512)
            A2 = takeF(512)
            ZS = takeF(2048).rearrange("p (t f) -> p t f", t=4)
            XTOK = takeF(512)
            ZSf = ZS.rearrange("p t f -> p (t f)")
            CS0T = ZSf[0:48, 0:1024]
            CVOUT = ZSf[0:48, 1024:2048]
            fb = [6656]

            def takeG(k):
                o = fb[0]
                fb[0] += k
                assert fb[0] <= 11104
                return aTf[:, o:o + k]
            XB = takeG(8 * 520).rearrange("p (c n) -> p c n", c=8)
            wa = [0]

            def takeW(k):
                o = wa[0]
                wa[0] += k
                assert wa[0] <= 3328
                return Wob[:, o:o + k]
            BT = takeW(2 * n).rearrange("p (g n) -> p g n", g=2)
            CT = takeW(2 * n).rearrange("p (g n) -> p g n", g=2)
            BTOK = takeW(256)
            XDT = takeW(512)
            XEND = takeW(512)
            wb = [8960]

            def takeW2(k):
                o = wb[0]
                wb[0] += k
                assert wb[0] <= 11264
                return Wob[:, o:o + k]
            _mm = takeW2(512)
            MM4 = [_mm, _mm]
            YN = takeW2(512)
            CTm = takeW2(512).rearrange("p (g c i) -> p g c i", g=2, c=2)

            for c in range(8):
                b = proj_fm(W, 2560 + c * 128, t0, n)
                if not sample:
                    P.op("dve", lambda e, c=c: e.tensor_copy(out=XB[:, c, 0:3], in_=TAIL[:, c, 0:3]),
                         reads=[("tail", c)], writes=[KM("XB", c)])
                    P.op("act", lambda e, c=c, b=b: e.activation(out=XB[:, c, 3:3 + n], in_=PS[b][:, 0:n], func=AF.Copy),
                         reads=[("ps", b)], writes=[KM("XB", c)])
                    P.op("dve", lambda e, c=c: e.tensor_copy(out=TAIL[:, c, 0:3], in_=XB[:, c, n:n + 3]),
                         reads=[KM("XB", c)], writes=[("tail", c)])
                    if si == 3:
                        P.op("sp", lambda e, c=c: e.dma_start(out=cvp[:, c * 128:(c + 1) * 128].rearrange("j f -> f j"),
                                                              in_=XB[:, c, n:n + 3]), reads=[KM("XB", c)], dma=True)
                    xin = lambda k, c=c: XB[:, c, k:k + n]
                    AC, ack = (A1, KM("A1")) if c % 2 == 0 else (A2, KM("A2"))
                    acc = AC[:, 0:n]
                else:
                    xb3 = XB[:, c, 0:112].rearrange("p (s t) -> p s t", t=7)
                    if c == 0:
                        P.op("sp", lambda e: e.dma_start(out=CS0T, in_=cs0.rearrange("s j f -> (s j) f")),
                             writes=[KM("ZS")], dma=True)
                    bc = nb()
                    P.op("pe", lambda e, c=c, bc=bc: e.transpose(out=PS[bc][:, 0:48], in_=CS0T[:, c * 128:(c + 1) * 128],
                                                                 identity=identF[0:48, 0:48]),
                         reads=[KM("ZS"), ("cst",)], writes=[("ps", bc)])
                    P.op("act", lambda e, bc=bc, xb3=xb3: e.activation(
                        out=xb3[:, :, 0:3], in_=PS[bc][:, 0:48].rearrange("p (s t) -> p s t", t=3), func=AF.Copy),
                        reads=[("ps", bc)], writes=[KM("XB", c)])
                    P.op("act", lambda e, c=c, b=b, xb3=xb3: e.activation(
                        out=xb3[:, :, 3:7], in_=PS[b][:, 0:64].rearrange("p (s t) -> p s t", t=4), func=AF.Copy),
                        reads=[("ps", b)], writes=[KM("XB", c)])
                    P.op("dve", lambda e, xb3=xb3: e.tensor_copy(out=A2[:, 0:48].rearrange("p (s t) -> p s t", t=3), in_=xb3[:, :, 4:7]),
                         reads=[KM("XB", c)], writes=[KM("A2")])
                    P.op("pe", lambda e, c=c: e.transpose(out=PS[6 + c // 4][0:48, (c % 4) * 128:(c % 4) * 128 + 128], in_=A2[:, 0:48],
                                                          identity=identF),
                         reads=[KM("A2"), ("cst",)], writes=[("ps", 6 + c // 4)])
                    if c == 7:
                        for hb_ in range(2):
                            P.op("act", lambda e, hb_=hb_: e.activation(out=CVOUT[:, hb_ * 512:(hb_ + 1) * 512],
                                                                        in_=PS[6 + hb_][0:48, :], func=AF.Copy),
                                 reads=[("ps", 6 + hb_)], writes=[KM("ZS")])
                        P.op("sp", lambda e: e.dma_start(out=cvs.rearrange("s j f -> (s j) f"), in_=CVOUT),
                             reads=[KM("ZS")], dma=True)
                    xin = lambda k, xb3=xb3: xb3[:, :, k:k + 4]
                    AC, ack = A1, KM("A1")
                    acc = A1[:, 0:64].rearrange("p (s t) -> p s t", t=4)
                P.op("dve", lambda e, c=c, xin=xin, acc=acc: e.tensor_scalar(
                    out=acc, in0=xin(0), scalar1=CW[:, c, 0:1], scalar2=CB[:, c:c + 1], op0=ALU.mult, op1=ALU.add),
                    reads=[KM("XB", c), ("cols",)], writes=[ack])
                for k in range(1, 4):
                    P.op("dve", lambda e, c=c, k=k, xin=xin, acc=acc: e.scalar_tensor_tensor(
                        out=acc, in0=xin(k), scalar=CW[:, c, k:k + 1], op0=ALU.mult, in1=acc, op1=ALU.add),
                        reads=[KM("XB", c), ack, ("cols",)], writes=[ack])
                if c < 4:
                    dst, dk = CONVX[:, c, 0:n], KM("CONVX", c)
                elif c < 6:
                    dst, dk = BT[:, c - 4, 0:n], KWM("BT", c - 4)
                else:
                    dst, dk = CT[:, c - 6, 0:n], KWM("CT", c - 6)
                P.op("act", lambda e, dst=dst, AC=AC: e.activation(out=dst, in_=AC[:, 0:n], func=AF.Silu),
                     reads=[ack], writes=[dk])
            for zc in range(4):
                proj_tm(W, 2048 + zc * 128, t0, n, lambda ti, rr, zc=zc: ZS[0:rr, ti, zc * 128:(zc + 1) * 128],
                        KM("ZS"), func=AF.Silu)
            sDT = wload(W, 3584, 8)
            barrier()
            fb[0] = 6656
            CBm = [takeG(128), takeG(128)]
            DG4 = [takeG(4 * r), takeG(4 * r)]
            DM4 = [takeG(4 * r), takeG(4 * r)]
            if sample:
                SNAT = [takeG(512).rearrange("p (a n) -> p a n", a=4) for _ in range(2)]
                SNEW = [takeG(512).rearrange("p (a n) -> p a n", a=4) for _ in range(2)]
                ETR = takeG(512)
                DECP = takeG(64).rearrange("p (a s) -> p a s", a=4)
                XENDm = takeW(512)
                S0T = [takeW(512), MSB]
            mask = maskS if sample else maskP
            mkey = ("maskS",) if sample else ("maskP",)

            if sample:
                XTOKs, A1s, XENDs, BTOKs = [XTOK, XTOK], [A1, A1], [XEND, XEND], [BTOK, BTOK]
            else:
                XTOKs, A1s = [XTOK, takeG(512)], [A1, takeG(512)]
                XENDs, BTOKs = [XEND, takeW2(512)], [BTOK, takeW2(256)]
            Wd = ntile * 8
            DTw, DTAw, CUMw, TOTw, ECUMw, EENDw, TAw, TBw, DECAw, DECBw, DTMw = [takeG(Wd) for _ in range(11)]
            SSQ, RSQ = takeG(8), takeG(8)
            v3 = lambda a: a[0:r, 0:Wd].rearrange("p (t h) -> p t h", h=8)
            b = nb()
            for ti in range(ntile):
                tt = t0 // 128 + ti
                for c in range(8):
                    P.op("pe", lambda e, c=c, b=b, tt=tt, ti=ti: e.matmul(
                        PS[b][0:r, ti * 8:(ti + 1) * 8], lhsT=hT[:, c, tt * 128:tt * 128 + r], rhs=Wr[sDT][:, c, 0:8],
                        start=(c == 0 and ti == 0), stop=(c == 7), skip_group_check=True),
                        reads=[("Wr", sDT), ("hT", tt)], writes=[("ps", b)])
            P.op("dve", lambda e, b=b: e.tensor_tensor(out=v3(DTw), in0=PS[b][0:r, 0:Wd].rearrange("p (t h) -> p t h", h=8),
                                                       in1=DTB[0:r].unsqueeze(1).to_broadcast([r, ntile, 8]), op=ALU.add),
                 reads=[("ps", b), ("cols2",)], writes=[KM("DT")])
            P.op("act", lambda e: e.activation(out=DTw[0:r, 0:Wd], in_=DTw[0:r, 0:Wd], func=AF.Exp), reads=[KM("DT")], writes=[KM("DT")])
            P.op("dve", lambda e: e.tensor_scalar(out=DTw[0:r, 0:Wd], in0=DTw[0:r, 0:Wd], scalar1=1.0, scalar2=None, op0=ALU.add),
                 reads=[KM("DT")], writes=[KM("DT")])
            P.op("act", lambda e: e.activation(out=DTw[0:r, 0:Wd], in_=DTw[0:r, 0:Wd], func=AF.Ln), reads=[KM("DT")], writes=[KM("DT")])
            P.op("dve", lambda e: e.tensor_tensor(out=v3(DTAw), in0=v3(DTw), in1=AROW[0:r].unsqueeze(1).to_broadcast([r, ntile, 8]),
                                                  op=ALU.mult), reads=[KM("DT"), ("cols2",)], writes=[KM("DTA")])
            b = nb()
            P.op("pe", lambda e, b=b: e.matmul(PS[b][0:r, 0:Wd], lhsT=mask[0:r, 0:r], rhs=DTAw[0:r, 0:Wd], start=True, stop=True),
                 reads=[mkey, KM("DTA")], writes=[("ps", b)])
            P.op("act", lambda e, b=b: e.activation(out=CUMw[0:r, 0:Wd], in_=PS[b][0:r, 0:Wd], func=AF.Copy),
                 reads=[("ps", b)], writes=[KM("CUM")])
            if not sample:
                for (cm, TX, DECX) in ((colA, TAw, DECAw), (colB, TBw, DECBw)):
                    P.op("dve", lambda e, cm=cm: e.tensor_scalar(out=DTMw[:, 0:Wd], in0=DTAw[:, 0:Wd], scalar1=cm, scalar2=None, op0=ALU.mult),
                         reads=[KM("DTA"), ("cst",)], writes=[KM("DTM")])
                    b = nb()
                    P.op("pe", lambda e, b=b: e.matmul(PS[b][:, 0:Wd], lhsT=onesF, rhs=DTMw[:, 0:Wd], start=True, stop=True),
                         reads=[("cst",), KM("DTM")], writes=[("ps", b)])
                    P.op("act", lambda e, b=b, TX=TX: e.activation(out=TX[:, 0:Wd], in_=PS[b][:, 0:Wd], func=AF.Copy),
                         reads=[("ps", b)], writes=[KM("TX")])
                    P.op("act", lambda e, TX=TX, DECX=DECX: e.activation(out=DECX[:, 0:Wd], in_=TX[:, 0:Wd], func=AF.Exp),
                         reads=[KM("TX")], writes=[KM("DEC")])
                P.op("dve", lambda e: e.tensor_scalar(out=TOTw[:, 0:Wd], in0=TAw[:, 0:Wd], scalar1=colA, scalar2=None, op0=ALU.mult),
                     reads=[KM("TX"), ("cst",)], writes=[KM("TOT")])
                P.op("dve", lambda e: e.scalar_tensor_tensor(out=TOTw[:, 0:Wd], in0=TBw[:, 0:Wd], scalar=colB, op0=ALU.mult,
                                                             in1=TOTw[:, 0:Wd], op1=ALU.add),
                     reads=[KM("TX"), KM("TOT"), ("cst",)], writes=[KM("TOT")])
            else:
                b = nb()
                P.op("pe", lambda e, b=b: e.matmul(PS[b][0:64, 0:8], lhsT=blkS, rhs=DTAw[0:64, 0:8], start=True, stop=True),
                     reads=[("cst",), KM("DTA")], writes=[("ps", b)])
                P.op("act", lambda e, b=b: e.activation(out=TOTw[0:64, 0:8], in_=PS[b][0:64, 0:8], func=AF.Copy),
                     reads=[("ps", b)], writes=[KM("TOT")])
            P.op("act", lambda e: e.activation(out=ECUMw[0:r, 0:Wd], in_=CUMw[0:r, 0:Wd], func=AF.Exp), reads=[KM("CUM")], writes=[KM("ECUM")])
            P.op("dve", lambda e: e.tensor_tensor(out=EENDw[0:r, 0:Wd], in0=TOTw[0:r, 0:Wd], in1=CUMw[0:r, 0:Wd], op=ALU.subtract),
                 reads=[KM("TOT"), KM("CUM")], writes=[KM("EEND")])
            P.op("act", lambda e: e.activation(out=EENDw[0:r, 0:Wd], in_=EENDw[0:r, 0:Wd], func=AF.Exp), reads=[KM("EEND")], writes=[KM("EEND")])

            def _tile(ti, part):
                tt = t0 // 128 + ti
                c0 = ti * 128
                DT, DTA, CUM, TOT, ECUM, EEND, TA, TB, DECA, DECB = [a_[:, ti * 8:(ti + 1) * 8] for a_ in
                                                                     (DTw, DTAw, CUMw, TOTw, ECUMw, EENDw, TAw, TBw, DECAw, DECBw)]
                pp = ti % 2
                XTOK, A1, XEND, BTOK = XTOKs[pp], A1s[pp], XENDs[pp], BTOKs[pp]
                x3 = XTOK[0:r].rearrange("p (h q) -> p h q", q=64)
                if part == 0:
                    b = nb()
                    for c in range(4):
                        P.op("pe", lambda e, c=c, b=b, c0=c0: e.transpose(out=PS[b][0:r, c * 128:(c + 1) * 128],
                                                                         in_=CONVX[:, c, c0:c0 + r], identity=identF),
                             reads=[KM("CONVX", c), ("cst",)], writes=[("ps", b)])
                    P.op("act", lambda e, b=b: e.activation(out=XTOK[0:r], in_=PS[b][0:r, :], func=AF.Copy),
                         reads=[("ps", b)], writes=[KM("XTOK", pp)])
                    P.op("dve", lambda e, x3=x3: e.tensor_tensor(
                        out=XDT[0:r].rearrange("p (h q) -> p h q", q=64), in0=x3,
                        in1=DT[0:r].unsqueeze(2).to_broadcast([r, 8, 64]), op=ALU.mult),
                        reads=[KM("XTOK", pp), KM("DT")], writes=[KWM("XDT")])
                    P.op("dve", lambda e: e.tensor_tensor(
                        out=XEND[0:r].rearrange("p (h q) -> p h q", q=64), in0=XDT[0:r].rearrange("p (h q) -> p h q", q=64),
                        in1=EEND[0:r].unsqueeze(2).to_broadcast([r, 8, 64]), op=ALU.mult),
                        reads=[KWM("XDT"), KM("EEND")], writes=[KWM("XEND", pp)])
                    b = nb()
                    pvb = PS[b][:].bitcast(BF16)
                    for g in range(2):
                        P.op("pe", lambda e, g=g, pvb=pvb, c0=c0: e.transpose(out=pvb[0:r, g * 128:(g + 1) * 128],
                                                                             in_=BT[:, g, c0:c0 + r], identity=ident[:, :]),
                             reads=[KWM("BT", g), ("ident",)], writes=[("ps", b)])
                    P.op("act", lambda e, pvb=pvb, b=b: e.activation(out=BTOK[0:r], in_=pvb[0:r, 0:256], func=AF.Copy),
                         reads=[("ps", b)], writes=[KWM("BTOK", pp)])
                    byi = 6
                    for g in range(2):
                        b = nb()
                        P.op("pe", lambda e, g=g, b=b, c0=c0: e.matmul(PS[b][0:r, 0:r], lhsT=BT[:, g, c0:c0 + r],
                                                                      rhs=CT[:, g, c0:c0 + r], start=True, stop=True),
                             reads=[KWM("BT", g), KWM("CT", g)], writes=[("ps", b)])
                        P.op("dve", lambda e, g=g, b=b: e.tensor_tensor(out=CBm[g][0:r, 0:r], in0=PS[b][0:r, 0:r],
                                                                        in1=mask[0:r, 0:r], op=ALU.mult),
                             reads=[("ps", b), mkey], writes=[KM("CBm", g)])
                    for g in range(2):
                        dg, dm, mm4 = DG4[g], DM4[g], MM4[g]
                        cum4 = CUM[0:r, 4 * g:4 * g + 4]
                        P.op("dve", lambda e, dg=dg, cum4=cum4: e.tensor_tensor(
                            out=dg[0:r, :].rearrange("p (h i) -> p h i", h=4),
                            in0=identF[0:r, 0:r].unsqueeze(1).to_broadcast([r, 4, r]),
                            in1=cum4.unsqueeze(2).to_broadcast([r, 4, r]), op=ALU.mult),
                            reads=[("cst",), KM("CUM")], writes=[KM("DG", g)])
                        b = nb()
                        P.op("pe", lambda e, b=b, dg=dg: e.matmul(PS[b][0:r, 0:4 * r], lhsT=onesF[0:r, 0:r], rhs=dg[0:r, 0:4 * r],
                                                                  start=True, stop=True),
                             reads=[("cst",), KM("DG", g)], writes=[("ps", b)])
                        P.op("dve", lambda e, b=b, dm=dm, cum4=cum4: e.tensor_tensor(
                            out=dm[0:r, :].rearrange("p (h i) -> p h i", h=4),
                            in0=PS[b][0:r, 0:4 * r].rearrange("p (h i) -> p h i", h=4),
                            in1=cum4.unsqueeze(2).to_broadcast([r, 4, r]), op=ALU.subtract),
                            reads=[("ps", b), KM("CUM")], writes=[KM("DM", g)])
                        P.op("act", lambda e, dm=dm: e.activation(out=dm[0:r, 0:4 * r], in_=dm[0:r, 0:4 * r], func=AF.Exp),
                             reads=[KM("DM", g)], writes=[KM("DM", g)])
                        P.op("dve", lambda e, dm=dm, mm4=mm4, g=g: e.scalar_tensor_tensor(
                            out=mm4[0:r, 0:4 * r].rearrange("p (h i) -> p h i", h=4),
                            in0=dm[0:r, :].rearrange("p (h i) -> p h i", h=4), scalar=1.0, op0=ALU.min,
                            in1=CBm[g][0:r, 0:r].unsqueeze(1).to_broadcast([r, 4, r]), op1=ALU.mult),
                            reads=[KM("DM", g), KM("CBm", g)], writes=[KWM("MM", 0)])
                        for hh in range(4):
                            h = 4 * g + hh
                            P.op("pe", lambda e, h=h, hh=hh, mm4=mm4, byi=byi: e.matmul(
                                PS[byi][0:r, h * 64:(h + 1) * 64], lhsT=mm4[0:r, hh * r:(hh + 1) * r], rhs=XDT[0:r, h * 64:(h + 1) * 64],
                                start=(h == 0), stop=(h == 7), skip_group_check=True),
                                reads=[KWM("MM", 0), KWM("XDT")], writes=[("ps", byi)])
                    P.op("act", lambda e, byi=byi: e.activation(out=A1[0:r, :], in_=PS[byi][0:r, :], func=AF.Copy),
                         reads=[("ps", byi)], writes=[KM("A1", pp)])
                    return
                byx = 7
                if not sample:
                    for cc in range(2):
                        P.op("dve", lambda e, cc=cc, c0=c0: e.tensor_copy(out=CTm[:, :, cc, cc * 64:cc * 64 + 64],
                                                                         in_=CT[:, :, c0 + cc * 64:c0 + cc * 64 + 64]),
                             reads=[KWM("CT")], writes=[KWM("CTm", cc)])
                        P.op("dve", lambda e, cc=cc: e.memset(CTm[:, :, cc, (1 - cc) * 64:(1 - cc) * 64 + 64], 0.0),
                             writes=[KWM("CTm", cc)])
                    for cc in range(2):
                        for g in range(2):
                            P.op("pe", lambda e, cc=cc, g=g, byx=byx: e.matmul(
                                PS[byx][:, g * 256:(g + 1) * 256], lhsT=CTm[:, g, cc, :], rhs=MSB[:, g * 256:(g + 1) * 256],
                                start=(cc == 0 and g == 0), stop=(cc == 1 and g == 1), skip_group_check=True),
                                reads=[KWM("CTm", cc), KWM("MSB")], writes=[("ps", byx)])
                        bu = nb()
                        for g in range(2):
                            P.op("pe", lambda e, cc=cc, g=g, bu=bu: e.matmul(
                                PS[bu][:, g * 256:(g + 1) * 256], lhsT=BTOK[cc * 64:cc * 64 + 64, g * 128:(g + 1) * 128],
                                rhs=XEND[cc * 64:cc * 64 + 64, g * 256:(g + 1) * 256], start=(g == 0), stop=(g == 1),
                                skip_group_check=True),
                                reads=[KWM("BTOK", pp), KWM("XEND", pp)], writes=[("ps", bu)])
                        DECX = DECA if cc == 0 else DECB
                        P.op("dve", lambda e, DECX=DECX: e.tensor_tensor(
                            out=MS.rearrange("p (h q) -> p h q", q=64), in0=MS.rearrange("p (h q) -> p h q", q=64),
                            in1=DECX.unsqueeze(2).to_broadcast([128, 8, 64]), op=ALU.mult),
                            reads=[KM("MS"), KM("DEC")], writes=[KM("MS")])
                        P.op("dve", lambda e, bu=bu: e.tensor_tensor(out=MS, in0=MS, in1=PS[bu][:, :], op=ALU.add),
                             reads=[KM("MS"), ("ps", bu)], writes=[KM("MS")])
                        P.op("act", lambda e: e.activation(out=MSB, in_=MS, func=AF.Copy), reads=[KM("MS")], writes=[KWM("MSB")])
                else:
                    P.op("act", lambda e: e.activation(out=TA[0:64], in_=TOT[0:64], func=AF.Exp), reads=[KM("TOT")], writes=[KM("TX")])
                    P.op("dve", lambda e: e.tensor_copy(out=ETR[0:64].rearrange("p (h q) -> p h q", q=64),
                                                        in_=TA[0:64].unsqueeze(2).to_broadcast([64, 8, 64])),
                         reads=[KM("TX")], writes=[KM("ETR")])
                    bd = nb()
                    for a in range(4):
                        P.op("pe", lambda e, a=a, bd=bd: e.matmul(PS[bd][:, a * 16:(a + 1) * 16], lhsT=ETR[0:64, a * 128:(a + 1) * 128],
                                                                  rhs=lastS, start=(a == 0), stop=(a == 3), skip_group_check=True),
                             reads=[KM("ETR"), ("cst",)], writes=[("ps", bd)])
                    P.op("act", lambda e, bd=bd: e.activation(out=DECP, in_=PS[bd][:, 0:64].rearrange("p (a s) -> p a s", a=4),
                                                              func=AF.Copy), reads=[("ps", bd)], writes=[KM("DECP")])
                    for sq in range(16):
                        sn, snew, s0t = SNAT[sq % 2], SNEW[sq % 2], S0T[sq % 2]
                        P.op("sp", lambda e, sq=sq, sn=sn: e.dma_start(
                            out=sn, in_=ss0[sq].rearrange("(a b) p n -> (b p) a n", b=2)), writes=[KM("SNAT", sq % 2)], dma=True)
                        bt_ = nb()
                        for a in range(4):
                            P.op("pe", lambda e, a=a, bt_=bt_, sn=sn: e.transpose(out=PS[bt_][:, a * 128:(a + 1) * 128],
                                                                               in_=sn[:, a, :], identity=identF),
                                 reads=[KM("SNAT", sq % 2), ("cst",)], writes=[("ps", bt_)])
                        P.op("act", lambda e, bt_=bt_, s0t=s0t: e.activation(out=s0t, in_=PS[bt_][:, :], func=AF.Copy),
                             reads=[("ps", bt_)], writes=[KWM("S0T", sq % 2)])
                        P.op("dve", lambda e: e.memset(CTm[:, :, 0, 0:64], 0.0), writes=[KWM("CTm", 0)])
                        P.op("dve", lambda e, sq=sq: e.tensor_copy(out=CTm[:, :, 0, sq * 4:sq * 4 + 4], in_=CT[:, :, sq * 4:sq * 4 + 4]),
                             reads=[KWM("CT")], writes=[KWM("CTm", 0)])
                        for g in range(2):
                            P.op("pe", lambda e, g=g, sq=sq, s0t=s0t, byx=byx: e.matmul(
                                PS[byx][0:64, g * 256:(g + 1) * 256], lhsT=CTm[:, g, 0, 0:64], rhs=s0t[:, g * 256:(g + 1) * 256],
                                start=(sq == 0 and g == 0), stop=(sq == 15 and g == 1), skip_group_check=True),
                                reads=[KWM("CTm", 0), KWM("S0T", sq % 2)], writes=[("ps", byx)])
                        P.op("dve", lambda e, sq=sq: e.tensor_scalar(out=XENDm[0:64], in0=XEND[0:64],
                                                                     scalar1=maskS[0:64, 64 + sq:65 + sq], scalar2=None, op0=ALU.mult),
                             reads=[KWM("XEND", pp), ("maskS",)], writes=[KWM("XENDm")])
                        bn = nb()
                        for a in range(4):
                            P.op("pe", lambda e, a=a, bn=bn: e.matmul(
                                PS[bn][:, a * 128:(a + 1) * 128], lhsT=XENDm[0:64, a * 128:(a + 1) * 128],
                                rhs=BTOK[0:64, (a // 2) * 128:(a // 2) * 128 + 128], start=(a == 0), stop=(a == 3),
                                skip_group_check=True),
                                reads=[KWM("XENDm"), KWM("BTOK", pp)], writes=[("ps", bn)])
                        for a in range(4):
                            P.op("dve", lambda e, a=a, sq=sq, bn=bn, sn=sn, snew=snew: e.scalar_tensor_tensor(
                                out=snew[:, a, :], in0=sn[:, a, :], scalar=DECP[:, a, sq:sq + 1], op0=ALU.mult,
                                in1=PS[bn][:, a * 128:(a + 1) * 128], op1=ALU.add),
                                reads=[KM("SNAT", sq % 2), KM("DECP"), ("ps", bn)], writes=[KM("SNEW", sq % 2)])
                        P.op("sp", lambda e, sq=sq, snew=snew: e.dma_start(
                            out=sss[sq].rearrange("(a b) p n -> (b p) a n", b=2), in_=snew), reads=[KM("SNEW", sq % 2)], dma=True)
                P.op("dve", lambda e, byx=byx: e.tensor_tensor(
                    out=A2[0:r, :].rearrange("p (h q) -> p h q", q=64), in0=PS[byx][0:r, :].rearrange("p (h q) -> p h q", q=64),
                    in1=ECUM[0:r].unsqueeze(2).to_broadcast([r, 8, 64]), op=ALU.mult),
                    reads=[("ps", byx), KM("ECUM")], writes=[KM("A2")])
                P.op("dve", lambda e: e.tensor_tensor(out=A2[0:r, :], in0=A2[0:r, :], in1=A1[0:r, :], op=ALU.add),
                     reads=[KM("A2"), KM("A1", pp)], writes=[KM("A2")])
                P.op("dve", lambda e, x3=x3: e.tensor_tensor(out=A1[0:r, :].rearrange("p (h q) -> p h q", q=64), in0=x3,
                                                             in1=DROW[0:r].unsqueeze(2).to_broadcast([r, 8, 64]), op=ALU.mult),
                     reads=[KM("XTOK", pp), ("cols2",)], writes=[KM("A1", pp)])
                P.op("dve", lambda e: e.tensor_tensor(out=A2[0:r, :], in0=A2[0:r, :], in1=A1[0:r, :], op=ALU.add),
                     reads=[KM("A2"), KM("A1", pp)], writes=[KM("A2")])
                P.op("dve", lambda e, ti=ti: e.tensor_tensor(out=A2[0:r, :], in0=A2[0:r, :], in1=ZS[0:r, ti, :], op=ALU.mult),
                     reads=[KM("A2"), KM("ZS")], writes=[KM("A2")])
                for g in range(2):
                    P.op("act", lambda e, g=g: e.activation(out=A1[0:r, g * 256:(g + 1) * 256], in_=A2[0:r, g * 256:(g + 1) * 256],
                                                            func=AF.Square, accum_out=SSQ[0:r, g:g + 1]),
                         reads=[KM("A2")], writes=[KM("A1", pp), KM("SSQ")])
                P.op("dve", lambda e: e.tensor_scalar(out=RSQ[0:r, 0:2], in0=SSQ[0:r, 0:2], scalar1=1.0 / 256, scalar2=EPS,
                                                      op0=ALU.mult, op1=ALU.add), reads=[KM("SSQ")], writes=[KM("RSQ")])
                P.op("act", lambda e: e.activation(out=RSQ[0:r, 0:2], in_=RSQ[0:r, 0:2], func=AF.Ln),
                     reads=[KM("RSQ")], writes=[KM("RSQ")])
                P.op("act", lambda e: e.activation(out=RSQ[0:r, 0:2], in_=RSQ[0:r, 0:2], func=AF.Exp, scale=-0.5),
                     reads=[KM("RSQ")], writes=[KM("RSQ")])
                for g in range(2):
                    P.op("dve", lambda e, g=g: e.tensor_scalar(out=YN[0:r, g * 256:(g + 1) * 256], in0=A2[0:r, g * 256:(g + 1) * 256],
                                                               scalar1=RSQ[0:r, g:g + 1], scalar2=None, op0=ALU.mult),
                         reads=[KM("A2"), KM("RSQ")], writes=[KWM("YN")])
                b = nb()
                pvy = PS[b][:].bitcast(BF16).rearrange("p (c t) -> p c t", c=8)
                for c in range(4):
                    P.op("pe", lambda e, c=c, pvy=pvy: e.transpose(out=pvy[:, c, 0:r], in_=YN[0:r, c * 128:(c + 1) * 128],
                                                                   identity=ident[0:r, 0:r]),
                         reads=[KWM("YN"), ("ident",)], writes=[("ps", b)])
                for c in range(4):
                    P.op("act", lambda e, c=c, pvy=pvy, c0=c0: e.activation(out=OAT[:, 4 + c, c0:c0 + r], in_=pvy[:, c, 0:r],
                                                                           func=AF.Copy, scale=SNW[:, c:c + 1]),
                         reads=[("ps", b), ("cols2",)], writes=[KW("OAT", 4 + c)])
            _tile(0, 0)
            for ti in range(ntile):
                if ti + 1 < ntile:
                    _tile(ti + 1, 0)
                _tile(ti, 1)
            if si == 3:
                b = nb()
                for a in range(4):
                    P.op("pe", lambda e, a=a, b=b: e.transpose(out=PS[b][:, a * 128:(a + 1) * 128], in_=MS[:, a * 128:(a + 1) * 128],
                                                               identity=identF), reads=[KM("MS"), ("cst",)], writes=[("ps", b)])
                P.op("act", lambda e, b=b: e.activation(out=A1[:, :], in_=PS[b][:, :], func=AF.Copy),
                     reads=[("ps", b)], writes=[KM("A1")])
                P.op("sp", lambda e: e.dma_start(out=ssp.rearrange("(a b) p n -> (b p) a n", b=2),
                                                 in_=A1[:, :].rearrange("p (a n) -> p a n", a=4)), reads=[KM("A1")], dma=True)
            barrier()

        def hgrn_mixer(norm_gi, next_gi):
            pend = []

            def norm_upto(idx):
                while pend and pend[0] <= idx:
                    gi_ = pend.pop(0)
                    norm_group(norm_gi, *STILES[gi_])
            mamba_setup()
            barrier()
            P.op("dve", lambda e: e.memset(SST[:, :, :], 0.0), writes=[K("SST")])
            P.op("dve", lambda e: e.memset(SBF[:, :, :], 0.0), writes=[KW("SBF")])
            for si, (t0, n) in enumerate(STILES):
                norm_upto(si + 1)
                if si == 4:
                    barrier()
                for h in range(4):
                    hgrn_head(si, t0, n, h, si == 4)
                mamba_tile_group(si, t0, n)
                out_proj(ab_w_out[0], t0, n, 8)
                norm_group(next_gi, t0, n)
                if si == 3:
                    for h in range(4):
                        P.op("sp", lambda e, h=h: e.dma_start(out=hgp[h], in_=SST[:, h, 0:128]),
                             reads=[K("SST", 0, h)], dma=True)
            barrier()

        def gla_head(si, t0, n, h, sample):
            W = gla_w_in[0]
            for vc in range(2):
                b2 = proj_fm(W, 2048 + h * 256 + vc * 128, t0, n)
                evac(AF.Silu, (GTa, GTb)[vc][:, 0:n], K("GT", vc), b2, n)
            b = proj_fm(W, h * 128, t0, n)
            P.op("dve", lambda e, b=b: e.tensor_scalar(out=QV[:, 0:n], in0=PS[b][:, 0:n], scalar1=float(128 ** -0.5),
                                                       scalar2=None, op0=ALU.mult),
                 reads=[("ps", b)], writes=[K("QV")])
            b = proj_fm(W, 512 + h * 128, t0, n)
            evac(AF.Copy, KV[:, 0:n], K("KV"), b, n)
            b = nb()
            P.op("pe", lambda e, b=b: e.matmul(PS[b][:, 0:n], lhsT=WGK[:, h * 128:(h + 1) * 128],
                                               rhs=GKL[:, 0:n], start=True, stop=True),
                 reads=[KW("WGK"), KW("GKL")], writes=[("ps", b)])
            P.op("dve", lambda e, b=b: e.tensor_scalar(out=T1[:, 0:n], in0=PS[b][:, 0:n], scalar1=cols[:, 16 + h:17 + h],
                                                       scalar2=None, op0=ALU.subtract),
                 reads=[("ps", b), ("cols",)], writes=[K("T1")])
            P.op("act", lambda e: e.activation(out=T1[:, 0:n], in_=T1[:, 0:n], func=AF.Exp, scale=-1.0),
                 reads=[K("T1")], writes=[K("T1")])
            P.op("dve", lambda e: e.tensor_scalar(out=T1[:, 0:n], in0=T1[:, 0:n], scalar1=1.0, scalar2=None, op0=ALU.add),
                 reads=[K("T1")], writes=[K("T1")])
            P.op("act", lambda e: e.activation(out=T1[:, 0:n], in_=T1[:, 0:n], func=AF.Ln),
                 reads=[K("T1")], writes=[K("T1")])
            P.op("dve", lambda e: e.tensor_scalar(out=LF[:, 0:n], in0=T1[:, 0:n], scalar1=-1.0 / 16.0,
                                                  scalar2=None, op0=ALU.mult),
                 reads=[K("T1")], writes=[K("LF")])
            for vc in range(2):
                proj_tm(W, 1024 + h * 256 + vc * 128, t0, n,
                        lambda ti, r, vc=vc: VT[0:r, ti, vc * 128:(vc + 1) * 128], KW("VT"))
            gla_core(si, t0, n, h, 256, sample, gs0, glp, gls, 20, 2 * h)

        def gla_gk(t0, n):
            b = proj_fm(gla_w_in[0], 3072, t0, n, ncols=16)
            evac(AF.Copy, GKL[:, 0:n], KW("GKL"), b, n, rows_=16)

        def gla_mixer(norm_gi, next_gi):
            pend = []

            def norm_upto(idx):
                while pend and pend[0] <= idx:
                    gi_ = pend.pop(0)
                    norm_group(norm_gi, *STILES[gi_])
            barrier()
            P.op("dve", lambda e: e.memset(SST[:, :, :], 0.0), writes=[K("SST")])
            P.op("dve", lambda e: e.memset(SBF[:, :, :], 0.0), writes=[KW("SBF")])
            P.op("pool", lambda e: e.dma_start(out=WGK, in_=gla_w_gk[0]), writes=[KW("WGK")], dma=True)
            for si, (t0, n) in enumerate(STILES):
                norm_upto(si + 1)
                if si == 4:
                    barrier()
                gla_gk(t0, n)
                for h in range(4):
                    gla_head(si, t0, n, h, si == 4)
                out_proj(gla_w_out[0], t0, n, 8)
                norm_group(next_gi, t0, n)
                if si == 3:
                    for h in range(4):
                        P.op("sp", lambda e, h=h: e.dma_start(out=glp[h], in_=SST[:, h, 0:256]),
                             reads=[K("SST", 0, h)], dma=True)
            barrier()

        for layer in range(2):
            if layer == 0:
                norm_to_hT(0)
            ffn(ffn_w_in[0][layer], ffn_w_out[0][layer], tile_epilogue=norm_epilogue(3 * layer + 1))
            if layer == 0:
                hgrn_mixer(3 * layer + 1, 3 * layer + 2)
            else:
                gla_mixer(3 * layer + 1, 3 * layer + 2)
            ffn(ffn_w_in[1][layer], ffn_w_out[1][layer], tile_epilogue=(final_tile if layer == 1 else norm_epilogue(3)))

        with nc.allow_non_contiguous_dma(reason="small strided parameter/state transfers"):
            P.emit()
    return nc


_CACHE = {}


def kernel(**inputs):
    f32 = lambda a: np.ascontiguousarray(np.asarray(a, dtype=np.float32))
    x_prompt = f32(inputs["x_prompt"])
    x_sample = f32(inputs["x_sample"]).reshape(128 * 4, D)
    shared = {k: f32(inputs[k]) for k in ("norm_ffn1", "norm_mix", "norm_ffn2", "norm_final",
                                          "ffn1_w_in", "ffn1_w_out", "ffn2_w_in", "ffn2_w_out")}
    for k in ("ab_w_in", "ab_w_out", "hgrn_lb_logits", "hgrn_norm", "gla_w_in", "gla_w_gk", "gla_b_gk",
              "gla_norm", "gla_w_out", "ssm_conv_w", "ssm_conv_b", "ssm_dt_bias", "ssm_a_log", "ssm_d", "ssm_norm"):
        shared[k] = f32(inputs[k])
    shared["ident"] = np.eye(128, dtype=np.float32).astype(ml_dtypes.bfloat16)
    shared["ones_bf"] = np.ones((128, 128), dtype=np.float32).astype(ml_dtypes.bfloat16)
    jj, ii = np.meshgrid(np.arange(128), np.arange(128), indexing="ij")
    shared["maskP"] = ((jj // 64 == ii // 64) & (jj <= ii)).astype(np.float32)
    mS = np.zeros((128, 128), np.float32)
    mS[:64, :64] = ((jj // 4 == ii // 4) & (jj <= ii))[:64, :64]
    mS[:64, 64:80] = (np.arange(64)[:, None] // 4 == np.arange(16)[None, :])
    shared["maskS"] = mS
    shared["resetP"] = np.broadcast_to((np.arange(512) % 64 != 0).astype(np.float32), (128, 512)).copy()
    shared["resetS"] = np.broadcast_to((np.arange(512) % 4 != 0).astype(np.float32), (128, 512)).copy()
    cst = np.zeros((128, 384), np.float32)
    cst[:, 0:128] = np.eye(128)
    cst[:, 128:256] = 1.0
    cst[:64, 256:320] = (np.arange(64)[:, None] // 4 == np.arange(64)[None, :] // 4)
    cst[:64, 320:336] = (np.arange(64)[:, None] == 4 * np.arange(16)[None, :] + 3)
    cst[:64, 336] = 1.0
    cst[64:, 337] = 1.0
    shared["cst"] = cst
    st_s = f32(inputs["state_ssm"])[0]
    st_c = f32(inputs["state_conv"])[0]
    st_h = f32(inputs["state_hgrn"])[0]
    st_g = f32(inputs["state_gla"])[0]
    if "nc" not in _CACHE:
        _CACHE["nc"] = build_program()
    nc = _CACHE["nc"]
    in_maps = []
    for c in range(N_CORES):
        m = dict(shared)
        m["xp"] = x_prompt[c]
        m["xs"] = x_sample[c * 64:(c + 1) * 64]
        m["hs0"] = st_h[c * 16:(c + 1) * 16]
        m["gs0"] = st_g[c * 16:(c + 1) * 16]
        m["ss0"] = st_s[c * 16:(c + 1) * 16]
        m["cs0"] = st_c[c * 16:(c + 1) * 16]
        in_maps.append(m)
    res = run_bass_kernel_spmd(nc, in_maps, core_ids=list(range(N_CORES)))
    outs = res.results
    y_prompt = np.stack([outs[c]["yp"] for c in range(N_CORES)], axis=0)
    y_sample = np.concatenate([outs[c]["ys"] for c in range(N_CORES)], axis=0).reshape(128, 4, D)
    cat = lambda k: np.concatenate([outs[c][k] for c in range(N_CORES)], axis=0)
    stk = lambda k: np.stack([outs[c][k] for c in range(N_CORES)], axis=0)
    hgrn_p = stk("hgp")[None]
    hgrn_s = cat("hgs")[None]
    gla_p = stk("glp")[None]
    gla_s = cat("gls")[None]
    z = lambda *sh: np.zeros(sh, np.float32)
    return (y_prompt, y_sample, hgrn_p, hgrn_s, stk("ssp")[None], cat("sss")[None],
            stk("cvp")[None], cat("cvs")[None], gla_p, gla_s)
```

```python
import contextlib
import numpy as np
import ml_dtypes
import concourse.bass as bass
import concourse.mybir as mybir
from concourse.bass_utils import run_bass_kernel_spmd

F32 = mybir.dt.float32
BF16 = mybir.dt.bfloat16
AF = mybir.ActivationFunctionType
ALU = mybir.AluOpType

N_CORES = 8
D = 1024
DFF = 2816
NFC = 22
SEQ = 2048
NT = 17
TTOK = 2112
EPS = 1e-6
STILES = [(0, 512), (512, 512), (1024, 512), (1536, 512), (2048, 64)]
FSTILES = [(0, 448), (448, 448), (896, 448), (1344, 448), (1792, 320)]

ENGS = ("pe", "act", "dve", "pool", "sp")
NLANES = {"sp": 12, "pool": 12, "act": 6}

DEBUG = {"mixers": True, "strict": True, "hgrn": True, "gla": True}


class _Op:
    __slots__ = ("eng", "fn", "waits", "signal", "ticket", "dma", "lane", "lane_ticket", "idx")


class Prog:
    def __init__(self, nc, strict_same_engine=True):
        self.nc = nc
        self.ops = {e: [] for e in ENGS}
        self.last_w = {}
        self.readers = {}
        self.children = {}
        self.dma_count = {"sp": 0, "pool": 0, "act": 0}
        self.strict = strict_same_engine

    def _conflicts(self, key):
        out = [key[:i] for i in range(1, len(key) + 1)]
        out.extend(self.children.get(key, ()))
        return out

    def _register(self, key):
        for i in range(1, len(key)):
            self.children.setdefault(key[:i], set()).add(key)

    def op(self, eng, fn, reads=(), writes=(), dma=False):
        o = _Op()
        o.eng, o.fn, o.dma, o.signal, o.ticket = eng, fn, dma, False, None
        o.idx = len(self.ops[eng])
        deps = []
        for k in reads:
            for c in self._conflicts(k):
                t = self.last_w.get(c)
                if t is not None:
                    deps.append((t, "raw"))
        for k in writes:
            for c in self._conflicts(k):
                t = self.last_w.get(c)
                if t is not None:
                    deps.append((t, "waw"))
                for t in self.readers.get(c, {}).values():
                    deps.append((t, "war"))
        if dma:
            n = self.dma_count[eng]
            self.dma_count[eng] = n + 1
            nl = NLANES[eng]
            o.lane = n % nl
            o.lane_ticket = 16 * (n // nl + 1)
            tok = ("d", eng, o.lane, o.lane_ticket, o.idx)
            if n >= nl:
                deps.append((("d", eng, o.lane, o.lane_ticket - 16, -1), "lane"))
        else:
            tok = ("c", eng, o.idx)
        o.waits = []
        for t, kind in deps:
            if t[0] == "c" and t[1] == eng and not dma:
                if eng == "pe" or not self.strict:
                    continue
            o.waits.append(t)
        for k in writes:
            self._register(k)
            self.last_w[k] = tok
            self.readers[k] = {}
            for ch in list(self.children.get(k, ())):
                self.last_w.pop(ch, None)
                self.readers.pop(ch, None)
        for k in reads:
            self._register(k)
            src = (tok[0], tok[1], tok[2] if tok[0] == "d" else 0)
            self.readers.setdefault(k, {})[src] = tok
        self.ops[eng].append(o)
        return o

    def emit(self, final_wait_eng="sp"):
        nc = self.nc
        for e in ENGS:
            for o in self.ops[e]:
                for t in o.waits:
                    if t[0] == "c":
                        self.ops[t[1]][t[2]].signal = True
        for e in ENGS:
            n = 0
            for o in self.ops[e]:
                if o.signal and not o.dma:
                    n += 1
                    o.ticket = n
        with contextlib.ExitStack() as st:
            csem = {e: st.enter_context(nc.semaphore("c_" + e)) for e in ENGS}
            lsem = {}
            for q, nl in NLANES.items():
                for l in range(nl):
                    lsem[(q, l)] = st.enter_context(nc.semaphore("l_%s_%d" % (q, l)))
            block = st.enter_context(nc.Block())
            engobj = {"pe": "tensor", "act": "scalar", "dve": "vector", "pool": "gpsimd", "sp": "sync"}

            def emit_engine(e, eng):
                seen = {}
                for o in self.ops[e]:
                    need = {}
                    for t in o.waits:
                        if t[0] == "c":
                            key = ("c", t[1])
                            val = self.ops[t[1]][t[2]].ticket
                        else:
                            key = ("d", t[1], t[2])
                            val = t[3]
                        if seen.get(key, 0) >= val:
                            continue
                        if need.get(key, 0) < val:
                            need[key] = val
                    for key, val in need.items():
                        seen[key] = val
                        sem = csem[key[1]] if key[0] == "c" else lsem[(key[1], key[2])]
                        eng.wait_ge(sem, val)
                    inst = o.fn(eng)
                    if o.dma:
                        inst.then_inc(lsem[(e, o.lane)], 16)
                    elif o.signal:
                        inst.then_inc(csem[e], 1)
                if e == final_wait_eng:
                    for q, nl in NLANES.items():
                        n = self.dma_count[q]
                        for l in range(nl):
                            cnt = (n - l + nl - 1) // nl if n > l else 0
                            if cnt > 0:
                                eng.wait_ge(lsem[(q, l)], 16 * cnt)

            for e in ENGS:
                if not self.ops[e] and e != final_wait_eng:
                    continue

                def body(eng, e=e):
                    emit_engine(e, eng)
                getattr(block, engobj[e])(body)


def build_program():
    nc = bass.Bass("TRN2", target_bir_lowering=False)

    def din(name, shape, dt=F32):
        return nc.dram_tensor(name, list(shape), dt, kind="ExternalInput").ap()

    def dout(name, shape):
        return nc.dram_tensor(name, list(shape), F32, kind="ExternalOutput").ap()

    xp = din("xp", [SEQ, D])
    xs = din("xs", [64, D])
    norm_ffn1 = din("norm_ffn1", [2, D])
    norm_mix = din("norm_mix", [2, D])
    norm_ffn2 = din("norm_ffn2", [2, D])
    norm_final = din("norm_final", [D])
    ffn_w_in = [din("ffn1_w_in", [2, D, 2 * DFF]), din("ffn2_w_in", [2, D, 2 * DFF])]
    ffn_w_out = [din("ffn1_w_out", [2, DFF, D]), din("ffn2_w_out", [2, DFF, D])]
    ident_d = din("ident", [128, 128], BF16)
    ones_d = din("ones_bf", [128, 128], BF16)
    maskP_d = din("maskP", [128, 128])
    maskS_d = din("maskS", [128, 128])
    resetP_d = din("resetP", [128, 512])
    resetS_d = din("resetS", [128, 512])
    ab_w_in = din("ab_w_in", [1, D, 3592])
    ab_w_out = din("ab_w_out", [1, D, D])
    lb_logits = din("hgrn_lb_logits", [2, 512])
    hgrn_norm = din("hgrn_norm", [1, 128])
    gla_w_in = din("gla_w_in", [1, D, 3088])
    gla_w_gk = din("gla_w_gk", [1, 16, 512])
    gla_b_gk = din("gla_b_gk", [1, 512])
    gla_norm = din("gla_norm", [1, 256])
    gla_w_out = din("gla_w_out", [1, D, D])
    hs0 = din("hs0", [16, 4, 128, 128])
    gs0 = din("gs0", [16, 4, 128, 256])
    hgp = dout("hgp", [4, 128, 128])
    hgs = dout("hgs", [16, 4, 128, 128])
    glp = dout("glp", [4, 128, 256])
    gls = dout("gls", [16, 4, 128, 256])
    cst_d = din("cst", [128, 384])
    conv_w = din("ssm_conv_w", [1, 4, 1024])
    conv_b = din("ssm_conv_b", [1, 1024])
    dt_bias = din("ssm_dt_bias", [1, 8])
    a_log = din("ssm_a_log", [1, 8])
    ssm_d = din("ssm_d", [1, 8])
    ssm_norm = din("ssm_norm", [1, 512])
    ss0 = din("ss0", [16, 8, 64, 128])
    cs0 = din("cs0", [16, 3, 1024])
    ssp = dout("ssp", [8, 64, 128])
    sss = dout("sss", [16, 8, 64, 128])
    cvp = dout("cvp", [3, 1024])
    cvs = dout("cvs", [16, 3, 1024])

    yp = dout("yp", [SEQ, D])
    ys = dout("ys", [64, D])

    with contextlib.ExitStack() as st:
        def sb(name, shape, dt=F32):
            return st.enter_context(nc.sbuf_tensor(name, list(shape), dt))

        X = sb("X", [128, NT, D])
        hT = sb("hT", [128, 8, TTOK], BF16)
        aT = sb("aT", [128, 11, TTOK], BF16)
        Wo = sb("Wo", [128, 11, D], BF16)
        Wr = [sb("Wr%d" % i, [128, 8, 128], BF16) for i in range(8)]
        sg = [sb("sg%d" % i, [128, 512]) for i in range(2)]
        hn = [sb("hn%d" % i, [128, D], BF16) for i in range(2)]
        junk = sb("junk", [128, D], BF16)
        ident = sb("ident_sb", [128, 128], BF16)
        gcol = sb("gcol", [128, 6, 8])
        gfin = sb("gfin", [128, D])
        ss = sb("ss", [128, 32])
        rs = sb("rs", [128, 32])
        ones_bf = sb("ones_sb", [128, 128], BF16)
        maskP = sb("maskP_sb", [128, 128])
        maskS = sb("maskS_sb", [128, 128])
        resetP = sb("resetP_sb", [128, 512])
        resetS = sb("resetS_sb", [128, 512])
        cols = sb("cols", [128, 64])
        dummy = sb("dummy_sb", [128, 8])
        CST = sb("cst_sb", [128, 384])
        cols2 = sb("cols2", [128, 64])
        TAIL = sb("tail_sb", [128, 8, 4])
        PS = [st.enter_context(nc.psum_tensor("ps%d" % i, [128, 512], F32)) for i in range(8)]

        P = Prog(nc, strict_same_engine=DEBUG["strict"])

        P.op("act", lambda e: e.dma_start(out=ident[:], in_=ident_d), writes=[("ident",)], dma=True)
        gains = [norm_ffn1[0], norm_mix[0], norm_ffn2[0], norm_ffn1[1], norm_mix[1], norm_ffn2[1]]
        for i, g in enumerate(gains):
            P.op("act", lambda e, i=i, g=g: e.dma_start(out=gcol[:, i, :], in_=g.rearrange("(c p) -> p c", p=128)),
                 writes=[("gcol", i)], dma=True)
        P.op("act", lambda e: e.dma_start(out=gfin[:], in_=norm_final.partition_broadcast(128)),
             writes=[("gfin",)], dma=True)
        for tt in range(16):
            P.op("sp", lambda e, tt=tt: e.dma_start(out=X[:, tt, :], in_=xp[tt * 128:(tt + 1) * 128, :]),
                 writes=[("X", tt)], dma=True)
        P.op("sp", lambda e: e.dma_start(out=X[0:64, 16, :], in_=xs), writes=[("X", 16)], dma=True)

        def rows(tt):
            return 64 if tt == 16 else 128

        nrm_ctr = [0]

        def norm_to_hT(gi):
            for (t0, n) in STILES:
                norm_group(gi, t0, n)

        def norm_group(gi, t0, n, only_ti=None, evac=True):
            if True:
                ntile = max(1, n // 128)
                r = min(128, n)
                pvs = [PS[bk][:].bitcast(BF16) for bk in range(4)]
                for ti in range(ntile):
                    if only_ti is not None and ti != only_ti:
                        continue
                    tt = t0 // 128 + ti
                    k = nrm_ctr[0] % 32
                    nrm_ctr[0] += 1
                    P.op("act", lambda e, tt=tt, k=k, r=r: e.activation(
                        out=junk[0:r, :], in_=X[0:r, tt, :], func=AF.Square, accum_out=ss[0:r, k:k + 1]),
                        reads=[("X", tt)], writes=[("junk",), ("ss", k)])
                    P.op("act", lambda e, k=k, r=r: e.activation(
                        out=rs[0:r, k:k + 1], in_=ss[0:r, k:k + 1], func=AF.Sqrt, scale=1.0 / D, bias=EPS),
                        reads=[("ss", k)], writes=[("rs", k)])
                    P.op("dve", lambda e, k=k, r=r: e.reciprocal(out=rs[0:r, k:k + 1], in_=rs[0:r, k:k + 1]),
                         reads=[("rs", k)], writes=[("rs", k)])
                    hb = tt % 2
                    P.op("dve", lambda e, tt=tt, k=k, hb=hb, r=r: e.tensor_scalar(
                        out=hn[hb][0:r, :], in0=X[0:r, tt, :], scalar1=rs[0:r, k:k + 1], scalar2=None, op0=ALU.mult),
                        reads=[("X", tt), ("rs", k)], writes=[("hn", hb)])
                    for c in range(8):
                        o0 = (c % 2) * 512 + ti * 128
                        P.op("pe", lambda e, c=c, hb=hb, o0=o0, r=r: e.transpose(
                            out=pvs[c // 2][:, o0:o0 + r], in_=hn[hb][0:r, c * 128:(c + 1) * 128], identity=ident[0:r, 0:r]),
                            reads=[("hn", hb), ("ident",)], writes=[("ps", c // 2)])
                if not evac:
                    return
                tkeys = [("hT", t0 // 128 + ti) for ti in range(ntile)]
                for c in range(8):
                    src = pvs[c // 2][:, (c % 2) * 512:(c % 2) * 512 + n]
                    if c % 2 == 0:
                        P.op("act", lambda e, c=c, src=src, t0=t0, n=n: e.activation(
                            out=hT[:, c, t0:t0 + n], in_=src, func=AF.Copy, scale=gcol[:, gi, c:c + 1]),
                            reads=[("ps", c // 2), ("gcol", gi)], writes=tkeys)
                    else:
                        P.op("dve", lambda e, c=c, src=src, t0=t0, n=n: e.tensor_scalar(
                            out=hT[:, c, t0:t0 + n], in0=src, scalar1=gcol[:, gi, c:c + 1], scalar2=None, op0=ALU.mult),
                            reads=[("ps", c // 2), ("gcol", gi)], writes=tkeys)

        def final_tile(tt):
            r = rows(tt)
            k = nrm_ctr[0] % 32
            nrm_ctr[0] += 1
            P.op("act", lambda e, tt=tt, r=r, k=k: e.activation(
                out=junk[0:r, :], in_=X[0:r, tt, :], func=AF.Square, accum_out=ss[0:r, k:k + 1]),
                reads=[("X", tt)], writes=[("junk",), ("ss", k)])
            P.op("act", lambda e, r=r, k=k: e.activation(
                out=rs[0:r, k:k + 1], in_=ss[0:r, k:k + 1], func=AF.Sqrt, scale=1.0 / D, bias=EPS),
                reads=[("ss", k)], writes=[("rs", k)])
            P.op("dve", lambda e, r=r, k=k: e.reciprocal(out=rs[0:r, k:k + 1], in_=rs[0:r, k:k + 1]),
                 reads=[("rs", k)], writes=[("rs", k)])
            P.op("dve", lambda e, tt=tt, r=r, k=k: e.scalar_tensor_tensor(
                out=X[0:r, tt, :], in0=X[0:r, tt, :], scalar=rs[0:r, k:k + 1], op0=ALU.mult,
                in1=gfin[0:r, :], op1=ALU.mult),
                reads=[("X", tt), ("rs", k), ("gfin",)], writes=[("X", tt)])
            if tt < 16:
                P.op("sp", lambda e, tt=tt: e.dma_start(out=yp[tt * 128:(tt + 1) * 128, :], in_=X[:, tt, :]),
                     reads=[("X", tt)], dma=True)
            else:
                P.op("sp", lambda e: e.dma_start(out=ys, in_=X[0:64, 16, :]), reads=[("X", 16)], dma=True)


        def norm_epilogue(gi):
            def one(t_):
                si_ = min(t_ // 4, 4)
                t0_, n_ = STILES[si_]
                ti_ = t_ - t0_ // 128
                norm_group(gi, t0_, n_, only_ti=ti_, evac=False)
                if ti_ == max(1, n_ // 128) - 1:
                    norm_group(gi, t0_, n_, only_ti=-1, evac=True)

            def ep(tt):
                if tt >= 1:
                    one(tt - 1)
                if tt == NT - 1:
                    one(tt)
            return ep

        wslot = [0]
        psrot = [0]

        def wload(wmat, col0, ncols=128):
            s = wslot[0] % 8
            wslot[0] += 1
            P.op("pool", lambda e, s=s: e.dma_start(
                out=Wr[s][:, :, 0:ncols], in_=wmat[:, col0:col0 + ncols].rearrange("(c p) n -> p c n", p=128)),
                writes=[("Wr", s)], dma=True)
            return s

        def ffn(w_in, w_out, norm_gi=None, tile_epilogue=None):
            pend = list(range(len(STILES))) if norm_gi is not None else []

            def norm_upto(idx):
                while pend and pend[0] <= idx:
                    gi_ = pend.pop(0)
                    norm_group(norm_gi, *STILES[gi_])
            for g in range(2):
                j0 = g * 11
                for jl in range(11):
                    j = j0 + jl
                    if jl == 4:
                        P.op("pool", lambda e, j0=j0: e.dma_start(
                            out=Wo[:], in_=w_out[j0 * 128:(j0 + 11) * 128, :].rearrange("(j p) d -> p j d", p=128)),
                            writes=[("Wo",)], dma=True)
                    sG = wload(w_in, j * 128)
                    sU = wload(w_in, DFF + j * 128)
                    for (t0, n) in FSTILES:
                        norm_upto(min(4, (t0 + n - 1) // 512) + 1)
                        bg = 4 + (psrot[0] % 2) * 2
                        bu = bg + 1
                        sgi = psrot[0] % 2
                        psrot[0] += 1
                        tiles = [("hT", tt) for tt in range(t0 // 128, (t0 + n - 1) // 128 + 1)]
                        for c in range(8):
                            P.op("pe", lambda e, c=c, s=sG, t0=t0, n=n, bg=bg: e.matmul(
                                PS[bg][:, 0:n], lhsT=Wr[s][:, c, :], rhs=hT[:, c, t0:t0 + n],
                                start=(c == 0), stop=(c == 7)),
                                reads=[("Wr", sG)] + tiles, writes=[("ps", bg)])
                        for c in range(8):
                            P.op("pe", lambda e, c=c, s=sU, t0=t0, n=n, bu=bu: e.matmul(
                                PS[bu][:, 0:n], lhsT=Wr[s][:, c, :], rhs=hT[:, c, t0:t0 + n],
                                start=(c == 0), stop=(c == 7)),
                                reads=[("Wr", sU)] + tiles, writes=[("ps", bu)])
                        P.op("act", lambda e, n=n, bg=bg, sgi=sgi: e.activation(
                            out=sg[sgi][:, 0:n], in_=PS[bg][:, 0:n], func=AF.Silu),
                            reads=[("ps", bg)], writes=[("sg", sgi)])
                        P.op("dve", lambda e, n=n, bu=bu, sgi=sgi, jl=jl, t0=t0: e.tensor_tensor(
                            out=aT[:, jl, t0:t0 + n], in0=sg[sgi][:, 0:n], in1=PS[bu][:, 0:n], op=ALU.mult),
                            reads=[("sg", sgi), ("ps", bu)], writes=[("aT", jl, t0)])
                for tt in range(NT):
                    r = rows(tt)
                    akeys = [(f0) for (f0, fn) in FSTILES if f0 < tt * 128 + r and f0 + fn > tt * 128]
                    for dh in range(2):
                        bo = dh
                        bo = (tt % 2) * 2 + dh + (4 if g == 1 else 0)
                        for jl in range(11):
                            P.op("pe", lambda e, jl=jl, tt=tt, r=r, dh=dh, bo=bo: e.matmul(
                                PS[bo][0:r, :], lhsT=aT[:, jl, tt * 128:tt * 128 + r],
                                rhs=Wo[:, jl, dh * 512:(dh + 1) * 512], start=(jl == 0), stop=(jl == 10)),
                                reads=[("aT", jl, f0) for f0 in akeys] + [("Wo",)], writes=[("ps", bo)])
                        P.op("dve", lambda e, tt=tt, r=r, dh=dh, bo=bo: e.scalar_tensor_tensor(
                            out=X[0:r, tt, dh * 512:(dh + 1) * 512], in0=PS[bo][0:r, :], scalar=0.5, op0=ALU.mult,
                            in1=X[0:r, tt, dh * 512:(dh + 1) * 512], op1=ALU.add),
                            reads=[("ps", bo), ("X", tt)], writes=[("X", tt)])
                    if g == 1 and tile_epilogue is not None:
                        tile_epilogue(tt)

        for nm, t_sb, t_d in (("ones", ones_bf, ones_d), ("maskP", maskP, maskP_d), ("maskS", maskS, maskS_d),
                              ("resetP", resetP, resetP_d), ("resetS", resetS, resetS_d)):
            P.op("sp", lambda e, t_sb=t_sb, t_d=t_d: e.dma_start(out=t_sb[:], in_=t_d), writes=[(nm,)], dma=True)
        P.op("sp", lambda e: e.dma_start(out=cols[:, 0:4], in_=lb_logits[0].rearrange("(h p) -> p h", p=128)),
             writes=[("cols", "lb")], dma=True)
        P.op("sp", lambda e: e.dma_start(out=cols[:, 4:8], in_=lb_logits[1].rearrange("(h p) -> p h", p=128)),
             writes=[("cols", "l1")], dma=True)
        P.op("sp", lambda e: e.dma_start(out=cols[:, 12:13], in_=hgrn_norm[0].rearrange("(h p) -> p h", p=128)),
             writes=[("cols", "hn")], dma=True)
        P.op("sp", lambda e: e.dma_start(out=cols[:, 16:20], in_=gla_b_gk[0].rearrange("(h p) -> p h", p=128)),
             writes=[("cols", "bgk")], dma=True)
        P.op("sp", lambda e: e.dma_start(out=cols[:, 20:22], in_=gla_norm[0].rearrange("(h p) -> p h", p=128)),
             writes=[("cols", "gn")], dma=True)
        P.op("dve", lambda e: e.tensor_tensor(out=cols[:, 0:4], in0=cols[:, 0:4], in1=cols[:, 4:8], op=ALU.subtract),
             reads=[("cols", "lb"), ("cols", "l1")], writes=[("cols", "lb")])
        P.op("act", lambda e: e.activation(out=cols[:, 0:4], in_=cols[:, 0:4], func=AF.Sigmoid),
             reads=[("cols", "lb")], writes=[("cols", "lb")])
        P.op("dve", lambda e: e.tensor_scalar(out=cols[:, 8:12], in0=cols[:, 0:4], scalar1=-1.0, scalar2=1.0,
                                              op0=ALU.mult, op1=ALU.add),
             reads=[("cols", "lb")], writes=[("cols", "oml")])
        P.op("dve", lambda e: e.tensor_scalar(out=cols[:, 16:20], in0=cols[:, 16:20], scalar1=-1.0, scalar2=None,
                                              op0=ALU.mult),
             reads=[("cols", "bgk")], writes=[("cols", "bgk")])

        aTf = aT[:].rearrange("p a b -> p (a b)").bitcast(F32)
        Wob = Wo[:].rearrange("p a b -> p (a b)")

        def fA(i, w=512):
            return aTf[:, i * 512:i * 512 + w]
        QV, KV, LF, GG, T1, T2, O32a, O32b, GTa, GTb, RSTD = [fA(i) for i in range(11)]
        SST = aTf[:, 5632:6656].rearrange("p (h v) -> p h v", h=4)
        S0F = aTf[:, 6656:10752].rearrange("p (s v) -> p s v", s=16)
        QD = Wob[:, 0:512]
        KI = Wob[:, 512:1024]
        KE = Wob[:, 1024:1536]
        STm = [Wob[:, 1536:1664], Wob[:, 1664:1792], Wob[:, 11008:11136], Wob[:, 11136:11264]]
        SBFp = [Wob[:, 7424:7680], Wob[:, 7680:7936]]
        SST2 = [SST, aTf[:, 6656:7680].rearrange("p (h v) -> p h v", h=4)]
        VT = Wob[:, 1792:2816].rearrange("p (t v) -> p t v", t=4)
        KEt = Wob[:, 2816:3328].rearrange("p (t k) -> p t k", t=4)
        OAT = Wob[:, 3328:7424].rearrange("p (c n) -> p c n", c=8)
        SBF = Wob[:, 7424:8448].rearrange("p (h v) -> p h v", h=4)
        S0C = [Wob[:, 8448:8704], Wob[:, 8704:8960]]
        GKL = Wob[0:16, 8960:9472]
        WGK = Wob[0:16, 9472:9984]
        VBLK = Wob[0:64, 9984:11008].rearrange("p (s v) -> p s v", s=4)
        bankrr = [0]

        def nb():
            b = bankrr[0] % 6
            bankrr[0] += 1
            return b

        def K(*a):
            return ("aT", "mx") + a

        def KW(*a):
            return ("Wo", "mx") + a

        def barrier():
            P.op("dve", lambda e: e.memset(dummy[:, 0:1], 0.0), writes=[("aT",), ("Wo",), ("dummy",)])

        def proj_fm(wmat, col0, t0, n, ncols=128):
            sW = wload(wmat, col0, ncols)
            b = nb()
            tiles = [("hT", tt) for tt in range(t0 // 128, t0 // 128 + max(1, n // 128))]
            for c in range(8):
                P.op("pe", lambda e, c=c, b=b, sW=sW: e.matmul(
                    PS[b][0:ncols, 0:n], lhsT=Wr[sW][:, c, 0:ncols], rhs=hT[:, c, t0:t0 + n],
                    start=(c == 0), stop=(c == 7)), reads=[("Wr", sW)] + tiles, writes=[("ps", b)])
            return b

        def proj_tm(wmat, col0, t0, n, dst, dkey, func=AF.Copy):
            sW = wload(wmat, col0, 128)
            for ti in range(max(1, n // 128)):
                r = min(128, n)
                tt = t0 // 128 + ti
                b = nb()
                for c in range(8):
                    P.op("pe", lambda e, c=c, b=b, sW=sW, tt=tt, r=r: e.matmul(
                        PS[b][0:r, 0:128], lhsT=hT[:, c, tt * 128:tt * 128 + r], rhs=Wr[sW][:, c, :],
                        start=(c == 0), stop=(c == 7)), reads=[("Wr", sW), ("hT", tt)], writes=[("ps", b)])
                P.op("act", lambda e, b=b, ti=ti, r=r: e.activation(out=dst(ti, r), in_=PS[b][0:r, 0:128], func=func),
                     reads=[("ps", b)], writes=[dkey])

        def gla_core(si, t0, n, h, V, sample, s0_d, sp_d, ss_d, normcol0, oc0):
            nvc = V // 128
            reset = resetS if sample else resetP
            C = 4 if sample else 64
            nch = n // C
            P.op("dve", lambda e: e.tensor_tensor_scan(out=GG[:, 0:n], data0=reset[:, 0:n], data1=LF[:, 0:n],
                                                       initial=0.0, op0=ALU.mult, op1=ALU.add),
                 reads=[K("LF"), ("resetS" if sample else "resetP",)], writes=[K("GG")])
            P.op("act", lambda e: e.activation(out=T1[:, 0:n], in_=GG[:, 0:n], func=AF.Exp),
                 reads=[K("GG")], writes=[K("T1")])
            P.op("dve", lambda e: e.tensor_tensor(out=QD[:, 0:n], in0=QV[:, 0:n], in1=T1[:, 0:n], op=ALU.mult),
                 reads=[K("QV"), K("T1")], writes=[KW("QD")])
            P.op("act", lambda e: e.activation(out=T2[:, 0:n], in_=GG[:, 0:n], func=AF.Exp, scale=-1.0),
                 reads=[K("GG")], writes=[K("T2")])
            P.op("dve", lambda e: e.tensor_tensor(out=KI[:, 0:n], in0=KV[:, 0:n], in1=T2[:, 0:n], op=ALU.mult),
                 reads=[K("KV"), K("T2")], writes=[KW("KI")])
            g3 = GG[:, 0:n].rearrange("p (c j) -> p c j", j=C)
            P.op("dve", lambda e: e.tensor_tensor(out=T2[:, 0:n].rearrange("p (c j) -> p c j", j=C),
                                                  in0=g3[:, :, C - 1:C].to_broadcast([128, nch, C]), in1=g3,
                                                  op=ALU.subtract),
                 reads=[K("GG")], writes=[K("T2")])
            P.op("act", lambda e: e.activation(out=T2[:, 0:n], in_=T2[:, 0:n], func=AF.Exp),
                 reads=[K("T2")], writes=[K("T2")])
            P.op("dve", lambda e: e.tensor_tensor(out=KE[:, 0:n], in0=KV[:, 0:n], in1=T2[:, 0:n], op=ALU.mult),
                 reads=[K("KV"), K("T2")], writes=[KW("KE")])
            ntile = max(1, n // 128)
            r = min(128, n)
            for ti in range(ntile):
                b = nb()
                pv = PS[b][:].bitcast(BF16)
                P.op("pe", lambda e, ti=ti, pv=pv: e.transpose(out=pv[0:r, 0:128], in_=KE[:, ti * 128:ti * 128 + r],
                                                               identity=ident[:, :]),
                     reads=[KW("KE"), ("ident",)], writes=[("ps", b)])
                P.op("act", lambda e, ti=ti, pv=pv: e.activation(out=KEt[0:r, ti, :], in_=pv[0:r, 0:128], func=AF.Copy),
                     reads=[("ps", b)], writes=[KW("KEt", ti)])
            mask = maskS if sample else maskP
            if sample:
                P.op("sp", lambda e: e.dma_start(out=S0F[:, :, 0:V], in_=s0_d[:, h].rearrange("s k v -> k s v")),
                     writes=[K("S0F")], dma=True)
            if False:
                for ti in range(ntile):
                    c0 = ti * 128
                    b = nb()
                    P.op("pe", lambda e, b=b, c0=c0: e.matmul(PS[b][:, 0:128], lhsT=KI[:, c0:c0 + 128], rhs=QD[:, c0:c0 + 128],
                                                              start=True, stop=True),
                         reads=[KW("KI"), KW("QD")], writes=[("ps", b)])
                    P.op("dve", lambda e, b=b, ti=ti: e.tensor_tensor(out=STm[ti][:, 0:128], in0=PS[b][:, 0:128],
                                                                     in1=maskP[:, :], op=ALU.mult),
                         reads=[("ps", b), ("maskP",)], writes=[KW("ST", ti)])
                per = 512 // V
                ub = {}
                for c in range(2 * ntile):
                    ti, cc = divmod(c, 2)
                    if c % per == 0:
                        bU = nb()
                    col = (c % per) * V
                    P.op("pe", lambda e, bU=bU, col=col, cc=cc, ti=ti, c=c: e.matmul(
                        PS[bU][:, col:col + V], lhsT=KEt[cc * 64:cc * 64 + 64, ti, :], rhs=VT[cc * 64:cc * 64 + 64, ti, 0:V],
                        start=(c % per == 0), stop=True, skip_group_check=True),
                        reads=[KW("KEt", ti), KW("VT", ti)], writes=[("ps", bU)])
                    ub[c] = (bU, col)
                P.op("act", lambda e: e.activation(out=SBFp[0][:, 0:V], in_=SST2[0][:, h, 0:V], func=AF.Copy),
                     reads=[K("SST", 0, h)], writes=[KW("SBFp", 0)])
                for c in range(2 * ntile):
                    ti, cc = divmod(c, 2)
                    q0 = c * 64
                    if cc == 0:
                        bo = nb()
                        for vc in range(nvc):
                            P.op("pe", lambda e, bo=bo, ti=ti, vc=vc: e.matmul(
                                PS[bo][:, vc * 128:vc * 128 + 128], lhsT=VT[:, ti, vc * 128:(vc + 1) * 128], rhs=STm[ti][:, 0:128],
                                start=(vc == 0), stop=False, skip_group_check=True),
                                reads=[KW("VT", ti), KW("ST", ti)], writes=[("ps", bo)])
                    for vc in range(nvc):
                        P.op("pe", lambda e, vc=vc, q0=q0, cc=cc, bo=bo, c=c: e.matmul(
                            PS[bo][:, vc * 128 + cc * 64:vc * 128 + cc * 64 + 64],
                            lhsT=SBFp[c % 2][:, vc * 128:(vc + 1) * 128], rhs=QD[:, q0:q0 + 64],
                            start=False, stop=True, skip_group_check=True),
                            reads=[KW("SBFp", c % 2), KW("QD")], writes=[("ps", bo)])
                    bU, col = ub[c]
                    P.op("dve", lambda e, bU=bU, col=col, q0=q0, c=c: e.scalar_tensor_tensor(
                        out=SST2[(c + 1) % 2][:, h, 0:V], in0=SST2[c % 2][:, h, 0:V], scalar=T1[:, q0 + 63:q0 + 64], op0=ALU.mult,
                        in1=PS[bU][:, col:col + V], op1=ALU.add),
                        reads=[K("SST", c % 2, h), K("T1"), ("ps", bU)], writes=[K("SST", (c + 1) % 2, h)])
                    if c < 2 * ntile - 1:
                        P.op("act", lambda e, c=c: e.activation(out=SBFp[(c + 1) % 2][:, 0:V], in_=SST2[(c + 1) % 2][:, h, 0:V],
                                                              func=AF.Copy),
                             reads=[K("SST", (c + 1) % 2, h)], writes=[KW("SBFp", (c + 1) % 2)])
                    if cc == 1:
                        c0 = ti * 128
                        for vc, o32 in zip(range(nvc), (O32a, O32b)):
                            P.op("act", lambda e, vc=vc, o32=o32, c0=c0, bo=bo: e.activation(
                                out=o32[:, c0:c0 + 128], in_=PS[bo][:, vc * 128:vc * 128 + 128], func=AF.Copy),
                                reads=[("ps", bo)], writes=[K("O32", vc, ti)])
            if not sample:
                for ti in range(ntile):
                    c0 = ti * 128
                    b = nb()
                    P.op("pe", lambda e, b=b, c0=c0: e.matmul(PS[b][:, 0:128], lhsT=KI[:, c0:c0 + 128], rhs=QD[:, c0:c0 + 128],
                                                              start=True, stop=True),
                         reads=[KW("KI"), KW("QD")], writes=[("ps", b)])
                    P.op("dve", lambda e, b=b, ti=ti: e.tensor_tensor(out=STm[ti][:, 0:128], in0=PS[b][:, 0:128],
                                                                     in1=maskP[:, :], op=ALU.mult),
                         reads=[("ps", b), ("maskP",)], writes=[KW("ST", ti)])
                per = 512 // V
                ub = {}
                ubank = {}
                for c in range(2 * ntile):
                    ti, cc = divmod(c, 2)
                    slot = (cc, ti // per)
                    if slot not in ubank:
                        ubank[slot] = nb()
                    bU = ubank[slot]
                    col = (ti % per) * V
                    P.op("pe", lambda e, bU=bU, col=col, cc=cc, ti=ti: e.matmul(
                        PS[bU][:, col:col + V], lhsT=KEt[cc * 64:cc * 64 + 64, ti, :], rhs=VT[cc * 64:cc * 64 + 64, ti, 0:V],
                        start=(ti % per == 0), stop=True, skip_group_check=True),
                        reads=[KW("KEt", ti), KW("VT", ti)], writes=[("ps", bU)])
                    ub[c] = (bU, col)
                P.op("act", lambda e: e.activation(out=SBFp[0][:, 0:V], in_=SST2[0][:, h, 0:V], func=AF.Copy),
                     reads=[K("SST", 0, h)], writes=[KW("SBFp", 0)])
                for c in range(2 * ntile):
                    ti, cc = divmod(c, 2)
                    q0 = c * 64
                    if cc == 0:
                        bo = nb()
                        for vc in range(nvc):
                            P.op("pe", lambda e, bo=bo, ti=ti, vc=vc: e.matmul(
                                PS[bo][:, vc * 128:vc * 128 + 128], lhsT=VT[:, ti, vc * 128:(vc + 1) * 128], rhs=STm[ti][:, 0:128],
                                start=(vc == 0), stop=False, skip_group_check=True),
                                reads=[KW("VT", ti), KW("ST", ti)], writes=[("ps", bo)])
                    for vc in range(nvc):
                        P.op("pe", lambda e, vc=vc, q0=q0, cc=cc, bo=bo, c=c: e.matmul(
                            PS[bo][:, vc * 128 + cc * 64:vc * 128 + cc * 64 + 64],
                            lhsT=SBFp[c % 2][:, vc * 128:(vc + 1) * 128], rhs=QD[:, q0:q0 + 64],
                            start=False, stop=True, skip_group_check=True),
                            reads=[KW("SBFp", c % 2), KW("QD")], writes=[("ps", bo)])
                    bU, col = ub[c]
                    P.op("dve", lambda e, bU=bU, col=col, q0=q0, c=c: e.scalar_tensor_tensor(
                        out=SST2[(c + 1) % 2][:, h, 0:V], in0=SST2[c % 2][:, h, 0:V], scalar=T1[:, q0 + 63:q0 + 64], op0=ALU.mult,
                        in1=PS[bU][:, col:col + V], op1=ALU.add),
                        reads=[K("SST", c % 2, h), K("T1"), ("ps", bU)], writes=[K("SST", (c + 1) % 2, h)])
                    if c < 2 * ntile - 1:
                        P.op("act", lambda e, c=c: e.activation(out=SBFp[(c + 1) % 2][:, 0:V], in_=SST2[(c + 1) % 2][:, h, 0:V],
                                                              func=AF.Copy),
                             reads=[K("SST", (c + 1) % 2, h)], writes=[KW("SBFp", (c + 1) % 2)])
                    if cc == 1:
                        c0 = ti * 128
                        for vc, o32 in zip(range(nvc), (O32a, O32b)):
                            P.op("act", lambda e, vc=vc, o32=o32, c0=c0, bo=bo: e.activation(
                                out=o32[:, c0:c0 + 128], in_=PS[bo][:, vc * 128:vc * 128 + 128], func=AF.Copy),
                                reads=[("ps", bo)], writes=[K("O32", vc, ti)])
            for ti in (range(ntile) if sample else ()):
                c0 = ti * 128
                b = nb()
                P.op("pe", lambda e, b=b, c0=c0: e.matmul(PS[b][0:r, 0:r], lhsT=KI[:, c0:c0 + r], rhs=QD[:, c0:c0 + r],
                                                          start=True, stop=True),
                     reads=[KW("KI"), KW("QD")], writes=[("ps", b)])
                sm = STm[ti % 2]
                P.op("dve", lambda e, b=b, sm=sm: e.tensor_tensor(out=sm[0:r, 0:r], in0=PS[b][0:r, 0:r],
                                                                 in1=mask[0:r, 0:r], op=ALU.mult),
                     reads=[("ps", b), ("maskS" if sample else "maskP",)], writes=[KW("ST", ti % 2)])
                bo = nb()
                for vc in range(nvc):
                    ov = PS[bo][:, vc * 128:vc * 128 + r]
                    P.op("pe", lambda e, ov=ov, ti=ti, vc=vc, sm=sm: e.matmul(
                        ov, lhsT=VT[0:r, ti, vc * 128:(vc + 1) * 128], rhs=sm[0:r, 0:r], start=(vc == 0), stop=False,
                        skip_group_check=True),
                        reads=[KW("VT", ti), KW("ST", ti % 2)], writes=[("ps", bo)])
                if not sample:
                    for cc in range(2):
                        q0 = c0 + cc * 64
                        for vc in range(nvc):
                            P.op("pe", lambda e, vc=vc, q0=q0, cc=cc, bo=bo: e.matmul(
                                PS[bo][:, vc * 128 + cc * 64:vc * 128 + cc * 64 + 64],
                                lhsT=SBF[:, h, vc * 128:(vc + 1) * 128], rhs=QD[:, q0:q0 + 64],
                                start=False, stop=True, skip_group_check=True),
                                reads=[KW("SBF", h), KW("QD")], writes=[("ps", bo)])
                        bs = nb()
                        P.op("pe", lambda e, bs=bs, cc=cc, ti=ti: e.matmul(
                            PS[bs][:, 0:V], lhsT=KEt[cc * 64:cc * 64 + 64, ti, :], rhs=VT[cc * 64:cc * 64 + 64, ti, 0:V],
                            start=True, stop=True), reads=[KW("KEt", ti), KW("VT", ti)], writes=[("ps", bs)])
                        P.op("dve", lambda e, bs=bs, q0=q0: e.scalar_tensor_tensor(
                            out=SST[:, h, 0:V], in0=SST[:, h, 0:V], scalar=T1[:, q0 + 63:q0 + 64], op0=ALU.mult,
                            in1=PS[bs][:, 0:V], op1=ALU.add),
                            reads=[K("SST", 0, h), K("T1"), ("ps", bs)], writes=[K("SST", 0, h)])
                        P.op("act", lambda e: e.activation(out=SBF[:, h, 0:V], in_=SST[:, h, 0:V], func=AF.Copy),
                             reads=[K("SST", 0, h)], writes=[KW("SBF", h)])
                else:
                    nper = 1024 // V
                    for q0_ in range(0, 16, nper):
                        hb_ = (q0_ // nper) % 2
                        P.op("act", lambda e, q0_=q0_, hb_=hb_: e.activation(
                            out=hn[hb_][:, 0:nper * V].rearrange("p (s v) -> p s v", v=V), in_=S0F[:, q0_:q0_ + nper, 0:V], func=AF.Copy),
                            reads=[K("S0F")], writes=[("hn", hb_)])
                        for sq in range(q0_, q0_ + nper):
                            for vc in range(nvc):
                                o_ = (sq - q0_) * V + vc * 128
                                P.op("pe", lambda e, vc=vc, sq=sq, hb_=hb_, o_=o_, bo=bo: e.matmul(
                                    PS[bo][:, vc * 128 + sq * 4:vc * 128 + sq * 4 + 4],
                                    lhsT=hn[hb_][:, o_:o_ + 128], rhs=QD[:, sq * 4:sq * 4 + 4],
                                    start=False, stop=True, skip_group_check=True),
                                    reads=[("hn", hb_), KW("QD")], writes=[("ps", bo)])
                for vc, o32 in zip(range(nvc), (O32a, O32b)):
                    P.op("act", lambda e, vc=vc, o32=o32, c0=c0, bo=bo: e.activation(
                        out=o32[:, c0:c0 + r], in_=PS[bo][:, vc * 128:vc * 128 + r], func=AF.Copy),
                        reads=[("ps", bo)], writes=[K("O32", vc, ti)])
            if sample:
                for q4 in range(4):
                    P.op("dve", lambda e, q4=q4: e.tensor_tensor(
                        out=VBLK[:, :, 0:V], in0=VT[0:64, 0:1, 0:V].to_broadcast([64, 4, V]),
                        in1=maskS[0:64, 64 + q4 * 4:64 + q4 * 4 + 4].unsqueeze(2).to_broadcast([64, 4, V]), op=ALU.mult),
                        reads=[KW("VT", 0), ("maskS",)], writes=[KW("VBLK")])
                    for s4 in range(4):
                        sq = q4 * 4 + s4
                        bs = nb()
                        P.op("pe", lambda e, bs=bs, s4=s4: e.matmul(
                            PS[bs][:, 0:V], lhsT=KEt[0:64, 0, :], rhs=VBLK[:, s4, 0:V], start=True, stop=True),
                            reads=[KW("KEt", 0), KW("VBLK")], writes=[("ps", bs)])
                        P.op("dve", lambda e, bs=bs, sq=sq: e.scalar_tensor_tensor(
                            out=S0F[:, sq, 0:V], in0=S0F[:, sq, 0:V], scalar=T1[:, sq * 4 + 3:sq * 4 + 4], op0=ALU.mult,
                            in1=PS[bs][:, 0:V], op1=ALU.add),
                            reads=[K("S0F"), K("T1"), ("ps", bs)], writes=[K("S0F")])
                P.op("sp", lambda e: e.dma_start(out=ss_d[:, h].rearrange("s k v -> k s v"), in_=S0F[:, :, 0:V]),
                     reads=[K("S0F")], dma=True)
            bq = nb()
            for vc, o32 in zip(range(nvc), (O32a, O32b)):
                sqb, sqk = (QD, KW("QD")) if vc == 0 else (KI, KW("KI"))
                P.op("act", lambda e, o32=o32, sqb=sqb: e.activation(out=sqb[:, 0:n], in_=o32[:, 0:n], func=AF.Square),
                     reads=[K("O32", vc)], writes=[sqk])
                P.op("pe", lambda e, vc=vc, sqb=sqb: e.matmul(PS[bq][:, 0:n], lhsT=ones_bf[:, :], rhs=sqb[:, 0:n],
                                                              start=(vc == 0), stop=(vc == nvc - 1)),
                     reads=[sqk, ("ones",)], writes=[("ps", bq)])
            P.op("dve", lambda e: e.tensor_scalar(out=RSTD[:, 0:n], in0=PS[bq][:, 0:n], scalar1=1.0 / V, scalar2=EPS,
                                                  op0=ALU.mult, op1=ALU.add),
                 reads=[("ps", bq)], writes=[K("RSTD")])
            P.op("act", lambda e: e.activation(out=RSTD[:, 0:n], in_=RSTD[:, 0:n], func=AF.Ln),
                 reads=[K("RSTD")], writes=[K("RSTD")])
            P.op("act", lambda e: e.activation(out=RSTD[:, 0:n], in_=RSTD[:, 0:n], func=AF.Exp, scale=-0.5),
                 reads=[K("RSTD")], writes=[K("RSTD")])
            for vc, o32, gt in zip(range(nvc), (O32a, O32b), (GTa, GTb)):
                P.op("dve", lambda e, o32=o32, vc=vc: e.scalar_tensor_tensor(
                    out=o32[:, 0:n], in0=o32[:, 0:n], scalar=cols[:, normcol0 + vc:normcol0 + vc + 1], op0=ALU.mult,
                    in1=RSTD[:, 0:n], op1=ALU.mult),
                    reads=[K("O32", vc), K("RSTD"), ("cols",)], writes=[K("O32", vc)])
                P.op("dve", lambda e, o32=o32, gt=gt, vc=vc: e.tensor_tensor(
                    out=OAT[:, oc0 + vc, 0:n], in0=o32[:, 0:n], in1=gt[:, 0:n], op=ALU.mult),
                    reads=[K("O32", vc), K("GT", vc)], writes=[KW("OAT", oc0 + vc)])

        def out_proj(w_out, t0, n, nfc):
            slots = []
            for fc in range(nfc):
                s_ = wslot[0] % 8
                wslot[0] += 1
                P.op("pool", lambda e, s_=s_, fc=fc: e.dma_start(
                    out=Wr[s_][:].rearrange("p c n -> p (c n)"), in_=w_out[fc * 128:(fc + 1) * 128, :]),
                    writes=[("Wr", s_)], dma=True)
                slots.append(s_)
            for ti in range(max(1, n // 128)):
                r = min(128, n)
                tt = t0 // 128 + ti
                for dh in range(2):
                    b = nb()
                    for fc in range(nfc):
                        P.op("pe", lambda e, fc=fc, b=b, ti=ti, dh=dh: e.matmul(
                            PS[b][0:r, :], lhsT=OAT[:, fc, ti * 128:ti * 128 + r],
                            rhs=Wr[slots[fc]][:].rearrange("p c n -> p (c n)")[:, dh * 512:(dh + 1) * 512],
                            start=(fc == 0), stop=(fc == nfc - 1)),
                            reads=[KW("OAT", fc), ("Wr", slots[fc])], writes=[("ps", b)])
                    P.op("dve", lambda e, b=b, tt=tt, dh=dh: e.tensor_tensor(
                        out=X[0:r, tt, dh * 512:(dh + 1) * 512], in0=PS[b][0:r, :],
                        in1=X[0:r, tt, dh * 512:(dh + 1) * 512], op=ALU.add),
                        reads=[("ps", b), ("X", tt)], writes=[("X", tt)])

        def evac(func, dst, dkey, b, n, rows_=128, **kw):
            P.op("act", lambda e: e.activation(out=dst, in_=PS[b][0:rows_, 0:n], func=func, **kw),
                 reads=[("ps", b)], writes=[dkey])

        def hgrn_head(si, t0, n, h, sample):
            W = ab_w_in[0]
            b = proj_fm(W, h * 128, t0, n)
            evac(AF.Silu, QV[:, 0:n], K("QV"), b, n)
            b = proj_fm(W, 1536 + h * 128, t0, n)
            evac(AF.Silu, GTa[:, 0:n], K("GT", 0), b, n)
            b = proj_fm(W, 512 + h * 128, t0, n)
            evac(AF.Exp, T1[:, 0:n], K("T1"), b, n, scale=-1.0)
            P.op("dve", lambda e: e.tensor_scalar(out=T1[:, 0:n], in0=T1[:, 0:n], scalar1=1.0, scalar2=None, op0=ALU.add),
                 reads=[K("T1")], writes=[K("T1")])
            P.op("act", lambda e: e.activation(out=T1[:, 0:n], in_=T1[:, 0:n], func=AF.Ln),
                 reads=[K("T1")], writes=[K("T1")])
            P.op("act", lambda e: e.activation(out=T1[:, 0:n], in_=T1[:, 0:n], func=AF.Exp, scale=-1.0),
                 reads=[K("T1")], writes=[K("T1")])
            P.op("dve", lambda e: e.tensor_scalar(out=T1[:, 0:n], in0=T1[:, 0:n], scalar1=cols[:, 8 + h:9 + h],
                                                  scalar2=cols[:, h:h + 1], op0=ALU.mult, op1=ALU.add),
                 reads=[K("T1"), ("cols",)], writes=[K("T1")])
            P.op("dve", lambda e: e.tensor_scalar(out=KV[:, 0:n], in0=T1[:, 0:n], scalar1=-1.0, scalar2=1.0,
                                                  op0=ALU.mult, op1=ALU.add),
                 reads=[K("T1")], writes=[K("KV")])
            P.op("act", lambda e: e.activation(out=LF[:, 0:n], in_=T1[:, 0:n], func=AF.Ln),
                 reads=[K("T1")], writes=[K("LF")])
            proj_tm(W, 1024 + h * 128, t0, n, lambda ti, r: VT[0:r, ti, 0:128], KW("VT"))
            gla_core(si, t0, n, h, 128, sample, hs0, hgp, hgs, 12, h)

        identF = CST[:, 0:128]
        onesF = CST[:, 128:256]
        blkS = CST[0:64, 256:320]
        lastS = CST[0:64, 320:336]
        colA = CST[:, 336:337]
        colB = CST[:, 337:338]
        CW = cols[:, 24:56].rearrange("p (c k) -> p c k", k=4)
        CB = cols[:, 56:64]
        DTB, AROW, DROW, SNW = cols2[:, 0:8], cols2[:, 8:16], cols2[:, 16:24], cols2[:, 24:28]
        MS = aTf[:, 11104:11616]
        MSB = Wob[:, 8448:8960]

        def mamba_setup():
            P.op("sp", lambda e: e.dma_start(out=CST[:], in_=cst_d), writes=[("cst",)], dma=True)
            for k in range(4):
                P.op("sp", lambda e, k=k: e.dma_start(out=CW[:, :, k], in_=conv_w[0, k].rearrange("(c p) -> p c", p=128)),
                     writes=[("cols", "cw", k)], dma=True)
            P.op("sp", lambda e: e.dma_start(out=CB, in_=conv_b[0].rearrange("(c p) -> p c", p=128)),
                 writes=[("cols", "cb")], dma=True)
            P.op("sp", lambda e: e.dma_start(out=DTB, in_=dt_bias[0].partition_broadcast(128)), writes=[("cols2", "dtb")], dma=True)
            P.op("sp", lambda e: e.dma_start(out=AROW, in_=a_log[0].partition_broadcast(128)), writes=[("cols2", "a")], dma=True)
            P.op("sp", lambda e: e.dma_start(out=DROW, in_=ssm_d[0].partition_broadcast(128)), writes=[("cols2", "d")], dma=True)
            P.op("sp", lambda e: e.dma_start(out=SNW, in_=ssm_norm[0].rearrange("(c p) -> p c", p=128)),
                 writes=[("cols2", "snw")], dma=True)
            P.op("act", lambda e: e.activation(out=AROW, in_=AROW, func=AF.Exp), reads=[("cols2", "a")], writes=[("cols2", "a")])
            P.op("dve", lambda e: e.tensor_scalar(out=AROW, in0=AROW, scalar1=-1.0, scalar2=None, op0=ALU.mult),
                 reads=[("cols2", "a")], writes=[("cols2", "a")])
            P.op("dve", lambda e: e.memset(TAIL[:, :, :], 0.0), writes=[("tail",)])

        def KM(*a):
            return ("aT", "mx", "m") + a

        def KWM(*a):
            return ("Wo", "mx", "m") + a

        def mamba_tile_group(si, t0, n):
            sample = (si == 4)
            W = ab_w_in[0]
            r = min(128, n)
            ntile = max(1, n // 128)
            barrier()
            if si == 0:
                P.op("dve", lambda e: e.memset(MS, 0.0), writes=[KM("MS")])
                P.op("dve", lambda e: e.memset(MSB, 0.0), writes=[KWM("MSB")])
            fa = [0]

            def takeF(k, lo=None):
                o = fa[0]
                fa[0] += k
                assert fa[0] <= 5632
                return aTf[:, o:o + k]
            CONVX = takeF(2048).rearrange("p (c n) -> p c n", c=4)
            A1 = takeF(512)
            A2 = takeF(512)
            ZS = takeF(2048).rearrange("p (t f) -> p t f", t=4)
            XTOK = takeF(512)
            ZSf = ZS.rearrange("p t f -> p (t f)")
            CS0T = ZSf[0:48, 0:1024]
            CVOUT = ZSf[0:48, 1024:2048]
            fb = [6656]

            def takeG(k):
                o = fb[0]
                fb[0] += k
                assert fb[0] <= 11104
                return aTf[:, o:o + k]
            XB = takeG(8 * 520).rearrange("p (c n) -> p c n", c=8)
            wa = [0]

            def takeW(k):
                o = wa[0]
                wa[0] += k
                assert wa[0] <= 3328
                return Wob[:, o:o + k]
            BT = takeW(2 * n).rearrange("p (g n) -> p g n", g=2)
            CT = takeW(2 * n).rearrange("p (g n) -> p g n", g=2)
            BTOK = takeW(256)
            XDT = takeW(512)
            XEND = takeW(512)
            wb = [8960]

            def takeW2(k):
                o = wb[0]
                wb[0] += k
                assert wb[0] <= 11264
                return Wob[:, o:o + k]
            _mm = takeW2(512)
            MM4 = [_mm, _mm]
            YN = takeW2(512)
            CTm = takeW2(512).rearrange("p (g c i) -> p g c i", g=2, c=2)

            for c in range(8):
                b = proj_fm(W, 2560 + c * 128, t0, n)
                if not sample:
                    P.op("dve", lambda e, c=c: e.tensor_copy(out=XB[:, c, 0:3], in_=TAIL[:, c, 0:3]),
                         reads=[("tail", c)], writes=[KM("XB", c)])
                    P.op("act", lambda e, c=c, b=b: e.activation(out=XB[:, c, 3:3 + n], in_=PS[b][:, 0:n], func=AF.Copy),
                         reads=[("ps", b)], writes=[KM("XB", c)])
                    P.op("dve", lambda e, c=c: e.tensor_copy(out=TAIL[:, c, 0:3], in_=XB[:, c, n:n + 3]),
                         reads=[KM("XB", c)], writes=[("tail", c)])
                    if si == 3:
                        P.op("sp", lambda e, c=c: e.dma_start(out=cvp[:, c * 128:(c + 1) * 128].rearrange("j f -> f j"),
                                                              in_=XB[:, c, n:n + 3]), reads=[KM("XB", c)], dma=True)
                    xin = lambda k, c=c: XB[:, c, k:k + n]
                    AC, ack = (A1, KM("A1")) if c % 2 == 0 else (A2, KM("A2"))
                    acc = AC[:, 0:n]
                else:
                    xb3 = XB[:, c, 0:112].rearrange("p (s t) -> p s t", t=7)
                    if c == 0:
                        P.op("sp", lambda e: e.dma_start(out=CS0T, in_=cs0.rearrange("s j f -> (s j) f")),
                             writes=[KM("ZS")], dma=True)
                    bc = nb()
                    P.op("pe", lambda e, c=c, bc=bc: e.transpose(out=PS[bc][:, 0:48], in_=CS0T[:, c * 128:(c + 1) * 128],
                                                                 identity=identF[0:48, 0:48]),
                         reads=[KM("ZS"), ("cst",)], writes=[("ps", bc)])
                    P.op("act", lambda e, bc=bc, xb3=xb3: e.activation(
                        out=xb3[:, :, 0:3], in_=PS[bc][:, 0:48].rearrange("p (s t) -> p s t", t=3), func=AF.Copy),
                        reads=[("ps", bc)], writes=[KM("XB", c)])
                    P.op("act", lambda e, c=c, b=b, xb3=xb3: e.activation(
                        out=xb3[:, :, 3:7], in_=PS[b][:, 0:64].rearrange("p (s t) -> p s t", t=4), func=AF.Copy),
                        reads=[("ps", b)], writes=[KM("XB", c)])
                    P.op("dve", lambda e, xb3=xb3: e.tensor_copy(out=A2[:, 0:48].rearrange("p (s t) -> p s t", t=3), in_=xb3[:, :, 4:7]),
                         reads=[KM("XB", c)], writes=[KM("A2")])
                    P.op("pe", lambda e, c=c: e.transpose(out=PS[6 + c // 4][0:48, (c % 4) * 128:(c % 4) * 128 + 128], in_=A2[:, 0:48],
                                                          identity=identF),
                         reads=[KM("A2"), ("cst",)], writes=[("ps", 6 + c // 4)])
                    if c == 7:
                        for hb_ in range(2):
                            P.op("act", lambda e, hb_=hb_: e.activation(out=CVOUT[:, hb_ * 512:(hb_ + 1) * 512],
                                                                        in_=PS[6 + hb_][0:48, :], func=AF.Copy),
                                 reads=[("ps", 6 + hb_)], writes=[KM("ZS")])
                        P.op("sp", lambda e: e.dma_start(out=cvs.rearrange("s j f -> (s j) f"), in_=CVOUT),
                             reads=[KM("ZS")], dma=True)
                    xin = lambda k, xb3=xb3: xb3[:, :, k:k + 4]
                    AC, ack = A1, KM("A1")
                    acc = A1[:, 0:64].rearrange("p (s t) -> p s t", t=4)
                P.op("dve", lambda e, c=c, xin=xin, acc=acc: e.tensor_scalar(
                    out=acc, in0=xin(0), scalar1=CW[:, c, 0:1], scalar2=CB[:, c:c + 1], op0=ALU.mult, op1=ALU.add),
                    reads=[KM("XB", c), ("cols",)], writes=[ack])
                for k in range(1, 4):
                    P.op("dve", lambda e, c=c, k=k, xin=xin, acc=acc: e.scalar_tensor_tensor(
                        out=acc, in0=xin(k), scalar=CW[:, c, k:k + 1], op0=ALU.mult, in1=acc, op1=ALU.add),
                        reads=[KM("XB", c), ack, ("cols",)], writes=[ack])
                if c < 4:
                    dst, dk = CONVX[:, c, 0:n], KM("CONVX", c)
                elif c < 6:
                    dst, dk = BT[:, c - 4, 0:n], KWM("BT", c - 4)
                else:
                    dst, dk = CT[:, c - 6, 0:n], KWM("CT", c - 6)
                P.op("act", lambda e, dst=dst, AC=AC: e.activation(out=dst, in_=AC[:, 0:n], func=AF.Silu),
                     reads=[ack], writes=[dk])
            for zc in range(4):
                proj_tm(W, 2048 + zc * 128, t0, n, lambda ti, rr, zc=zc: ZS[0:rr, ti, zc * 128:(zc + 1) * 128],
                        KM("ZS"), func=AF.Silu)
            sDT = wload(W, 3584, 8)
            barrier()
            fb[0] = 6656
            CBm = [takeG(128), takeG(128)]
            DG4 = [takeG(4 * r), takeG(4 * r)]
            DM4 = [takeG(4 * r), takeG(4 * r)]
            if sample:
                SNAT = [takeG(512).rearrange("p (a n) -> p a n", a=4) for _ in range(2)]
                SNEW = [takeG(512).rearrange("p (a n) -> p a n", a=4) for _ in range(2)]
                ETR = takeG(512)
                DECP = takeG(64).rearrange("p (a s) -> p a s", a=4)
                XENDm = takeW(512)
                S0T = [takeW(512), MSB]
            mask = maskS if sample else maskP
            mkey = ("maskS",) if sample else ("maskP",)

            if sample:
                XTOKs, A1s, XENDs, BTOKs = [XTOK, XTOK], [A1, A1], [XEND, XEND], [BTOK, BTOK]
            else:
                XTOKs, A1s = [XTOK, takeG(512)], [A1, takeG(512)]
                XENDs, BTOKs = [XEND, takeW2(512)], [BTOK, takeW2(256)]
            Wd = ntile * 8
            DTw, DTAw, CUMw, TOTw, ECUMw, EENDw, TAw, TBw, DECAw, DECBw, DTMw = [takeG(Wd) for _ in range(11)]
            SSQ, RSQ = takeG(8), takeG(8)
            v3 = lambda a: a[0:r, 0:Wd].rearrange("p (t h) -> p t h", h=8)
            b = nb()
            for ti in range(ntile):
                tt = t0 // 128 + ti
                for c in range(8):
                    P.op("pe", lambda e, c=c, b=b, tt=tt, ti=ti: e.matmul(
                        PS[b][0:r, ti * 8:(ti + 1) * 8], lhsT=hT[:, c, tt * 128:tt * 128 + r], rhs=Wr[sDT][:, c, 0:8],
                        start=(c == 0 and ti == 0), stop=(c == 7), skip_group_check=True),
                        reads=[("Wr", sDT), ("hT", tt)], writes=[("ps", b)])
            P.op("dve", lambda e, b=b: e.tensor_tensor(out=v3(DTw), in0=PS[b][0:r, 0:Wd].rearrange("p (t h) -> p t h", h=8),
                                                       in1=DTB[0:r].unsqueeze(1).to_broadcast([r, ntile, 8]), op=ALU.add),
                 reads=[("ps", b), ("cols2",)], writes=[KM("DT")])
            P.op("act", lambda e: e.activation(out=DTw[0:r, 0:Wd], in_=DTw[0:r, 0:Wd], func=AF.Exp), reads=[KM("DT")], writes=[KM("DT")])
            P.op("dve", lambda e: e.tensor_scalar(out=DTw[0:r, 0:Wd], in0=DTw[0:r, 0:Wd], scalar1=1.0, scalar2=None, op0=ALU.add),
                 reads=[KM("DT")], writes=[KM("DT")])
            P.op("act", lambda e: e.activation(out=DTw[0:r, 0:Wd], in_=DTw[0:r, 0:Wd], func=AF.Ln), reads=[KM("DT")], writes=[KM("DT")])
            P.op("dve", lambda e: e.tensor_tensor(out=v3(DTAw), in0=v3(DTw), in1=AROW[0:r].unsqueeze(1).to_broadcast([r, ntile, 8]),
                                                  op=ALU.mult), reads=[KM("DT"), ("cols2",)], writes=[KM("DTA")])
            b = nb()
            P.op("pe", lambda e, b=b: e.matmul(PS[b][0:r, 0:Wd], lhsT=mask[0:r, 0:r], rhs=DTAw[0:r, 0:Wd], start=True, stop=True),
                 reads=[mkey, KM("DTA")], writes=[("ps", b)])
            P.op("act", lambda e, b=b: e.activation(out=CUMw[0:r, 0:Wd], in_=PS[b][0:r, 0:Wd], func=AF.Copy),
                 reads=[("ps", b)], writes=[KM("CUM")])
            if not sample:
                for (cm, TX, DECX) in ((colA, TAw, DECAw), (colB, TBw, DECBw)):
                    P.op("dve", lambda e, cm=cm: e.tensor_scalar(out=DTMw[:, 0:Wd], in0=DTAw[:, 0:Wd], scalar1=cm, scalar2=None, op0=ALU.mult),
                         reads=[KM("DTA"), ("cst",)], writes=[KM("DTM")])
                    b = nb()
                    P.op("pe", lambda e, b=b: e.matmul(PS[b][:, 0:Wd], lhsT=onesF, rhs=DTMw[:, 0:Wd], start=True, stop=True),
                         reads=[("cst",), KM("DTM")], writes=[("ps", b)])
                    P.op("act", lambda e, b=b, TX=TX: e.activation(out=TX[:, 0:Wd], in_=PS[b][:, 0:Wd], func=AF.Copy),
                         reads=[("ps", b)], writes=[KM("TX")])
                    P.op("act", lambda e, TX=TX, DECX=DECX: e.activation(out=DECX[:, 0:Wd], in_=TX[:, 0:Wd], func=AF.Exp),
                         reads=[KM("TX")], writes=[KM("DEC")])
                P.op("dve", lambda e: e.tensor_scalar(out=TOTw[:, 0:Wd], in0=TAw[:, 0:Wd], scalar1=colA, scalar2=None, op0=ALU.mult),
                     reads=[KM("TX"), ("cst",)], writes=[KM("TOT")])
                P.op("dve", lambda e: e.scalar_tensor_tensor(out=TOTw[:, 0:Wd], in0=TBw[:, 0:Wd], scalar=colB, op0=ALU.mult,
                                                             in1=TOTw[:, 0:Wd], op1=ALU.add),
                     reads=[KM("TX"), KM("TOT"), ("cst",)], writes=[KM("TOT")])
            else:
                b = nb()
                P.op("pe", lambda e, b=b: e.matmul(PS[b][0:64, 0:8], lhsT=blkS, rhs=DTAw[0:64, 0:8], start=True, stop=True),
                     reads=[("cst",), KM("DTA")], writes=[("ps", b)])
                P.op("act", lambda e, b=b: e.activation(out=TOTw[0:64, 0:8], in_=PS[b][0:64, 0:8], func=AF.Copy),
                     reads=[("ps", b)], writes=[KM("TOT")])
            P.op("act", lambda e: e.activation(out=ECUMw[0:r, 0:Wd], in_=CUMw[0:r, 0:Wd], func=AF.Exp), reads=[KM("CUM")], writes=[KM("ECUM")])
            P.op("dve", lambda e: e.tensor_tensor(out=EENDw[0:r, 0:Wd], in0=TOTw[0:r, 0:Wd], in1=CUMw[0:r, 0:Wd], op=ALU.subtract),
                 reads=[KM("TOT"), KM("CUM")], writes=[KM("EEND")])
            P.op("act", lambda e: e.activation(out=EENDw[0:r, 0:Wd], in_=EENDw[0:r, 0:Wd], func=AF.Exp), reads=[KM("EEND")], writes=[KM("EEND")])

            def _tile(ti, part):
                tt = t0 // 128 + ti
                c0 = ti * 128
                DT, DTA, CUM, TOT, ECUM, EEND, TA, TB, DECA, DECB = [a_[:, ti * 8:(ti + 1) * 8] for a_ in
                                                                     (DTw, DTAw, CUMw, TOTw, ECUMw, EENDw, TAw, TBw, DECAw, DECBw)]
                pp = ti % 2
                XTOK, A1, XEND, BTOK = XTOKs[pp], A1s[pp], XENDs[pp], BTOKs[pp]
                x3 = XTOK[0:r].rearrange("p (h q) -> p h q", q=64)
                if part == 0:
                    b = nb()
                    for c in range(4):
                        P.op("pe", lambda e, c=c, b=b, c0=c0: e.transpose(out=PS[b][0:r, c * 128:(c + 1) * 128],
                                                                         in_=CONVX[:, c, c0:c0 + r], identity=identF),
                             reads=[KM("CONVX", c), ("cst",)], writes=[("ps", b)])
                    P.op("act", lambda e, b=b: e.activation(out=XTOK[0:r], in_=PS[b][0:r, :], func=AF.Copy),
                         reads=[("ps", b)], writes=[KM("XTOK", pp)])
                    P.op("dve", lambda e, x3=x3: e.tensor_tensor(
                        out=XDT[0:r].rearrange("p (h q) -> p h q", q=64), in0=x3,
                        in1=DT[0:r].unsqueeze(2).to_broadcast([r, 8, 64]), op=ALU.mult),
                        reads=[KM("XTOK", pp), KM("DT")], writes=[KWM("XDT")])
                    P.op("dve", lambda e: e.tensor_tensor(
                        out=XEND[0:r].rearrange("p (h q) -> p h q", q=64), in0=XDT[0:r].rearrange("p (h q) -> p h q", q=64),
                        in1=EEND[0:r].unsqueeze(2).to_broadcast([r, 8, 64]), op=ALU.mult),
                        reads=[KWM("XDT"), KM("EEND")], writes=[KWM("XEND", pp)])
                    b = nb()
                    pvb = PS[b][:].bitcast(BF16)
                    for g in range(2):
                        P.op("pe", lambda e, g=g, pvb=pvb, c0=c0: e.transpose(out=pvb[0:r, g * 128:(g + 1) * 128],
                                                                             in_=BT[:, g, c0:c0 + r], identity=ident[:, :]),
                             reads=[KWM("BT", g), ("ident",)], writes=[("ps", b)])
                    P.op("act", lambda e, pvb=pvb, b=b: e.activation(out=BTOK[0:r], in_=pvb[0:r, 0:256], func=AF.Copy),
                         reads=[("ps", b)], writes=[KWM("BTOK", pp)])
                    byi = 6
                    for g in range(2):
                        b = nb()
                        P.op("pe", lambda e, g=g, b=b, c0=c0: e.matmul(PS[b][0:r, 0:r], lhsT=BT[:, g, c0:c0 + r],
                                                                      rhs=CT[:, g, c0:c0 + r], start=True, stop=True),
                             reads=[KWM("BT", g), KWM("CT", g)], writes=[("ps", b)])
                        P.op("dve", lambda e, g=g, b=b: e.tensor_tensor(out=CBm[g][0:r, 0:r], in0=PS[b][0:r, 0:r],
                                                                        in1=mask[0:r, 0:r], op=ALU.mult),
                             reads=[("ps", b), mkey], writes=[KM("CBm", g)])
                    for g in range(2):
                        dg, dm, mm4 = DG4[g], DM4[g], MM4[g]
                        cum4 = CUM[0:r, 4 * g:4 * g + 4]
                        P.op("dve", lambda e, dg=dg, cum4=cum4: e.tensor_tensor(
                            out=dg[0:r, :].rearrange("p (h i) -> p h i", h=4),
                            in0=identF[0:r, 0:r].unsqueeze(1).to_broadcast([r, 4, r]),
                            in1=cum4.unsqueeze(2).to_broadcast([r, 4, r]), op=ALU.mult),
                            reads=[("cst",), KM("CUM")], writes=[KM("DG", g)])
                        b = nb()
                        P.op("pe", lambda e, b=b, dg=dg: e.matmul(PS[b][0:r, 0:4 * r], lhsT=onesF[0:r, 0:r], rhs=dg[0:r, 0:4 * r],
                                                                  start=True, stop=True),
                             reads=[("cst",), KM("DG", g)], writes=[("ps", b)])
                        P.op("dve", lambda e, b=b, dm=dm, cum4=cum4: e.tensor_tensor(
                            out=dm[0:r, :].rearrange("p (h i) -> p h i", h=4),
                            in0=PS[b][0:r, 0:4 * r].rearrange("p (h i) -> p h i", h=4),
                            in1=cum4.unsqueeze(2).to_broadcast([r, 4, r]), op=ALU.subtract),
                            reads=[("ps", b), KM("CUM")], writes=[KM("DM", g)])
                        P.op("act", lambda e, dm=dm: e.activation(out=dm[0:r, 0:4 * r], in_=dm[0:r, 0:4 * r], func=AF.Exp),
                             reads=[KM("DM", g)], writes=[KM("DM", g)])
                        P.op("dve", lambda e, dm=dm, mm4=mm4, g=g: e.scalar_tensor_tensor(
                            out=mm4[0:r, 0:4 * r].rearrange("p (h i) -> p h i", h=4),
                            in0=dm[0:r, :].rearrange("p (h i) -> p h i", h=4), scalar=1.0, op0=ALU.min,
                            in1=CBm[g][0:r, 0:r].unsqueeze(1).to_broadcast([r, 4, r]), op1=ALU.mult),
                            reads=[KM("DM", g), KM("CBm", g)], writes=[KWM("MM", 0)])
                        for hh in range(4):
                            h = 4 * g + hh
                            P.op("pe", lambda e, h=h, hh=hh, mm4=mm4, byi=byi: e.matmul(
                                PS[byi][0:r, h * 64:(h + 1) * 64], lhsT=mm4[0:r, hh * r:(hh + 1) * r], rhs=XDT[0:r, h * 64:(h + 1) * 64],
                                start=(h == 0), stop=(h == 7), skip_group_check=True),
                                reads=[KWM("MM", 0), KWM("XDT")], writes=[("ps", byi)])
                    P.op("act", lambda e, byi=byi: e.activation(out=A1[0:r, :], in_=PS[byi][0:r, :], func=AF.Copy),
                         reads=[("ps", byi)], writes=[KM("A1", pp)])
                    return
                byx = 7
                if not sample:
                    for cc in range(2):
                        P.op("dve", lambda e, cc=cc, c0=c0: e.tensor_copy(out=CTm[:, :, cc, cc * 64:cc * 64 + 64],
                                                                         in_=CT[:, :, c0 + cc * 64:c0 + cc * 64 + 64]),
                             reads=[KWM("CT")], writes=[KWM("CTm", cc)])
                        P.op("dve", lambda e, cc=cc: e.memset(CTm[:, :, cc, (1 - cc) * 64:(1 - cc) * 64 + 64], 0.0),
                             writes=[KWM("CTm", cc)])
                    for cc in range(2):
                        for g in range(2):
                            P.op("pe", lambda e, cc=cc, g=g, byx=byx: e.matmul(
                                PS[byx][:, g * 256:(g + 1) * 256], lhsT=CTm[:, g, cc, :], rhs=MSB[:, g * 256:(g + 1) * 256],
                                start=(cc == 0 and g == 0), stop=(cc == 1 and g == 1), skip_group_check=True),
                                reads=[KWM("CTm", cc), KWM("MSB")], writes=[("ps", byx)])
                        bu = nb()
                        for g in range(2):
                            P.op("pe", lambda e, cc=cc, g=g, bu=bu: e.matmul(
                                PS[bu][:, g * 256:(g + 1) * 256], lhsT=BTOK[cc * 64:cc * 64 + 64, g * 128:(g + 1) * 128],
                                rhs=XEND[cc * 64:cc * 64 + 64, g * 256:(g + 1) * 256], start=(g == 0), stop=(g == 1),
                                skip_group_check=True),
                                reads=[KWM("BTOK", pp), KWM("XEND", pp)], writes=[("ps", bu)])
                        DECX = DECA if cc == 0 else DECB
                        P.op("dve", lambda e, DECX=DECX: e.tensor_tensor(
                            out=MS.rearrange("p (h q) -> p h q", q=64), in0=MS.rearrange("p (h q) -> p h q", q=64),
                            in1=DECX.unsqueeze(2).to_broadcast([128, 8, 64]), op=ALU.mult),
                            reads=[KM("MS"), KM("DEC")], writes=[KM("MS")])
                        P.op("dve", lambda e, bu=bu: e.tensor_tensor(out=MS, in0=MS, in1=PS[bu][:, :], op=ALU.add),
                             reads=[KM("MS"), ("ps", bu)], writes=[KM("MS")])
                        P.op("act", lambda e: e.activation(out=MSB, in_=MS, func=AF.Copy), reads=[KM("MS")], writes=[KWM("MSB")])
                else:
                    P.op("act", lambda e: e.activation(out=TA[0:64], in_=TOT[0:64], func=AF.Exp), reads=[KM("TOT")], writes=[KM("TX")])
                    P.op("dve", lambda e: e.tensor_copy(out=ETR[0:64].rearrange("p (h q) -> p h q", q=64),
                                                        in_=TA[0:64].unsqueeze(2).to_broadcast([64, 8, 64])),
                         reads=[KM("TX")], writes=[KM("ETR")])
                    bd = nb()
                    for a in range(4):
                        P.op("pe", lambda e, a=a, bd=bd: e.matmul(PS[bd][:, a * 16:(a + 1) * 16], lhsT=ETR[0:64, a * 128:(a + 1) * 128],
                                                                  rhs=lastS, start=(a == 0), stop=(a == 3), skip_group_check=True),
                             reads=[KM("ETR"), ("cst",)], writes=[("ps", bd)])
                    P.op("act", lambda e, bd=bd: e.activation(out=DECP, in_=PS[bd][:, 0:64].rearrange("p (a s) -> p a s", a=4),
                                                              func=AF.Copy), reads=[("ps", bd)], writes=[KM("DECP")])
                    for sq in range(16):
                        sn, snew, s0t = SNAT[sq % 2], SNEW[sq % 2], S0T[sq % 2]
                        P.op("sp", lambda e, sq=sq, sn=sn: e.dma_start(
                            out=sn, in_=ss0[sq].rearrange("(a b) p n -> (b p) a n", b=2)), writes=[KM("SNAT", sq % 2)], dma=True)
                        bt_ = nb()
                        for a in range(4):
                            P.op("pe", lambda e, a=a, bt_=bt_, sn=sn: e.transpose(out=PS[bt_][:, a * 128:(a + 1) * 128],
                                                                               in_=sn[:, a, :], identity=identF),
                                 reads=[KM("SNAT", sq % 2), ("cst",)], writes=[("ps", bt_)])
                        P.op("act", lambda e, bt_=bt_, s0t=s0t: e.activation(out=s0t, in_=PS[bt_][:, :], func=AF.Copy),
                             reads=[("ps", bt_)], writes=[KWM("S0T", sq % 2)])
                        P.op("dve", lambda e: e.memset(CTm[:, :, 0, 0:64], 0.0), writes=[KWM("CTm", 0)])
                        P.op("dve", lambda e, sq=sq: e.tensor_copy(out=CTm[:, :, 0, sq * 4:sq * 4 + 4], in_=CT[:, :, sq * 4:sq * 4 + 4]),
                             reads=[KWM("CT")], writes=[KWM("CTm", 0)])
                        for g in range(2):
                            P.op("pe", lambda e, g=g, sq=sq, s0t=s0t, byx=byx: e.matmul(
                                PS[byx][0:64, g * 256:(g + 1) * 256], lhsT=CTm[:, g, 0, 0:64], rhs=s0t[:, g * 256:(g + 1) * 256],
                                start=(sq == 0 and g == 0), stop=(sq == 15 and g == 1), skip_group_check=True),
                                reads=[KWM("CTm", 0), KWM("S0T", sq % 2)], writes=[("ps", byx)])
                        P.op("dve", lambda e, sq=sq: e.tensor_scalar(out=XENDm[0:64], in0=XEND[0:64],
                                                                     scalar1=maskS[0:64, 64 + sq:65 + sq], scalar2=None, op0=ALU.mult),
                             reads=[KWM("XEND", pp), ("maskS",)], writes=[KWM("XENDm")])
                        bn = nb()
                        for a in range(4):
                            P.op("pe", lambda e, a=a, bn=bn: e.matmul(
                                PS[bn][:, a * 128:(a + 1) * 128], lhsT=XENDm[0:64, a * 128:(a + 1) * 128],
                                rhs=BTOK[0:64, (a // 2) * 128:(a // 2) * 128 + 128], start=(a == 0), stop=(a == 3),
                                skip_group_check=True),
                                reads=[KWM("XENDm"), KWM("BTOK", pp)], writes=[("ps", bn)])
                        for a in range(4):
                            P.op("dve", lambda e, a=a, sq=sq, bn=bn, sn=sn, snew=snew: e.scalar_tensor_tensor(
                                out=snew[:, a, :], in0=sn[:, a, :], scalar=DECP[:, a, sq:sq + 1], op0=ALU.mult,
                                in1=PS[bn][:, a * 128:(a + 1) * 128], op1=ALU.add),
                                reads=[KM("SNAT", sq % 2), KM("DECP"), ("ps", bn)], writes=[KM("SNEW", sq % 2)])
                        P.op("sp", lambda e, sq=sq, snew=snew: e.dma_start(
                            out=sss[sq].rearrange("(a b) p n -> (b p) a n", b=2), in_=snew), reads=[KM("SNEW", sq % 2)], dma=True)
                P.op("dve", lambda e, byx=byx: e.tensor_tensor(
                    out=A2[0:r, :].rearrange("p (h q) -> p h q", q=64), in0=PS[byx][0:r, :].rearrange("p (h q) -> p h q", q=64),
                    in1=ECUM[0:r].unsqueeze(2).to_broadcast([r, 8, 64]), op=ALU.mult),
                    reads=[("ps", byx), KM("ECUM")], writes=[KM("A2")])
                P.op("dve", lambda e: e.tensor_tensor(out=A2[0:r, :], in0=A2[0:r, :], in1=A1[0:r, :], op=ALU.add),
                     reads=[KM("A2"), KM("A1", pp)], writes=[KM("A2")])
                P.op("dve", lambda e, x3=x3: e.tensor_tensor(out=A1[0:r, :].rearrange("p (h q) -> p h q", q=64), in0=x3,
                                                             in1=DROW[0:r].unsqueeze(2).to_broadcast([r, 8, 64]), op=ALU.mult),
                     reads=[KM("XTOK", pp), ("cols2",)], writes=[KM("A1", pp)])
                P.op("dve", lambda e: e.tensor_tensor(out=A2[0:r, :], in0=A2[0:r, :], in1=A1[0:r, :], op=ALU.add),
                     reads=[KM("A2"), KM("A1", pp)], writes=[KM("A2")])
                P.op("dve", lambda e, ti=ti: e.tensor_tensor(out=A2[0:r, :], in0=A2[0:r, :], in1=ZS[0:r, ti, :], op=ALU.mult),
                     reads=[KM("A2"), KM("ZS")], writes=[KM("A2")])
                for g in range(2):
                    P.op("act", lambda e, g=g: e.activation(out=A1[0:r, g * 256:(g + 1) * 256], in_=A2[0:r, g * 256:(g + 1) * 256],
                                                            func=AF.Square, accum_out=SSQ[0:r, g:g + 1]),
                         reads=[KM("A2")], writes=[KM("A1", pp), KM("SSQ")])
                P.op("dve", lambda e: e.tensor_scalar(out=RSQ[0:r, 0:2], in0=SSQ[0:r, 0:2], scalar1=1.0 / 256, scalar2=EPS,
                                                      op0=ALU.mult, op1=ALU.add), reads=[KM("SSQ")], writes=[KM("RSQ")])
                P.op("act", lambda e: e.activation(out=RSQ[0:r, 0:2], in_=RSQ[0:r, 0:2], func=AF.Ln),
                     reads=[KM("RSQ")], writes=[KM("RSQ")])
                P.op("act", lambda e: e.activation(out=RSQ[0:r, 0:2], in_=RSQ[0:r, 0:2], func=AF.Exp, scale=-0.5),
                     reads=[KM("RSQ")], writes=[KM("RSQ")])
                for g in range(2):
                    P.op("dve", lambda e, g=g: e.tensor_scalar(out=YN[0:r, g * 256:(g + 1) * 256], in0=A2[0:r, g * 256:(g + 1) * 256],
                                                               scalar1=RSQ[0:r, g:g + 1], scalar2=None, op0=ALU.mult),
                         reads=[KM("A2"), KM("RSQ")], writes=[KWM("YN")])
                b = nb()
                pvy = PS[b][:].bitcast(BF16).rearrange("p (c t) -> p c t", c=8)
                for c in range(4):
                    P.op("pe", lambda e, c=c, pvy=pvy: e.transpose(out=pvy[:, c, 0:r], in_=YN[0:r, c * 128:(c + 1) * 128],
                                                                   identity=ident[0:r, 0:r]),
                         reads=[KWM("YN"), ("ident",)], writes=[("ps", b)])
                for c in range(4):
                    P.op("act", lambda e, c=c, pvy=pvy, c0=c0: e.activation(out=OAT[:, 4 + c, c0:c0 + r], in_=pvy[:, c, 0:r],
                                                                           func=AF.Copy, scale=SNW[:, c:c + 1]),
                         reads=[("ps", b), ("cols2",)], writes=[KW("OAT", 4 + c)])
            _tile(0, 0)
            for ti in range(ntile):
                if ti + 1 < ntile:
                    _tile(ti + 1, 0)
                _tile(ti, 1)
            if si == 3:
                b = nb()
                for a in range(4):
                    P.op("pe", lambda e, a=a, b=b: e.transpose(out=PS[b][:, a * 128:(a + 1) * 128], in_=MS[:, a * 128:(a + 1) * 128],
                                                               identity=identF), reads=[KM("MS"), ("cst",)], writes=[("ps", b)])
                P.op("act", lambda e, b=b: e.activation(out=A1[:, :], in_=PS[b][:, :], func=AF.Copy),
                     reads=[("ps", b)], writes=[KM("A1")])
                P.op("sp", lambda e: e.dma_start(out=ssp.rearrange("(a b) p n -> (b p) a n", b=2),
                                                 in_=A1[:, :].rearrange("p (a n) -> p a n", a=4)), reads=[KM("A1")], dma=True)
            barrier()

        def hgrn_mixer(norm_gi, next_gi):
            pend = list(range(len(STILES)))

            def norm_upto(idx):
                while pend and pend[0] <= idx:
                    gi_ = pend.pop(0)
                    norm_group(norm_gi, *STILES[gi_])
            mamba_setup()
            barrier()
            P.op("dve", lambda e: e.memset(SST[:, :, :], 0.0), writes=[K("SST")])
            P.op("dve", lambda e: e.memset(SBF[:, :, :], 0.0), writes=[KW("SBF")])
            for si, (t0, n) in enumerate(STILES):
                norm_upto(si + 1)
                if si == 4:
                    barrier()
                for h in range(4):
                    hgrn_head(si, t0, n, h, si == 4)
                mamba_tile_group(si, t0, n)
                out_proj(ab_w_out[0], t0, n, 8)
                norm_group(next_gi, t0, n)
                if si == 3:
                    for h in range(4):
                        P.op("sp", lambda e, h=h: e.dma_start(out=hgp[h], in_=SST[:, h, 0:128]),
                             reads=[K("SST", 0, h)], dma=True)
            barrier()

        def gla_head(si, t0, n, h, sample):
            W = gla_w_in[0]
            for vc in range(2):
                b2 = proj_fm(W, 2048 + h * 256 + vc * 128, t0, n)
                evac(AF.Silu, (GTa, GTb)[vc][:, 0:n], K("GT", vc), b2, n)
            b = proj_fm(W, h * 128, t0, n)
            P.op("dve", lambda e, b=b: e.tensor_scalar(out=QV[:, 0:n], in0=PS[b][:, 0:n], scalar1=float(128 ** -0.5),
                                                       scalar2=None, op0=ALU.mult),
                 reads=[("ps", b)], writes=[K("QV")])
            b = proj_fm(W, 512 + h * 128, t0, n)
            evac(AF.Copy, KV[:, 0:n], K("KV"), b, n)
            b = nb()
            P.op("pe", lambda e, b=b: e.matmul(PS[b][:, 0:n], lhsT=WGK[:, h * 128:(h + 1) * 128],
                                               rhs=GKL[:, 0:n], start=True, stop=True),
                 reads=[KW("WGK"), KW("GKL")], writes=[("ps", b)])
            P.op("dve", lambda e, b=b: e.tensor_scalar(out=T1[:, 0:n], in0=PS[b][:, 0:n], scalar1=cols[:, 16 + h:17 + h],
                                                       scalar2=None, op0=ALU.subtract),
                 reads=[("ps", b), ("cols",)], writes=[K("T1")])
            P.op("act", lambda e: e.activation(out=T1[:, 0:n], in_=T1[:, 0:n], func=AF.Exp, scale=-1.0),
                 reads=[K("T1")], writes=[K("T1")])
            P.op("dve", lambda e: e.tensor_scalar(out=T1[:, 0:n], in0=T1[:, 0:n], scalar1=1.0, scalar2=None, op0=ALU.add),
                 reads=[K("T1")], writes=[K("T1")])
            P.op("act", lambda e: e.activation(out=T1[:, 0:n], in_=T1[:, 0:n], func=AF.Ln),
                 reads=[K("T1")], writes=[K("T1")])
            P.op("dve", lambda e: e.tensor_scalar(out=LF[:, 0:n], in0=T1[:, 0:n], scalar1=-1.0 / 16.0,
                                                  scalar2=None, op0=ALU.mult),
                 reads=[K("T1")], writes=[K("LF")])
            for vc in range(2):
                proj_tm(W, 1024 + h * 256 + vc * 128, t0, n,
                        lambda ti, r, vc=vc: VT[0:r, ti, vc * 128:(vc + 1) * 128], KW("VT"))
            gla_core(si, t0, n, h, 256, sample, gs0, glp, gls, 20, 2 * h)

        def gla_gk(t0, n):
            b = proj_fm(gla_w_in[0], 3072, t0, n, ncols=16)
            evac(AF.Copy, GKL[:, 0:n], KW("GKL"), b, n, rows_=16)

        def gla_mixer(norm_gi, next_gi):
            pend = list(range(len(STILES)))

            def norm_upto(idx):
                while pend and pend[0] <= idx:
                    gi_ = pend.pop(0)
                    norm_group(norm_gi, *STILES[gi_])
            barrier()
            P.op("dve", lambda e: e.memset(SST[:, :, :], 0.0), writes=[K("SST")])
            P.op("dve", lambda e: e.memset(SBF[:, :, :], 0.0), writes=[KW("SBF")])
            P.op("pool", lambda e: e.dma_start(out=WGK, in_=gla_w_gk[0]), writes=[KW("WGK")], dma=True)
            for si, (t0, n) in enumerate(STILES):
                norm_upto(si + 1)
                if si == 4:
                    barrier()
                gla_gk(t0, n)
                for h in range(4):
                    gla_head(si, t0, n, h, si == 4)
                out_proj(gla_w_out[0], t0, n, 8)
                norm_group(next_gi, t0, n)
                if si == 3:
                    for h in range(4):
                        P.op("sp", lambda e, h=h: e.dma_start(out=glp[h], in_=SST[:, h, 0:256]),
                             reads=[K("SST", 0, h)], dma=True)
            barrier()

        for layer in range(2):
            if layer == 0:
                norm_to_hT(0)
            ffn(ffn_w_in[0][layer], ffn_w_out[0][layer])
            if layer == 0:
                hgrn_mixer(3 * layer + 1, 3 * layer + 2)
            else:
                gla_mixer(3 * layer + 1, 3 * layer + 2)
            ffn(ffn_w_in[1][layer], ffn_w_out[1][layer], tile_epilogue=(final_tile if layer == 1 else norm_epilogue(3)))

        with nc.allow_non_contiguous_dma(reason="small strided parameter/state transfers"):
            P.emit()
    return nc


_CACHE = {}


def kernel(**inputs):
    f32 = lambda a: np.ascontiguousarray(np.asarray(a, dtype=np.float32))
    x_prompt = f32(inputs["x_prompt"])
    x_sample = f32(inputs["x_sample"]).reshape(128 * 4, D)
    shared = {k: f32(inputs[k]) for k in ("norm_ffn1", "norm_mix", "norm_ffn2", "norm_final",
                                          "ffn1_w_in", "ffn1_w_out", "ffn2_w_in", "ffn2_w_out")}
    for k in ("ab_w_in", "ab_w_out", "hgrn_lb_logits", "hgrn_norm", "gla_w_in", "gla_w_gk", "gla_b_gk",
              "gla_norm", "gla_w_out", "ssm_conv_w", "ssm_conv_b", "ssm_dt_bias", "ssm_a_log", "ssm_d", "ssm_norm"):
        shared[k] = f32(inputs[k])
    shared["ident"] = np.eye(128, dtype=np.float32).astype(ml_dtypes.bfloat16)
    shared["ones_bf"] = np.ones((128, 128), dtype=np.float32).astype(ml_dtypes.bfloat16)
    jj, ii = np.meshgrid(np.arange(128), np.arange(128), indexing="ij")
    shared["maskP"] = ((jj // 64 == ii // 64) & (jj <= ii)).astype(np.float32)
    mS = np.zeros((128, 128), np.float32)
    mS[:64, :64] = ((jj // 4 == ii // 4) & (jj <= ii))[:64, :64]
    mS[:64, 64:80] = (np.arange(64)[:, None] // 4 == np.arange(16)[None, :])
    shared["maskS"] = mS
    shared["resetP"] = np.broadcast_to((np.arange(512) % 64 != 0).astype(np.float32), (128, 512)).copy()
    shared["resetS"] = np.broadcast_to((np.arange(512) % 4 != 0).astype(np.float32), (128, 512)).copy()
    cst = np.zeros((128, 384), np.float32)
    cst[:, 0:128] = np.eye(128)
    cst[:, 128:256] = 1.0
    cst[:64, 256:320] = (np.arange(64)[:, None] // 4 == np.arange(64)[None, :] // 4)
    cst[:64, 320:336] = (np.arange(64)[:, None] == 4 * np.arange(16)[None, :] + 3)
    cst[:64, 336] = 1.0
    cst[64:, 337] = 1.0
    shared["cst"] = cst
    st_s = f32(inputs["state_ssm"])[0]
    st_c = f32(inputs["state_conv"])[0]
    st_h = f32(inputs["state_hgrn"])[0]
    st_g = f32(inputs["state_gla"])[0]
    if "nc" not in _CACHE:
        _CACHE["nc"] = build_program()
    nc = _CACHE["nc"]
    in_maps = []
    for c in range(N_CORES):
        m = dict(shared)
        m["xp"] = x_prompt[c]
        m["xs"] = x_sample[c * 64:(c + 1) * 64]
        m["hs0"] = st_h[c * 16:(c + 1) * 16]
        m["gs0"] = st_g[c * 16:(c + 1) * 16]
        m["ss0"] = st_s[c * 16:(c + 1) * 16]
        m["cs0"] = st_c[c * 16:(c + 1) * 16]
        in_maps.append(m)
    res = run_bass_kernel_spmd(nc, in_maps, core_ids=list(range(N_CORES)))
    outs = res.results
    y_prompt = np.stack([outs[c]["yp"] for c in range(N_CORES)], axis=0)
    y_sample = np.concatenate([outs[c]["ys"] for c in range(N_CORES)], axis=0).reshape(128, 4, D)
    cat = lambda k: np.concatenate([outs[c][k] for c in range(N_CORES)], axis=0)
    stk = lambda k: np.stack([outs[c][k] for c in range(N_CORES)], axis=0)
    hgrn_p = stk("hgp")[None]
    hgrn_s = cat("hgs")[None]
    gla_p = stk("glp")[None]
    gla_s = cat("gls")[None]
    z = lambda *sh: np.zeros(sh, np.float32)
    return (y_prompt, y_sample, hgrn_p, hgrn_s, stk("ssp")[None], cat("sss")[None],
            stk("cvp")[None], cat("cvs")[None], gla_p, gla_s)
```

```python
import contextlib
import numpy as np
import ml_dtypes
import concourse.bass as bass
import concourse.mybir as mybir
from concourse.bass_utils import run_bass_kernel_spmd

F32 = mybir.dt.float32
BF16 = mybir.dt.bfloat16
AF = mybir.ActivationFunctionType
ALU = mybir.AluOpType

N_CORES = 8
D = 1024
DFF = 2816
NFC = 22
SEQ = 2048
NT = 17
TTOK = 2112
EPS = 1e-6
STILES = [(0, 512), (512, 512), (1024, 512), (1536, 512), (2048, 64)]
FSTILES = [(0, 448), (448, 448), (896, 448), (1344, 448), (1792, 320)]

ENGS = ("pe", "act", "dve", "pool", "sp")
NLANES = {"sp": 12, "pool": 12, "act": 6}

DEBUG = {"mixers": True, "strict": True, "hgrn": True, "gla": True}


class _Op:
    __slots__ = ("eng", "fn", "waits", "signal", "ticket", "dma", "lane", "lane_ticket", "idx")


class Prog:
    def __init__(self, nc, strict_same_engine=True):
        self.nc = nc
        self.ops = {e: [] for e in ENGS}
        self.last_w = {}
        self.readers = {}
        self.children = {}
        self.dma_count = {"sp": 0, "pool": 0, "act": 0}
        self.strict = strict_same_engine

    def _conflicts(self, key):
        out = [key[:i] for i in range(1, len(key) + 1)]
        out.extend(self.children.get(key, ()))
        return out

    def _register(self, key):
        for i in range(1, len(key)):
            self.children.setdefault(key[:i], set()).add(key)

    def op(self, eng, fn, reads=(), writes=(), dma=False):
        o = _Op()
        o.eng, o.fn, o.dma, o.signal, o.ticket = eng, fn, dma, False, None
        o.idx = len(self.ops[eng])
        deps = []
        for k in reads:
            for c in self._conflicts(k):
                t = self.last_w.get(c)
                if t is not None:
                    deps.append((t, "raw"))
        for k in writes:
            for c in self._conflicts(k):
                t = self.last_w.get(c)
                if t is not None:
                    deps.append((t, "waw"))
                for t in self.readers.get(c, {}).values():
                    deps.append((t, "war"))
        if dma:
            n = self.dma_count[eng]
            self.dma_count[eng] = n + 1
            nl = NLANES[eng]
            o.lane = n % nl
            o.lane_ticket = 16 * (n // nl + 1)
            tok = ("d", eng, o.lane, o.lane_ticket, o.idx)
            if n >= nl:
                deps.append((("d", eng, o.lane, o.lane_ticket - 16, -1), "lane"))
        else:
            tok = ("c", eng, o.idx)
        o.waits = []
        for t, kind in deps:
            if t[0] == "c" and t[1] == eng and not dma:
                if eng == "pe" or not self.strict:
                    continue
            o.waits.append(t)
        for k in writes:
            self._register(k)
            self.last_w[k] = tok
            self.readers[k] = {}
            for ch in list(self.children.get(k, ())):
                self.last_w.pop(ch, None)
                self.readers.pop(ch, None)
        for k in reads:
            self._register(k)
            src = (tok[0], tok[1], tok[2] if tok[0] == "d" else 0)
            self.readers.setdefault(k, {})[src] = tok
        self.ops[eng].append(o)
        return o

    def emit(self, final_wait_eng="sp"):
        nc = self.nc
        for e in ENGS:
            for o in self.ops[e]:
                for t in o.waits:
                    if t[0] == "c":
                        self.ops[t[1]][t[2]].signal = True
        for e in ENGS:
            n = 0
            for o in self.ops[e]:
                if o.signal and not o.dma:
                    n += 1
                    o.ticket = n
        with contextlib.ExitStack() as st:
            csem = {e: st.enter_context(nc.semaphore("c_" + e)) for e in ENGS}
            lsem = {}
            for q, nl in NLANES.items():
                for l in range(nl):
                    lsem[(q, l)] = st.enter_context(nc.semaphore("l_%s_%d" % (q, l)))
            block = st.enter_context(nc.Block())
            engobj = {"pe": "tensor", "act": "scalar", "dve": "vector", "pool": "gpsimd", "sp": "sync"}

            def emit_engine(e, eng):
                seen = {}
                for o in self.ops[e]:
                    need = {}
                    for t in o.waits:
                        if t[0] == "c":
                            key = ("c", t[1])
                            val = self.ops[t[1]][t[2]].ticket
                        else:
                            key = ("d", t[1], t[2])
                            val = t[3]
                        if seen.get(key, 0) >= val:
                            continue
                        if need.get(key, 0) < val:
                            need[key] = val
                    for key, val in need.items():
                        seen[key] = val
                        sem = csem[key[1]] if key[0] == "c" else lsem[(key[1], key[2])]
                        eng.wait_ge(sem, val)
                    inst = o.fn(eng)
                    if o.dma:
                        inst.then_inc(lsem[(e, o.lane)], 16)
                    elif o.signal:
                        inst.then_inc(csem[e], 1)
                if e == final_wait_eng:
                    for q, nl in NLANES.items():
                        n = self.dma_count[q]
                        for l in range(nl):
                            cnt = (n - l + nl - 1) // nl if n > l else 0
                            if cnt > 0:
                                eng.wait_ge(lsem[(q, l)], 16 * cnt)

            for e in ENGS:
                if not self.ops[e] and e != final_wait_eng:
                    continue

                def body(eng, e=e):
                    emit_engine(e, eng)
                getattr(block, engobj[e])(body)


def build_program():
    nc = bass.Bass("TRN2", target_bir_lowering=False)

    def din(name, shape, dt=F32):
        return nc.dram_tensor(name, list(shape), dt, kind="ExternalInput").ap()

    def dout(name, shape):
        return nc.dram_tensor(name, list(shape), F32, kind="ExternalOutput").ap()

    xp = din("xp", [SEQ, D])
    xs = din("xs", [64, D])
    norm_ffn1 = din("norm_ffn1", [2, D])
    norm_mix = din("norm_mix", [2, D])
    norm_ffn2 = din("norm_ffn2", [2, D])
    norm_final = din("norm_final", [D])
    ffn_w_in = [din("ffn1_w_in", [2, D, 2 * DFF]), din("ffn2_w_in", [2, D, 2 * DFF])]
    ffn_w_out = [din("ffn1_w_out", [2, DFF, D]), din("ffn2_w_out", [2, DFF, D])]
    ident_d = din("ident", [128, 128], BF16)
    ones_d = din("ones_bf", [128, 128], BF16)
    maskP_d = din("maskP", [128, 128])
    maskS_d = din("maskS", [128, 128])
    resetP_d = din("resetP", [128, 512])
    resetS_d = din("resetS", [128, 512])
    ab_w_in = din("ab_w_in", [1, D, 3592])
    ab_w_out = din("ab_w_out", [1, D, D])
    lb_logits = din("hgrn_lb_logits", [2, 512])
    hgrn_norm = din("hgrn_norm", [1, 128])
    gla_w_in = din("gla_w_in", [1, D, 3088])
    gla_w_gk = din("gla_w_gk", [1, 16, 512])
    gla_b_gk = din("gla_b_gk", [1, 512])
    gla_norm = din("gla_norm", [1, 256])
    gla_w_out = din("gla_w_out", [1, D, D])
    hs0 = din("hs0", [16, 4, 128, 128])
    gs0 = din("gs0", [16, 4, 128, 256])
    hgp = dout("hgp", [4, 128, 128])
    hgs = dout("hgs", [16, 4, 128, 128])
    glp = dout("glp", [4, 128, 256])
    gls = dout("gls", [16, 4, 128, 256])
    cst_d = din("cst", [128, 384])
    conv_w = din("ssm_conv_w", [1, 4, 1024])
    conv_b = din("ssm_conv_b", [1, 1024])
    dt_bias = din("ssm_dt_bias", [1, 8])
    a_log = din("ssm_a_log", [1, 8])
    ssm_d = din("ssm_d", [1, 8])
    ssm_norm = din("ssm_norm", [1, 512])
    ss0 = din("ss0", [16, 8, 64, 128])
    cs0 = din("cs0", [16, 3, 1024])
    ssp = dout("ssp", [8, 64, 128])
    sss = dout("sss", [16, 8, 64, 128])
    cvp = dout("cvp", [3, 1024])
    cvs = dout("cvs", [16, 3, 1024])

    yp = dout("yp", [SEQ, D])
    ys = dout("ys", [64, D])

    with contextlib.ExitStack() as st:
        def sb(name, shape, dt=F32):
            return st.enter_context(nc.sbuf_tensor(name, list(shape), dt))

        X = sb("X", [128, NT, D])
        hT = sb("hT", [128, 8, TTOK], BF16)
        aT = sb("aT", [128, 11, TTOK], BF16)
        Wo = sb("Wo", [128, 11, D], BF16)
        Wr = [sb("Wr%d" % i, [128, 8, 128], BF16) for i in range(8)]
        sg = [sb("sg%d" % i, [128, 512]) for i in range(2)]
        hn = [sb("hn%d" % i, [128, D], BF16) for i in range(2)]
        junk = sb("junk", [128, D], BF16)
        ident = sb("ident_sb", [128, 128], BF16)
        gcol = sb("gcol", [128, 6, 8])
        gfin = sb("gfin", [128, D])
        ss = sb("ss", [128, 32])
        rs = sb("rs", [128, 32])
        ones_bf = sb("ones_sb", [128, 128], BF16)
        maskP = sb("maskP_sb", [128, 128])
        maskS = sb("maskS_sb", [128, 128])
        resetP = sb("resetP_sb", [128, 512])
        resetS = sb("resetS_sb", [128, 512])
        cols = sb("cols", [128, 64])
        dummy = sb("dummy_sb", [128, 8])
        CST = sb("cst_sb", [128, 384])
        cols2 = sb("cols2", [128, 64])
        TAIL = sb("tail_sb", [128, 8, 4])
        PS = [st.enter_context(nc.psum_tensor("ps%d" % i, [128, 512], F32)) for i in range(8)]

        P = Prog(nc, strict_same_engine=DEBUG["strict"])

        P.op("act", lambda e: e.dma_start(out=ident[:], in_=ident_d), writes=[("ident",)], dma=True)
        gains = [norm_ffn1[0], norm_mix[0], norm_ffn2[0], norm_ffn1[1], norm_mix[1], norm_ffn2[1]]
        for i, g in enumerate(gains):
            P.op("act", lambda e, i=i, g=g: e.dma_start(out=gcol[:, i, :], in_=g.rearrange("(c p) -> p c", p=128)),
                 writes=[("gcol", i)], dma=True)
        P.op("act", lambda e: e.dma_start(out=gfin[:], in_=norm_final.partition_broadcast(128)),
             writes=[("gfin",)], dma=True)
        for tt in range(16):
            P.op("sp", lambda e, tt=tt: e.dma_start(out=X[:, tt, :], in_=xp[tt * 128:(tt + 1) * 128, :]),
                 writes=[("X", tt)], dma=True)
        P.op("sp", lambda e: e.dma_start(out=X[0:64, 16, :], in_=xs), writes=[("X", 16)], dma=True)

        def rows(tt):
            return 64 if tt == 16 else 128

        nrm_ctr = [0]

        def norm_to_hT(gi):
            for (t0, n) in STILES:
                norm_group(gi, t0, n)

        def norm_group(gi, t0, n):
            if True:
                ntile = max(1, n // 128)
                r = min(128, n)
                pvs = [PS[bk][:].bitcast(BF16) for bk in range(4)]
                for ti in range(ntile):
                    tt = t0 // 128 + ti
                    k = nrm_ctr[0] % 32
                    nrm_ctr[0] += 1
                    P.op("act", lambda e, tt=tt, k=k, r=r: e.activation(
                        out=junk[0:r, :], in_=X[0:r, tt, :], func=AF.Square, accum_out=ss[0:r, k:k + 1]),
                        reads=[("X", tt)], writes=[("junk",), ("ss", k)])
                    P.op("act", lambda e, k=k, r=r: e.activation(
                        out=rs[0:r, k:k + 1], in_=ss[0:r, k:k + 1], func=AF.Sqrt, scale=1.0 / D, bias=EPS),
                        reads=[("ss", k)], writes=[("rs", k)])
                    P.op("dve", lambda e, k=k, r=r: e.reciprocal(out=rs[0:r, k:k + 1], in_=rs[0:r, k:k + 1]),
                         reads=[("rs", k)], writes=[("rs", k)])
                    hb = tt % 2
                    P.op("dve", lambda e, tt=tt, k=k, hb=hb, r=r: e.tensor_scalar(
                        out=hn[hb][0:r, :], in0=X[0:r, tt, :], scalar1=rs[0:r, k:k + 1], scalar2=None, op0=ALU.mult),
                        reads=[("X", tt), ("rs", k)], writes=[("hn", hb)])
                    for c in range(8):
                        o0 = (c % 2) * 512 + ti * 128
                        P.op("pe", lambda e, c=c, hb=hb, o0=o0, r=r: e.transpose(
                            out=pvs[c // 2][:, o0:o0 + r], in_=hn[hb][0:r, c * 128:(c + 1) * 128], identity=ident[0:r, 0:r]),
                            reads=[("hn", hb), ("ident",)], writes=[("ps", c // 2)])
                tkeys = [("hT", t0 // 128 + ti) for ti in range(ntile)]
                for c in range(8):
                    src = pvs[c // 2][:, (c % 2) * 512:(c % 2) * 512 + n]
                    if c % 2 == 0:
                        P.op("act", lambda e, c=c, src=src, t0=t0, n=n: e.activation(
                            out=hT[:, c, t0:t0 + n], in_=src, func=AF.Copy, scale=gcol[:, gi, c:c + 1]),
                            reads=[("ps", c // 2), ("gcol", gi)], writes=tkeys)
                    else:
                        P.op("dve", lambda e, c=c, src=src, t0=t0, n=n: e.tensor_scalar(
                            out=hT[:, c, t0:t0 + n], in0=src, scalar1=gcol[:, gi, c:c + 1], scalar2=None, op0=ALU.mult),
                            reads=[("ps", c // 2), ("gcol", gi)], writes=tkeys)

        def final_tile(tt):
            r = rows(tt)
            k = nrm_ctr[0] % 32
            nrm_ctr[0] += 1
            P.op("act", lambda e, tt=tt, r=r, k=k: e.activation(
                out=junk[0:r, :], in_=X[0:r, tt, :], func=AF.Square, accum_out=ss[0:r, k:k + 1]),
                reads=[("X", tt)], writes=[("junk",), ("ss", k)])
            P.op("act", lambda e, r=r, k=k: e.activation(
                out=rs[0:r, k:k + 1], in_=ss[0:r, k:k + 1], func=AF.Sqrt, scale=1.0 / D, bias=EPS),
                reads=[("ss", k)], writes=[("rs", k)])
            P.op("dve", lambda e, r=r, k=k: e.reciprocal(out=rs[0:r, k:k + 1], in_=rs[0:r, k:k + 1]),
                 reads=[("rs", k)], writes=[("rs", k)])
            P.op("dve", lambda e, tt=tt, r=r, k=k: e.scalar_tensor_tensor(
                out=X[0:r, tt, :], in0=X[0:r, tt, :], scalar=rs[0:r, k:k + 1], op0=ALU.mult,
                in1=gfin[0:r, :], op1=ALU.mult),
                reads=[("X", tt), ("rs", k), ("gfin",)], writes=[("X", tt)])
            if tt < 16:
                P.op("sp", lambda e, tt=tt: e.dma_start(out=yp[tt * 128:(tt + 1) * 128, :], in_=X[:, tt, :]),
                     reads=[("X", tt)], dma=True)
            else:
                P.op("sp", lambda e: e.dma_start(out=ys, in_=X[0:64, 16, :]), reads=[("X", 16)], dma=True)


        wslot = [0]
        psrot = [0]

        def wload(wmat, col0, ncols=128):
            s = wslot[0] % 8
            wslot[0] += 1
            P.op("pool", lambda e, s=s: e.dma_start(
                out=Wr[s][:, :, 0:ncols], in_=wmat[:, col0:col0 + ncols].rearrange("(c p) n -> p c n", p=128)),
                writes=[("Wr", s)], dma=True)
            return s

        def ffn(w_in, w_out, norm_gi=None, tile_epilogue=None):
            pend = list(range(len(STILES))) if norm_gi is not None else []

            def norm_upto(idx):
                while pend and pend[0] <= idx:
                    gi_ = pend.pop(0)
                    norm_group(norm_gi, *STILES[gi_])
            for g in range(2):
                j0 = g * 11
                for jl in range(11):
                    j = j0 + jl
                    if jl == 4:
                        P.op("pool", lambda e, j0=j0: e.dma_start(
                            out=Wo[:], in_=w_out[j0 * 128:(j0 + 11) * 128, :].rearrange("(j p) d -> p j d", p=128)),
                            writes=[("Wo",)], dma=True)
                    sG = wload(w_in, j * 128)
                    sU = wload(w_in, DFF + j * 128)
                    for (t0, n) in FSTILES:
                        norm_upto(min(4, (t0 + n - 1) // 512) + 1)
                        bg = 4 + (psrot[0] % 2) * 2
                        bu = bg + 1
                        sgi = psrot[0] % 2
                        psrot[0] += 1
                        tiles = [("hT", tt) for tt in range(t0 // 128, (t0 + n - 1) // 128 + 1)]
                        for c in range(8):
                            P.op("pe", lambda e, c=c, s=sG, t0=t0, n=n, bg=bg: e.matmul(
                                PS[bg][:, 0:n], lhsT=Wr[s][:, c, :], rhs=hT[:, c, t0:t0 + n],
                                start=(c == 0), stop=(c == 7)),
                                reads=[("Wr", sG)] + tiles, writes=[("ps", bg)])
                        for c in range(8):
                            P.op("pe", lambda e, c=c, s=sU, t0=t0, n=n, bu=bu: e.matmul(
                                PS[bu][:, 0:n], lhsT=Wr[s][:, c, :], rhs=hT[:, c, t0:t0 + n],
                                start=(c == 0), stop=(c == 7)),
                                reads=[("Wr", sU)] + tiles, writes=[("ps", bu)])
                        P.op("act", lambda e, n=n, bg=bg, sgi=sgi: e.activation(
                            out=sg[sgi][:, 0:n], in_=PS[bg][:, 0:n], func=AF.Silu),
                            reads=[("ps", bg)], writes=[("sg", sgi)])
                        P.op("dve", lambda e, n=n, bu=bu, sgi=sgi, jl=jl, t0=t0: e.tensor_tensor(
                            out=aT[:, jl, t0:t0 + n], in0=sg[sgi][:, 0:n], in1=PS[bu][:, 0:n], op=ALU.mult),
                            reads=[("sg", sgi), ("ps", bu)], writes=[("aT", jl, t0)])
                for tt in range(NT):
                    r = rows(tt)
                    akeys = [(f0) for (f0, fn) in FSTILES if f0 < tt * 128 + r and f0 + fn > tt * 128]
                    for dh in range(2):
                        bo = dh
                        bo = (tt % 2) * 2 + dh
                        for jl in range(11):
                            P.op("pe", lambda e, jl=jl, tt=tt, r=r, dh=dh, bo=bo: e.matmul(
                                PS[bo][0:r, :], lhsT=aT[:, jl, tt * 128:tt * 128 + r],
                                rhs=Wo[:, jl, dh * 512:(dh + 1) * 512], start=(jl == 0), stop=(jl == 10)),
                                reads=[("aT", jl, f0) for f0 in akeys] + [("Wo",)], writes=[("ps", bo)])
                        P.op("dve", lambda e, tt=tt, r=r, dh=dh, bo=bo: e.scalar_tensor_tensor(
                            out=X[0:r, tt, dh * 512:(dh + 1) * 512], in0=PS[bo][0:r, :], scalar=0.5, op0=ALU.mult,
                            in1=X[0:r, tt, dh * 512:(dh + 1) * 512], op1=ALU.add),
                            reads=[("ps", bo), ("X", tt)], writes=[("X", tt)])
                    if g == 1 and tile_epilogue is not None:
                        tile_epilogue(tt)

        for nm, t_sb, t_d in (("ones", ones_bf, ones_d), ("maskP", maskP, maskP_d), ("maskS", maskS, maskS_d),
                              ("resetP", resetP, resetP_d), ("resetS", resetS, resetS_d)):
            P.op("sp", lambda e, t_sb=t_sb, t_d=t_d: e.dma_start(out=t_sb[:], in_=t_d), writes=[(nm,)], dma=True)
        P.op("sp", lambda e: e.dma_start(out=cols[:, 0:4], in_=lb_logits[0].rearrange("(h p) -> p h", p=128)),
             writes=[("cols", "lb")], dma=True)
        P.op("sp", lambda e: e.dma_start(out=cols[:, 4:8], in_=lb_logits[1].rearrange("(h p) -> p h", p=128)),
             writes=[("cols", "l1")], dma=True)
        P.op("sp", lambda e: e.dma_start(out=cols[:, 12:13], in_=hgrn_norm[0].rearrange("(h p) -> p h", p=128)),
             writes=[("cols", "hn")], dma=True)
        P.op("sp", lambda e: e.dma_start(out=cols[:, 16:20], in_=gla_b_gk[0].rearrange("(h p) -> p h", p=128)),
             writes=[("cols", "bgk")], dma=True)
        P.op("sp", lambda e: e.dma_start(out=cols[:, 20:22], in_=gla_norm[0].rearrange("(h p) -> p h", p=128)),
             writes=[("cols", "gn")], dma=True)
        P.op("dve", lambda e: e.tensor_tensor(out=cols[:, 0:4], in0=cols[:, 0:4], in1=cols[:, 4:8], op=ALU.subtract),
             reads=[("cols", "lb"), ("cols", "l1")], writes=[("cols", "lb")])
        P.op("act", lambda e: e.activation(out=cols[:, 0:4], in_=cols[:, 0:4], func=AF.Sigmoid),
             reads=[("cols", "lb")], writes=[("cols", "lb")])
        P.op("dve", lambda e: e.tensor_scalar(out=cols[:, 8:12], in0=cols[:, 0:4], scalar1=-1.0, scalar2=1.0,
                                              op0=ALU.mult, op1=ALU.add),
             reads=[("cols", "lb")], writes=[("cols", "oml")])
        P.op("dve", lambda e: e.tensor_scalar(out=cols[:, 16:20], in0=cols[:, 16:20], scalar1=-1.0, scalar2=None,
                                              op0=ALU.mult),
             reads=[("cols", "bgk")], writes=[("cols", "bgk")])

        aTf = aT[:].rearrange("p a b -> p (a b)").bitcast(F32)
        Wob = Wo[:].rearrange("p a b -> p (a b)")

        def fA(i, w=512):
            return aTf[:, i * 512:i * 512 + w]
        QV, KV, LF, GG, T1, T2, O32a, O32b, GTa, GTb, RSTD = [fA(i) for i in range(11)]
        SST = aTf[:, 5632:6656].rearrange("p (h v) -> p h v", h=4)
        S0F = aTf[:, 6656:10752].rearrange("p (s v) -> p s v", s=16)
        QD = Wob[:, 0:512]
        KI = Wob[:, 512:1024]
        KE = Wob[:, 1024:1536]
        STm = [Wob[:, 1536:1664], Wob[:, 1664:1792], Wob[:, 11008:11136], Wob[:, 11136:11264]]
        SBFp = [Wob[:, 7424:7680], Wob[:, 7680:7936]]
        SST2 = [SST, aTf[:, 6656:7680].rearrange("p (h v) -> p h v", h=4)]
        VT = Wob[:, 1792:2816].rearrange("p (t v) -> p t v", t=4)
        KEt = Wob[:, 2816:3328].rearrange("p (t k) -> p t k", t=4)
        OAT = Wob[:, 3328:7424].rearrange("p (c n) -> p c n", c=8)
        SBF = Wob[:, 7424:8448].rearrange("p (h v) -> p h v", h=4)
        S0C = [Wob[:, 8448:8704], Wob[:, 8704:8960]]
        GKL = Wob[0:16, 8960:9472]
        WGK = Wob[0:16, 9472:9984]
        VBLK = Wob[0:64, 9984:11008].rearrange("p (s v) -> p s v", s=4)
        bankrr = [0]

        def nb():
            b = bankrr[0] % 6
            bankrr[0] += 1
            return b

        def K(*a):
            return ("aT", "mx") + a

        def KW(*a):
            return ("Wo", "mx") + a

        def barrier():
            P.op("dve", lambda e: e.memset(dummy[:, 0:1], 0.0), writes=[("aT",), ("Wo",), ("dummy",)])

        def proj_fm(wmat, col0, t0, n, ncols=128):
            sW = wload(wmat, col0, ncols)
            b = nb()
            tiles = [("hT", tt) for tt in range(t0 // 128, t0 // 128 + max(1, n // 128))]
            for c in range(8):
                P.op("pe", lambda e, c=c, b=b, sW=sW: e.matmul(
                    PS[b][0:ncols, 0:n], lhsT=Wr[sW][:, c, 0:ncols], rhs=hT[:, c, t0:t0 + n],
                    start=(c == 0), stop=(c == 7)), reads=[("Wr", sW)] + tiles, writes=[("ps", b)])
            return b

        def proj_tm(wmat, col0, t0, n, dst, dkey, func=AF.Copy):
            sW = wload(wmat, col0, 128)
            for ti in range(max(1, n // 128)):
                r = min(128, n)
                tt = t0 // 128 + ti
                b = nb()
                for c in range(8):
                    P.op("pe", lambda e, c=c, b=b, sW=sW, tt=tt, r=r: e.matmul(
                        PS[b][0:r, 0:128], lhsT=hT[:, c, tt * 128:tt * 128 + r], rhs=Wr[sW][:, c, :],
                        start=(c == 0), stop=(c == 7)), reads=[("Wr", sW), ("hT", tt)], writes=[("ps", b)])
                P.op("act", lambda e, b=b, ti=ti, r=r: e.activation(out=dst(ti, r), in_=PS[b][0:r, 0:128], func=func),
                     reads=[("ps", b)], writes=[dkey])

        def gla_core(si, t0, n, h, V, sample, s0_d, sp_d, ss_d, normcol0, oc0):
            nvc = V // 128
            reset = resetS if sample else resetP
            C = 4 if sample else 64
            nch = n // C
            P.op("dve", lambda e: e.tensor_tensor_scan(out=GG[:, 0:n], data0=reset[:, 0:n], data1=LF[:, 0:n],
                                                       initial=0.0, op0=ALU.mult, op1=ALU.add),
                 reads=[K("LF"), ("resetS" if sample else "resetP",)], writes=[K("GG")])
            P.op("act", lambda e: e.activation(out=T1[:, 0:n], in_=GG[:, 0:n], func=AF.Exp),
                 reads=[K("GG")], writes=[K("T1")])
            P.op("dve", lambda e: e.tensor_tensor(out=QD[:, 0:n], in0=QV[:, 0:n], in1=T1[:, 0:n], op=ALU.mult),
                 reads=[K("QV"), K("T1")], writes=[KW("QD")])
            P.op("act", lambda e: e.activation(out=T2[:, 0:n], in_=GG[:, 0:n], func=AF.Exp, scale=-1.0),
                 reads=[K("GG")], writes=[K("T2")])
            P.op("dve", lambda e: e.tensor_tensor(out=KI[:, 0:n], in0=KV[:, 0:n], in1=T2[:, 0:n], op=ALU.mult),
                 reads=[K("KV"), K("T2")], writes=[KW("KI")])
            g3 = GG[:, 0:n].rearrange("p (c j) -> p c j", j=C)
            P.op("dve", lambda e: e.tensor_tensor(out=T2[:, 0:n].rearrange("p (c j) -> p c j", j=C),
                                                  in0=g3[:, :, C - 1:C].to_broadcast([128, nch, C]), in1=g3,
                                                  op=ALU.subtract),
                 reads=[K("GG")], writes=[K("T2")])
            P.op("act", lambda e: e.activation(out=T2[:, 0:n], in_=T2[:, 0:n], func=AF.Exp),
                 reads=[K("T2")], writes=[K("T2")])
            P.op("dve", lambda e: e.tensor_tensor(out=KE[:, 0:n], in0=KV[:, 0:n], in1=T2[:, 0:n], op=ALU.mult),
                 reads=[K("KV"), K("T2")], writes=[KW("KE")])
            ntile = max(1, n // 128)
            r = min(128, n)
            for ti in range(ntile):
                b = nb()
                pv = PS[b][:].bitcast(BF16)
                P.op("pe", lambda e, ti=ti, pv=pv: e.transpose(out=pv[0:r, 0:128], in_=KE[:, ti * 128:ti * 128 + r],
                                                               identity=ident[:, :]),
                     reads=[KW("KE"), ("ident",)], writes=[("ps", b)])
                P.op("act", lambda e, ti=ti, pv=pv: e.activation(out=KEt[0:r, ti, :], in_=pv[0:r, 0:128], func=AF.Copy),
                     reads=[("ps", b)], writes=[KW("KEt", ti)])
            mask = maskS if sample else maskP
            if sample:
                P.op("sp", lambda e: e.dma_start(out=S0F[:, :, 0:V], in_=s0_d[:, h].rearrange("s k v -> k s v")),
                     writes=[K("S0F")], dma=True)
            if False:
                for ti in range(ntile):
                    c0 = ti * 128
                    b = nb()
                    P.op("pe", lambda e, b=b, c0=c0: e.matmul(PS[b][:, 0:128], lhsT=KI[:, c0:c0 + 128], rhs=QD[:, c0:c0 + 128],
                                                              start=True, stop=True),
                         reads=[KW("KI"), KW("QD")], writes=[("ps", b)])
                    P.op("dve", lambda e, b=b, ti=ti: e.tensor_tensor(out=STm[ti][:, 0:128], in0=PS[b][:, 0:128],
                                                                     in1=maskP[:, :], op=ALU.mult),
                         reads=[("ps", b), ("maskP",)], writes=[KW("ST", ti)])
                per = 512 // V
                ub = {}
                for c in range(2 * ntile):
                    ti, cc = divmod(c, 2)
                    if c % per == 0:
                        bU = nb()
                    col = (c % per) * V
                    P.op("pe", lambda e, bU=bU, col=col, cc=cc, ti=ti, c=c: e.matmul(
                        PS[bU][:, col:col + V], lhsT=KEt[cc * 64:cc * 64 + 64, ti, :], rhs=VT[cc * 64:cc * 64 + 64, ti, 0:V],
                        start=(c % per == 0), stop=True, skip_group_check=True),
                        reads=[KW("KEt", ti), KW("VT", ti)], writes=[("ps", bU)])
                    ub[c] = (bU, col)
                P.op("act", lambda e: e.activation(out=SBFp[0][:, 0:V], in_=SST2[0][:, h, 0:V], func=AF.Copy),
                     reads=[K("SST", 0, h)], writes=[KW("SBFp", 0)])
                for c in range(2 * ntile):
                    ti, cc = divmod(c, 2)
                    q0 = c * 64
                    if cc == 0:
                        bo = nb()
                        for vc in range(nvc):
                            P.op("pe", lambda e, bo=bo, ti=ti, vc=vc: e.matmul(
                                PS[bo][:, vc * 128:vc * 128 + 128], lhsT=VT[:, ti, vc * 128:(vc + 1) * 128], rhs=STm[ti][:, 0:128],
                                start=(vc == 0), stop=False, skip_group_check=True),
                                reads=[KW("VT", ti), KW("ST", ti)], writes=[("ps", bo)])
                    for vc in range(nvc):
                        P.op("pe", lambda e, vc=vc, q0=q0, cc=cc, bo=bo, c=c: e.matmul(
                            PS[bo][:, vc * 128 + cc * 64:vc * 128 + cc * 64 + 64],
                            lhsT=SBFp[c % 2][:, vc * 128:(vc + 1) * 128], rhs=QD[:, q0:q0 + 64],
                            start=False, stop=True, skip_group_check=True),
                            reads=[KW("SBFp", c % 2), KW("QD")], writes=[("ps", bo)])
                    bU, col = ub[c]
                    P.op("dve", lambda e, bU=bU, col=col, q0=q0, c=c: e.scalar_tensor_tensor(
                        out=SST2[(c + 1) % 2][:, h, 0:V], in0=SST2[c % 2][:, h, 0:V], scalar=T1[:, q0 + 63:q0 + 64], op0=ALU.mult,
                        in1=PS[bU][:, col:col + V], op1=ALU.add),
                        reads=[K("SST", c % 2, h), K("T1"), ("ps", bU)], writes=[K("SST", (c + 1) % 2, h)])
                    if c < 2 * ntile - 1:
                        P.op("act", lambda e, c=c: e.activation(out=SBFp[(c + 1) % 2][:, 0:V], in_=SST2[(c + 1) % 2][:, h, 0:V],
                                                              func=AF.Copy),
                             reads=[K("SST", (c + 1) % 2, h)], writes=[KW("SBFp", (c + 1) % 2)])
                    if cc == 1:
                        c0 = ti * 128
                        for vc, o32 in zip(range(nvc), (O32a, O32b)):
                            P.op("act", lambda e, vc=vc, o32=o32, c0=c0, bo=bo: e.activation(
                                out=o32[:, c0:c0 + 128], in_=PS[bo][:, vc * 128:vc * 128 + 128], func=AF.Copy),
                                reads=[("ps", bo)], writes=[K("O32", vc, ti)])
            if not sample:
                for ti in range(ntile):
                    c0 = ti * 128
                    b = nb()
                    P.op("pe", lambda e, b=b, c0=c0: e.matmul(PS[b][:, 0:128], lhsT=KI[:, c0:c0 + 128], rhs=QD[:, c0:c0 + 128],
                                                              start=True, stop=True),
                         reads=[KW("KI"), KW("QD")], writes=[("ps", b)])
                    P.op("dve", lambda e, b=b, ti=ti: e.tensor_tensor(out=STm[ti][:, 0:128], in0=PS[b][:, 0:128],
                                                                     in1=maskP[:, :], op=ALU.mult),
                         reads=[("ps", b), ("maskP",)], writes=[KW("ST", ti)])
                per = 512 // V
                ub = {}
                ubank = {}
                for c in range(2 * ntile):
                    ti, cc = divmod(c, 2)
                    slot = (cc, ti // per)
                    if slot not in ubank:
                        ubank[slot] = nb()
                    bU = ubank[slot]
                    col = (ti % per) * V
                    P.op("pe", lambda e, bU=bU, col=col, cc=cc, ti=ti: e.matmul(
                        PS[bU][:, col:col + V], lhsT=KEt[cc * 64:cc * 64 + 64, ti, :], rhs=VT[cc * 64:cc * 64 + 64, ti, 0:V],
                        start=(ti % per == 0), stop=True, skip_group_check=True),
                        reads=[KW("KEt", ti), KW("VT", ti)], writes=[("ps", bU)])
                    ub[c] = (bU, col)
                P.op("act", lambda e: e.activation(out=SBFp[0][:, 0:V], in_=SST2[0][:, h, 0:V], func=AF.Copy),
                     reads=[K("SST", 0, h)], writes=[KW("SBFp", 0)])
                for c in range(2 * ntile):
                    ti, cc = divmod(c, 2)
                    q0 = c * 64
                    if cc == 0:
                        bo = nb()
                        for vc in range(nvc):
                            P.op("pe", lambda e, bo=bo, ti=ti, vc=vc: e.matmul(
                                PS[bo][:, vc * 128:vc * 128 + 128], lhsT=VT[:, ti, vc * 128:(vc + 1) * 128], rhs=STm[ti][:, 0:128],
                                start=(vc == 0), stop=False, skip_group_check=True),
                                reads=[KW("VT", ti), KW("ST", ti)], writes=[("ps", bo)])
                    for vc in range(nvc):
                        P.op("pe", lambda e, vc=vc, q0=q0, cc=cc, bo=bo, c=c: e.matmul(
                            PS[bo][:, vc * 128 + cc * 64:vc * 128 + cc * 64 + 64],
                            lhsT=SBFp[c % 2][:, vc * 128:(vc + 1) * 128], rhs=QD[:, q0:q0 + 64],
                            start=False, stop=True, skip_group_check=True),
                            reads=[KW("SBFp", c % 2), KW("QD")], writes=[("ps", bo)])
                    bU, col = ub[c]
                    P.op("dve", lambda e, bU=bU, col=col, q0=q0, c=c: e.scalar_tensor_tensor(
                        out=SST2[(c + 1) % 2][:, h, 0:V], in0=SST2[c % 2][:, h, 0:V], scalar=T1[:, q0 + 63:q0 + 64], op0=ALU.mult,
                        in1=PS[bU][:, col:col + V], op1=ALU.add),
                        reads=[K("SST", c % 2, h), K("T1"), ("ps", bU)], writes=[K("SST", (c + 1) % 2, h)])
                    if c < 2 * ntile - 1:
                        P.op("act", lambda e, c=c: e.activation(out=SBFp[(c + 1) % 2][:, 0:V], in_=SST2[(c + 1) % 2][:, h, 0:V],
                                                              func=AF.Copy),
                             reads=[K("SST", (c + 1) % 2, h)], writes=[KW("SBFp", (c + 1) % 2)])
                    if cc == 1:
                        c0 = ti * 128
                        for vc, o32 in zip(range(nvc), (O32a, O32b)):
                            P.op("act", lambda e, vc=vc, o32=o32, c0=c0, bo=bo: e.activation(
                                out=o32[:, c0:c0 + 128], in_=PS[bo][:, vc * 128:vc * 128 + 128], func=AF.Copy),
                                reads=[("ps", bo)], writes=[K("O32", vc, ti)])
            for ti in (range(ntile) if sample else ()):
                c0 = ti * 128
                b = nb()
                P.op("pe", lambda e, b=b, c0=c0: e.matmul(PS[b][0:r, 0:r], lhsT=KI[:, c0:c0 + r], rhs=QD[:, c0:c0 + r],
                                                          start=True, stop=True),
                     reads=[KW("KI"), KW("QD")], writes=[("ps", b)])
                sm = STm[ti % 2]
                P.op("dve", lambda e, b=b, sm=sm: e.tensor_tensor(out=sm[0:r, 0:r], in0=PS[b][0:r, 0:r],
                                                                 in1=mask[0:r, 0:r], op=ALU.mult),
                     reads=[("ps", b), ("maskS" if sample else "maskP",)], writes=[KW("ST", ti % 2)])
                bo = nb()
                for vc in range(nvc):
                    ov = PS[bo][:, vc * 128:vc * 128 + r]
                    P.op("pe", lambda e, ov=ov, ti=ti, vc=vc, sm=sm: e.matmul(
                        ov, lhsT=VT[0:r, ti, vc * 128:(vc + 1) * 128], rhs=sm[0:r, 0:r], start=(vc == 0), stop=False,
                        skip_group_check=True),
                        reads=[KW("VT", ti), KW("ST", ti % 2)], writes=[("ps", bo)])
                if not sample:
                    for cc in range(2):
                        q0 = c0 + cc * 64
                        for vc in range(nvc):
                            P.op("pe", lambda e, vc=vc, q0=q0, cc=cc, bo=bo: e.matmul(
                                PS[bo][:, vc * 128 + cc * 64:vc * 128 + cc * 64 + 64],
                                lhsT=SBF[:, h, vc * 128:(vc + 1) * 128], rhs=QD[:, q0:q0 + 64],
                                start=False, stop=True, skip_group_check=True),
                                reads=[KW("SBF", h), KW("QD")], writes=[("ps", bo)])
                        bs = nb()
                        P.op("pe", lambda e, bs=bs, cc=cc, ti=ti: e.matmul(
                            PS[bs][:, 0:V], lhsT=KEt[cc * 64:cc * 64 + 64, ti, :], rhs=VT[cc * 64:cc * 64 + 64, ti, 0:V],
                            start=True, stop=True), reads=[KW("KEt", ti), KW("VT", ti)], writes=[("ps", bs)])
                        P.op("dve", lambda e, bs=bs, q0=q0: e.scalar_tensor_tensor(
                            out=SST[:, h, 0:V], in0=SST[:, h, 0:V], scalar=T1[:, q0 + 63:q0 + 64], op0=ALU.mult,
                            in1=PS[bs][:, 0:V], op1=ALU.add),
                            reads=[K("SST", 0, h), K("T1"), ("ps", bs)], writes=[K("SST", 0, h)])
                        P.op("act", lambda e: e.activation(out=SBF[:, h, 0:V], in_=SST[:, h, 0:V], func=AF.Copy),
                             reads=[K("SST", 0, h)], writes=[KW("SBF", h)])
                else:
                    nper = 1024 // V
                    for q0_ in range(0, 16, nper):
                        hb_ = (q0_ // nper) % 2
                        P.op("act", lambda e, q0_=q0_, hb_=hb_: e.activation(
                            out=hn[hb_][:, 0:nper * V].rearrange("p (s v) -> p s v", v=V), in_=S0F[:, q0_:q0_ + nper, 0:V], func=AF.Copy),
                            reads=[K("S0F")], writes=[("hn", hb_)])
                        for sq in range(q0_, q0_ + nper):
                            for vc in range(nvc):
                                o_ = (sq - q0_) * V + vc * 128
                                P.op("pe", lambda e, vc=vc, sq=sq, hb_=hb_, o_=o_, bo=bo: e.matmul(
                                    PS[bo][:, vc * 128 + sq * 4:vc * 128 + sq * 4 + 4],
                                    lhsT=hn[hb_][:, o_:o_ + 128], rhs=QD[:, sq * 4:sq * 4 + 4],
                                    start=False, stop=True, skip_group_check=True),
                                    reads=[("hn", hb_), KW("QD")], writes=[("ps", bo)])
                for vc, o32 in zip(range(nvc), (O32a, O32b)):
                    P.op("act", lambda e, vc=vc, o32=o32, c0=c0, bo=bo: e.activation(
                        out=o32[:, c0:c0 + r], in_=PS[bo][:, vc * 128:vc * 128 + r], func=AF.Copy),
                        reads=[("ps", bo)], writes=[K("O32", vc, ti)])
            if sample:
                for q4 in range(4):
                    P.op("dve", lambda e, q4=q4: e.tensor_tensor(
                        out=VBLK[:, :, 0:V], in0=VT[0:64, 0:1, 0:V].to_broadcast([64, 4, V]),
                        in1=maskS[0:64, 64 + q4 * 4:64 + q4 * 4 + 4].unsqueeze(2).to_broadcast([64, 4, V]), op=ALU.mult),
                        reads=[KW("VT", 0), ("maskS",)], writes=[KW("VBLK")])
                    for s4 in range(4):
                        sq = q4 * 4 + s4
                        bs = nb()
                        P.op("pe", lambda e, bs=bs, s4=s4: e.matmul(
                            PS[bs][:, 0:V], lhsT=KEt[0:64, 0, :], rhs=VBLK[:, s4, 0:V], start=True, stop=True),
                            reads=[KW("KEt", 0), KW("VBLK")], writes=[("ps", bs)])
                        P.op("dve", lambda e, bs=bs, sq=sq: e.scalar_tensor_tensor(
                            out=S0F[:, sq, 0:V], in0=S0F[:, sq, 0:V], scalar=T1[:, sq * 4 + 3:sq * 4 + 4], op0=ALU.mult,
                            in1=PS[bs][:, 0:V], op1=ALU.add),
                            reads=[K("S0F"), K("T1"), ("ps", bs)], writes=[K("S0F")])
                P.op("sp", lambda e: e.dma_start(out=ss_d[:, h].rearrange("s k v -> k s v"), in_=S0F[:, :, 0:V]),
                     reads=[K("S0F")], dma=True)
            bq = nb()
            for vc, o32 in zip(range(nvc), (O32a, O32b)):
                sqb, sqk = (QD, KW("QD")) if vc == 0 else (KI, KW("KI"))
                P.op("act", lambda e, o32=o32, sqb=sqb: e.activation(out=sqb[:, 0:n], in_=o32[:, 0:n], func=AF.Square),
                     reads=[K("O32", vc)], writes=[sqk])
                P.op("pe", lambda e, vc=vc, sqb=sqb: e.matmul(PS[bq][:, 0:n], lhsT=ones_bf[:, :], rhs=sqb[:, 0:n],
                                                              start=(vc == 0), stop=(vc == nvc - 1)),
                     reads=[sqk, ("ones",)], writes=[("ps", bq)])
            P.op("dve", lambda e: e.tensor_scalar(out=RSTD[:, 0:n], in0=PS[bq][:, 0:n], scalar1=1.0 / V, scalar2=EPS,
                                                  op0=ALU.mult, op1=ALU.add),
                 reads=[("ps", bq)], writes=[K("RSTD")])
            P.op("act", lambda e: e.activation(out=RSTD[:, 0:n], in_=RSTD[:, 0:n], func=AF.Ln),
                 reads=[K("RSTD")], writes=[K("RSTD")])
            P.op("act", lambda e: e.activation(out=RSTD[:, 0:n], in_=RSTD[:, 0:n], func=AF.Exp, scale=-0.5),
                 reads=[K("RSTD")], writes=[K("RSTD")])
            for vc, o32, gt in zip(range(nvc), (O32a, O32b), (GTa, GTb)):
                P.op("dve", lambda e, o32=o32, vc=vc: e.scalar_tensor_tensor(
                    out=o32[:, 0:n], in0=o32[:, 0:n], scalar=cols[:, normcol0 + vc:normcol0 + vc + 1], op0=ALU.mult,
                    in1=RSTD[:, 0:n], op1=ALU.mult),
                    reads=[K("O32", vc), K("RSTD"), ("cols",)], writes=[K("O32", vc)])
                P.op("dve", lambda e, o32=o32, gt=gt, vc=vc: e.tensor_tensor(
                    out=OAT[:, oc0 + vc, 0:n], in0=o32[:, 0:n], in1=gt[:, 0:n], op=ALU.mult),
                    reads=[K("O32", vc), K("GT", vc)], writes=[KW("OAT", oc0 + vc)])

        def out_proj(w_out, t0, n, nfc):
            slots = []
            for fc in range(nfc):
                s_ = wslot[0] % 8
                wslot[0] += 1
                P.op("pool", lambda e, s_=s_, fc=fc: e.dma_start(
                    out=Wr[s_][:].rearrange("p c n -> p (c n)"), in_=w_out[fc * 128:(fc + 1) * 128, :]),
                    writes=[("Wr", s_)], dma=True)
                slots.append(s_)
            for ti in range(max(1, n // 128)):
                r = min(128, n)
                tt = t0 // 128 + ti
                for dh in range(2):
                    b = nb()
                    for fc in range(nfc):
                        P.op("pe", lambda e, fc=fc, b=b, ti=ti, dh=dh: e.matmul(
                            PS[b][0:r, :], lhsT=OAT[:, fc, ti * 128:ti * 128 + r],
                            rhs=Wr[slots[fc]][:].rearrange("p c n -> p (c n)")[:, dh * 512:(dh + 1) * 512],
                            start=(fc == 0), stop=(fc == nfc - 1)),
                            reads=[KW("OAT", fc), ("Wr", slots[fc])], writes=[("ps", b)])
                    P.op("dve", lambda e, b=b, tt=tt, dh=dh: e.tensor_tensor(
                        out=X[0:r, tt, dh * 512:(dh + 1) * 512], in0=PS[b][0:r, :],
                        in1=X[0:r, tt, dh * 512:(dh + 1) * 512], op=ALU.add),
                        reads=[("ps", b), ("X", tt)], writes=[("X", tt)])

        def evac(func, dst, dkey, b, n, rows_=128, **kw):
            P.op("act", lambda e: e.activation(out=dst, in_=PS[b][0:rows_, 0:n], func=func, **kw),
                 reads=[("ps", b)], writes=[dkey])

        def hgrn_head(si, t0, n, h, sample):
            W = ab_w_in[0]
            b = proj_fm(W, h * 128, t0, n)
            evac(AF.Silu, QV[:, 0:n], K("QV"), b, n)
            b = proj_fm(W, 1536 + h * 128, t0, n)
            evac(AF.Silu, GTa[:, 0:n], K("GT", 0), b, n)
            b = proj_fm(W, 512 + h * 128, t0, n)
            evac(AF.Exp, T1[:, 0:n], K("T1"), b, n, scale=-1.0)
            P.op("dve", lambda e: e.tensor_scalar(out=T1[:, 0:n], in0=T1[:, 0:n], scalar1=1.0, scalar2=None, op0=ALU.add),
                 reads=[K("T1")], writes=[K("T1")])
            P.op("act", lambda e: e.activation(out=T1[:, 0:n], in_=T1[:, 0:n], func=AF.Ln),
                 reads=[K("T1")], writes=[K("T1")])
            P.op("act", lambda e: e.activation(out=T1[:, 0:n], in_=T1[:, 0:n], func=AF.Exp, scale=-1.0),
                 reads=[K("T1")], writes=[K("T1")])
            P.op("dve", lambda e: e.tensor_scalar(out=T1[:, 0:n], in0=T1[:, 0:n], scalar1=cols[:, 8 + h:9 + h],
                                                  scalar2=cols[:, h:h + 1], op0=ALU.mult, op1=ALU.add),
                 reads=[K("T1"), ("cols",)], writes=[K("T1")])
            P.op("dve", lambda e: e.tensor_scalar(out=KV[:, 0:n], in0=T1[:, 0:n], scalar1=-1.0, scalar2=1.0,
                                                  op0=ALU.mult, op1=ALU.add),
                 reads=[K("T1")], writes=[K("KV")])
            P.op("act", lambda e: e.activation(out=LF[:, 0:n], in_=T1[:, 0:n], func=AF.Ln),
                 reads=[K("T1")], writes=[K("LF")])
            proj_tm(W, 1024 + h * 128, t0, n, lambda ti, r: VT[0:r, ti, 0:128], KW("VT"))
            gla_core(si, t0, n, h, 128, sample, hs0, hgp, hgs, 12, h)

        identF = CST[:, 0:128]
        onesF = CST[:, 128:256]
        blkS = CST[0:64, 256:320]
        lastS = CST[0:64, 320:336]
        colA = CST[:, 336:337]
        colB = CST[:, 337:338]
        CW = cols[:, 24:56].rearrange("p (c k) -> p c k", k=4)
        CB = cols[:, 56:64]
        DTB, AROW, DROW, SNW = cols2[:, 0:8], cols2[:, 8:16], cols2[:, 16:24], cols2[:, 24:28]
        MS = aTf[:, 11104:11616]
        MSB = Wob[:, 8448:8960]

        def mamba_setup():
            P.op("sp", lambda e: e.dma_start(out=CST[:], in_=cst_d), writes=[("cst",)], dma=True)
            for k in range(4):
                P.op("sp", lambda e, k=k: e.dma_start(out=CW[:, :, k], in_=conv_w[0, k].rearrange("(c p) -> p c", p=128)),
                     writes=[("cols", "cw", k)], dma=True)
            P.op("sp", lambda e: e.dma_start(out=CB, in_=conv_b[0].rearrange("(c p) -> p c", p=128)),
                 writes=[("cols", "cb")], dma=True)
            P.op("sp", lambda e: e.dma_start(out=DTB, in_=dt_bias[0].partition_broadcast(128)), writes=[("cols2", "dtb")], dma=True)
            P.op("sp", lambda e: e.dma_start(out=AROW, in_=a_log[0].partition_broadcast(128)), writes=[("cols2", "a")], dma=True)
            P.op("sp", lambda e: e.dma_start(out=DROW, in_=ssm_d[0].partition_broadcast(128)), writes=[("cols2", "d")], dma=True)
            P.op("sp", lambda e: e.dma_start(out=SNW, in_=ssm_norm[0].rearrange("(c p) -> p c", p=128)),
                 writes=[("cols2", "snw")], dma=True)
            P.op("act", lambda e: e.activation(out=AROW, in_=AROW, func=AF.Exp), reads=[("cols2", "a")], writes=[("cols2", "a")])
            P.op("dve", lambda e: e.tensor_scalar(out=AROW, in0=AROW, scalar1=-1.0, scalar2=None, op0=ALU.mult),
                 reads=[("cols2", "a")], writes=[("cols2", "a")])
            P.op("dve", lambda e: e.memset(TAIL[:, :, :], 0.0), writes=[("tail",)])

        def KM(*a):
            return ("aT", "mx", "m") + a

        def KWM(*a):
            return ("Wo", "mx", "m") + a

        def mamba_tile_group(si, t0, n):
            sample = (si == 4)
            W = ab_w_in[0]
            r = min(128, n)
            ntile = max(1, n // 128)
            barrier()
            if si == 0:
                P.op("dve", lambda e: e.memset(MS, 0.0), writes=[KM("MS")])
                P.op("dve", lambda e: e.memset(MSB, 0.0), writes=[KWM("MSB")])
            fa = [0]

            def takeF(k, lo=None):
                o = fa[0]
                fa[0] += k
                assert fa[0] <= 5632
                return aTf[:, o:o + k]
            CONVX = takeF(2048).rearrange("p (c n) -> p c n", c=4)
            A1 = takeF(512)
            A2 = takeF(512)
            ZS = takeF(2048).rearrange("p (t f) -> p t f", t=4)
            XTOK = takeF(512)
            ZSf = ZS.rearrange("p t f -> p (t f)")
            CS0T = ZSf[0:48, 0:1024]
            CVOUT = ZSf[0:48, 1024:2048]
            fb = [6656]

            def takeG(k):
                o = fb[0]
                fb[0] += k
                assert fb[0] <= 11104
                return aTf[:, o:o + k]
            XB = takeG(8 * 520).rearrange("p (c n) -> p c n", c=8)
            wa = [0]

            def takeW(k):
                o = wa[0]
                wa[0] += k
                assert wa[0] <= 3328
                return Wob[:, o:o + k]
            BT = takeW(2 * n).rearrange("p (g n) -> p g n", g=2)
            CT = takeW(2 * n).rearrange("p (g n) -> p g n", g=2)
            BTOK = takeW(256)
            XDT = takeW(512)
            XEND = takeW(512)
            wb = [8960]

            def takeW2(k):
                o = wb[0]
                wb[0] += k
                assert wb[0] <= 11264
                return Wob[:, o:o + k]
            _mm = takeW2(512)
            MM4 = [_mm, _mm]
            YN = takeW2(512)
            CTm = takeW2(512).rearrange("p (g c i) -> p g c i", g=2, c=2)

            for c in range(8):
                b = proj_fm(W, 2560 + c * 128, t0, n)
                if not sample:
                    P.op("dve", lambda e, c=c: e.tensor_copy(out=XB[:, c, 0:3], in_=TAIL[:, c, 0:3]),
                         reads=[("tail", c)], writes=[KM("XB", c)])
                    P.op("act", lambda e, c=c, b=b: e.activation(out=XB[:, c, 3:3 + n], in_=PS[b][:, 0:n], func=AF.Copy),
                         reads=[("ps", b)], writes=[KM("XB", c)])
                    P.op("dve", lambda e, c=c: e.tensor_copy(out=TAIL[:, c, 0:3], in_=XB[:, c, n:n + 3]),
                         reads=[KM("XB", c)], writes=[("tail", c)])
                    if si == 3:
                        P.op("sp", lambda e, c=c: e.dma_start(out=cvp[:, c * 128:(c + 1) * 128].rearrange("j f -> f j"),
                                                              in_=XB[:, c, n:n + 3]), reads=[KM("XB", c)], dma=True)
                    xin = lambda k, c=c: XB[:, c, k:k + n]
                    AC, ack = (A1, KM("A1")) if c % 2 == 0 else (A2, KM("A2"))
                    acc = AC[:, 0:n]
                else:
                    xb3 = XB[:, c, 0:112].rearrange("p (s t) -> p s t", t=7)
                    if c == 0:
                        P.op("sp", lambda e: e.dma_start(out=CS0T, in_=cs0.rearrange("s j f -> (s j) f")),
                             writes=[KM("ZS")], dma=True)
                    bc = nb()
                    P.op("pe", lambda e, c=c, bc=bc: e.transpose(out=PS[bc][:, 0:48], in_=CS0T[:, c * 128:(c + 1) * 128],
                                                                 identity=identF[0:48, 0:48]),
                         reads=[KM("ZS"), ("cst",)], writes=[("ps", bc)])
                    P.op("act", lambda e, bc=bc, xb3=xb3: e.activation(
                        out=xb3[:, :, 0:3], in_=PS[bc][:, 0:48].rearrange("p (s t) -> p s t", t=3), func=AF.Copy),
                        reads=[("ps", bc)], writes=[KM("XB", c)])
                    P.op("act", lambda e, c=c, b=b, xb3=xb3: e.activation(
                        out=xb3[:, :, 3:7], in_=PS[b][:, 0:64].rearrange("p (s t) -> p s t", t=4), func=AF.Copy),
                        reads=[("ps", b)], writes=[KM("XB", c)])
                    P.op("dve", lambda e, xb3=xb3: e.tensor_copy(out=A2[:, 0:48].rearrange("p (s t) -> p s t", t=3), in_=xb3[:, :, 4:7]),
                         reads=[KM("XB", c)], writes=[KM("A2")])
                    P.op("pe", lambda e, c=c: e.transpose(out=PS[6 + c // 4][0:48, (c % 4) * 128:(c % 4) * 128 + 128], in_=A2[:, 0:48],
                                                          identity=identF),
                         reads=[KM("A2"), ("cst",)], writes=[("ps", 6 + c // 4)])
                    if c == 7:
                        for hb_ in range(2):
                            P.op("act", lambda e, hb_=hb_: e.activation(out=CVOUT[:, hb_ * 512:(hb_ + 1) * 512],
                                                                        in_=PS[6 + hb_][0:48, :], func=AF.Copy),
                                 reads=[("ps", 6 + hb_)], writes=[KM("ZS")])
                        P.op("sp", lambda e: e.dma_start(out=cvs.rearrange("s j f -> (s j) f"), in_=CVOUT),
                             reads=[KM("ZS")], dma=True)
                    xin = lambda k, xb3=xb3: xb3[:, :, k:k + 4]
                    AC, ack = A1, KM("A1")
                    acc = A1[:, 0:64].rearrange("p (s t) -> p s t", t=4)
                P.op("dve", lambda e, c=c, xin=xin, acc=acc: e.tensor_scalar(
                    out=acc, in0=xin(0), scalar1=CW[:, c, 0:1], scalar2=CB[:, c:c + 1], op0=ALU.mult, op1=ALU.add),
                    reads=[KM("XB", c), ("cols",)], writes=[ack])
                for k in range(1, 4):
                    P.op("dve", lambda e, c=c, k=k, xin=xin, acc=acc: e.scalar_tensor_tensor(
                        out=acc, in0=xin(k), scalar=CW[:, c, k:k + 1], op0=ALU.mult, in1=acc, op1=ALU.add),
                        reads=[KM("XB", c), ack, ("cols",)], writes=[ack])
                if c < 4:
                    dst, dk = CONVX[:, c, 0:n], KM("CONVX", c)
                elif c < 6:
                    dst, dk = BT[:, c - 4, 0:n], KWM("BT", c - 4)
                else:
                    dst, dk = CT[:, c - 6, 0:n], KWM("CT", c - 6)
                P.op("act", lambda e, dst=dst, AC=AC: e.activation(out=dst, in_=AC[:, 0:n], func=AF.Silu),
                     reads=[ack], writes=[dk])
            for zc in range(4):
                proj_tm(W, 2048 + zc * 128, t0, n, lambda ti, rr, zc=zc: ZS[0:rr, ti, zc * 128:(zc + 1) * 128],
                        KM("ZS"), func=AF.Silu)
            sDT = wload(W, 3584, 8)
            barrier()
            fb[0] = 6656
            CBm = [takeG(128), takeG(128)]
            DG4 = [takeG(4 * r), takeG(4 * r)]
            DM4 = [takeG(4 * r), takeG(4 * r)]
            if sample:
                SNAT = [takeG(512).rearrange("p (a n) -> p a n", a=4) for _ in range(2)]
                SNEW = [takeG(512).rearrange("p (a n) -> p a n", a=4) for _ in range(2)]
                ETR = takeG(512)
                DECP = takeG(64).rearrange("p (a s) -> p a s", a=4)
                XENDm = takeW(512)
                S0T = [takeW(512), MSB]
            mask = maskS if sample else maskP
            mkey = ("maskS",) if sample else ("maskP",)

            if sample:
                XTOKs, A1s, XENDs, BTOKs = [XTOK, XTOK], [A1, A1], [XEND, XEND], [BTOK, BTOK]
            else:
                XTOKs, A1s = [XTOK, takeG(512)], [A1, takeG(512)]
                XENDs, BTOKs = [XEND, takeW2(512)], [BTOK, takeW2(256)]
            Wd = ntile * 8
            DTw, DTAw, CUMw, TOTw, ECUMw, EENDw, TAw, TBw, DECAw, DECBw, DTMw = [takeG(Wd) for _ in range(11)]
            SSQ, RSQ = takeG(8), takeG(8)
            v3 = lambda a: a[0:r, 0:Wd].rearrange("p (t h) -> p t h", h=8)
            b = nb()
            for ti in range(ntile):
                tt = t0 // 128 + ti
                for c in range(8):
                    P.op("pe", lambda e, c=c, b=b, tt=tt, ti=ti: e.matmul(
                        PS[b][0:r, ti * 8:(ti + 1) * 8], lhsT=hT[:, c, tt * 128:tt * 128 + r], rhs=Wr[sDT][:, c, 0:8],
                        start=(c == 0 and ti == 0), stop=(c == 7), skip_group_check=True),
                        reads=[("Wr", sDT), ("hT", tt)], writes=[("ps", b)])
            P.op("dve", lambda e, b=b: e.tensor_tensor(out=v3(DTw), in0=PS[b][0:r, 0:Wd].rearrange("p (t h) -> p t h", h=8),
                                                       in1=DTB[0:r].unsqueeze(1).to_broadcast([r, ntile, 8]), op=ALU.add),
                 reads=[("ps", b), ("cols2",)], writes=[KM("DT")])
            P.op("act", lambda e: e.activation(out=DTw[0:r, 0:Wd], in_=DTw[0:r, 0:Wd], func=AF.Exp), reads=[KM("DT")], writes=[KM("DT")])
            P.op("dve", lambda e: e.tensor_scalar(out=DTw[0:r, 0:Wd], in0=DTw[0:r, 0:Wd], scalar1=1.0, scalar2=None, op0=ALU.add),
                 reads=[KM("DT")], writes=[KM("DT")])
            P.op("act", lambda e: e.activation(out=DTw[0:r, 0:Wd], in_=DTw[0:r, 0:Wd], func=AF.Ln), reads=[KM("DT")], writes=[KM("DT")])
            P.op("dve", lambda e: e.tensor_tensor(out=v3(DTAw), in0=v3(DTw), in1=AROW[0:r].unsqueeze(1).to_broadcast([r, ntile, 8]),
                                                  op=ALU.mult), reads=[KM("DT"), ("cols2",)], writes=[KM("DTA")])
            b = nb()
            P.op("pe", lambda e, b=b: e.matmul(PS[b][0:r, 0:Wd], lhsT=mask[0:r, 0:r], rhs=DTAw[0:r, 0:Wd], start=True, stop=True),
                 reads=[mkey, KM("DTA")], writes=[("ps", b)])
            P.op("act", lambda e, b=b: e.activation(out=CUMw[0:r, 0:Wd], in_=PS[b][0:r, 0:Wd], func=AF.Copy),
                 reads=[("ps", b)], writes=[KM("CUM")])
            if not sample:
                for (cm, TX, DECX) in ((colA, TAw, DECAw), (colB, TBw, DECBw)):
                    P.op("dve", lambda e, cm=cm: e.tensor_scalar(out=DTMw[:, 0:Wd], in0=DTAw[:, 0:Wd], scalar1=cm, scalar2=None, op0=ALU.mult),
                         reads=[KM("DTA"), ("cst",)], writes=[KM("DTM")])
                    b = nb()
                    P.op("pe", lambda e, b=b: e.matmul(PS[b][:, 0:Wd], lhsT=onesF, rhs=DTMw[:, 0:Wd], start=True, stop=True),
                         reads=[("cst",), KM("DTM")], writes=[("ps", b)])
                    P.op("act", lambda e, b=b, TX=TX: e.activation(out=TX[:, 0:Wd], in_=PS[b][:, 0:Wd], func=AF.Copy),
                         reads=[("ps", b)], writes=[KM("TX")])
                    P.op("act", lambda e, TX=TX, DECX=DECX: e.activation(out=DECX[:, 0:Wd], in_=TX[:, 0:Wd], func=AF.Exp),
                         reads=[KM("TX")], writes=[KM("DEC")])
                P.op("dve", lambda e: e.tensor_scalar(out=TOTw[:, 0:Wd], in0=TAw[:, 0:Wd], scalar1=colA, scalar2=None, op0=ALU.mult),
                     reads=[KM("TX"), ("cst",)], writes=[KM("TOT")])
                P.op("dve", lambda e: e.scalar_tensor_tensor(out=TOTw[:, 0:Wd], in0=TBw[:, 0:Wd], scalar=colB, op0=ALU.mult,
                                                             in1=TOTw[:, 0:Wd], op1=ALU.add),
                     reads=[KM("TX"), KM("TOT"), ("cst",)], writes=[KM("TOT")])
            else:
                b = nb()
                P.op("pe", lambda e, b=b: e.matmul(PS[b][0:64, 0:8], lhsT=blkS, rhs=DTAw[0:64, 0:8], start=True, stop=True),
                     reads=[("cst",), KM("DTA")], writes=[("ps", b)])
                P.op("act", lambda e, b=b: e.activation(out=TOTw[0:64, 0:8], in_=PS[b][0:64, 0:8], func=AF.Copy),
                     reads=[("ps", b)], writes=[KM("TOT")])
            P.op("act", lambda e: e.activation(out=ECUMw[0:r, 0:Wd], in_=CUMw[0:r, 0:Wd], func=AF.Exp), reads=[KM("CUM")], writes=[KM("ECUM")])
            P.op("dve", lambda e: e.tensor_tensor(out=EENDw[0:r, 0:Wd], in0=TOTw[0:r, 0:Wd], in1=CUMw[0:r, 0:Wd], op=ALU.subtract),
                 reads=[KM("TOT"), KM("CUM")], writes=[KM("EEND")])
            P.op("act", lambda e: e.activation(out=EENDw[0:r, 0:Wd], in_=EENDw[0:r, 0:Wd], func=AF.Exp), reads=[KM("EEND")], writes=[KM("EEND")])

            def _tile(ti, part):
                tt = t0 // 128 + ti
                c0 = ti * 128
                DT, DTA, CUM, TOT, ECUM, EEND, TA, TB, DECA, DECB = [a_[:, ti * 8:(ti + 1) * 8] for a_ in
                                                                     (DTw, DTAw, CUMw, TOTw, ECUMw, EENDw, TAw, TBw, DECAw, DECBw)]
                pp = ti % 2
                XTOK, A1, XEND, BTOK = XTOKs[pp], A1s[pp], XENDs[pp], BTOKs[pp]
                x3 = XTOK[0:r].rearrange("p (h q) -> p h q", q=64)
                if part == 0:
                    b = nb()
                    for c in range(4):
                        P.op("pe", lambda e, c=c, b=b, c0=c0: e.transpose(out=PS[b][0:r, c * 128:(c + 1) * 128],
                                                                         in_=CONVX[:, c, c0:c0 + r], identity=identF),
                             reads=[KM("CONVX", c), ("cst",)], writes=[("ps", b)])
                    P.op("act", lambda e, b=b: e.activation(out=XTOK[0:r], in_=PS[b][0:r, :], func=AF.Copy),
                         reads=[("ps", b)], writes=[KM("XTOK", pp)])
                    P.op("dve", lambda e, x3=x3: e.tensor_tensor(
                        out=XDT[0:r].rearrange("p (h q) -> p h q", q=64), in0=x3,
                        in1=DT[0:r].unsqueeze(2).to_broadcast([r, 8, 64]), op=ALU.mult),
                        reads=[KM("XTOK", pp), KM("DT")], writes=[KWM("XDT")])
                    P.op("dve", lambda e: e.tensor_tensor(
                        out=XEND[0:r].rearrange("p (h q) -> p h q", q=64), in0=XDT[0:r].rearrange("p (h q) -> p h q", q=64),
                        in1=EEND[0:r].unsqueeze(2).to_broadcast([r, 8, 64]), op=ALU.mult),
                        reads=[KWM("XDT"), KM("EEND")], writes=[KWM("XEND", pp)])
                    b = nb()
                    pvb = PS[b][:].bitcast(BF16)
                    for g in range(2):
                        P.op("pe", lambda e, g=g, pvb=pvb, c0=c0: e.transpose(out=pvb[0:r, g * 128:(g + 1) * 128],
                                                                             in_=BT[:, g, c0:c0 + r], identity=ident[:, :]),
                             reads=[KWM("BT", g), ("ident",)], writes=[("ps", b)])
                    P.op("act", lambda e, pvb=pvb, b=b: e.activation(out=BTOK[0:r], in_=pvb[0:r, 0:256], func=AF.Copy),
                         reads=[("ps", b)], writes=[KWM("BTOK", pp)])
                    byi = 6
                    for g in range(2):
                        b = nb()
                        P.op("pe", lambda e, g=g, b=b, c0=c0: e.matmul(PS[b][0:r, 0:r], lhsT=BT[:, g, c0:c0 + r],
                                                                      rhs=CT[:, g, c0:c0 + r], start=True, stop=True),
                             reads=[KWM("BT", g), KWM("CT", g)], writes=[("ps", b)])
                        P.op("dve", lambda e, g=g, b=b: e.tensor_tensor(out=CBm[g][0:r, 0:r], in0=PS[b][0:r, 0:r],
                                                                        in1=mask[0:r, 0:r], op=ALU.mult),
                             reads=[("ps", b), mkey], writes=[KM("CBm", g)])
                    for g in range(2):
                        dg, dm, mm4 = DG4[g], DM4[g], MM4[g]
                        cum4 = CUM[0:r, 4 * g:4 * g + 4]
                        P.op("dve", lambda e, dg=dg, cum4=cum4: e.tensor_tensor(
                            out=dg[0:r, :].rearrange("p (h i) -> p h i", h=4),
                            in0=identF[0:r, 0:r].unsqueeze(1).to_broadcast([r, 4, r]),
                            in1=cum4.unsqueeze(2).to_broadcast([r, 4, r]), op=ALU.mult),
                            reads=[("cst",), KM("CUM")], writes=[KM("DG", g)])
                        b = nb()
                        P.op("pe", lambda e, b=b, dg=dg: e.matmul(PS[b][0:r, 0:4 * r], lhsT=onesF[0:r, 0:r], rhs=dg[0:r, 0:4 * r],
                                                                  start=True, stop=True),
                             reads=[("cst",), KM("DG", g)], writes=[("ps", b)])
                        P.op("dve", lambda e, b=b, dm=dm, cum4=cum4: e.tensor_tensor(
                            out=dm[0:r, :].rearrange("p (h i) -> p h i", h=4),
                            in0=PS[b][0:r, 0:4 * r].rearrange("p (h i) -> p h i", h=4),
                            in1=cum4.unsqueeze(2).to_broadcast([r, 4, r]), op=ALU.subtract),
                            reads=[("ps", b), KM("CUM")], writes=[KM("DM", g)])
                        P.op("act", lambda e, dm=dm: e.activation(out=dm[0:r, 0:4 * r], in_=dm[0:r, 0:4 * r], func=AF.Exp),
                             reads=[KM("DM", g)], writes=[KM("DM", g)])
                        P.op("dve", lambda e, dm=dm, mm4=mm4, g=g: e.scalar_tensor_tensor(
                            out=mm4[0:r, 0:4 * r].rearrange("p (h i) -> p h i", h=4),
                            in0=dm[0:r, :].rearrange("p (h i) -> p h i", h=4), scalar=1.0, op0=ALU.min,
                            in1=CBm[g][0:r, 0:r].unsqueeze(1).to_broadcast([r, 4, r]), op1=ALU.mult),
                            reads=[KM("DM", g), KM("CBm", g)], writes=[KWM("MM", 0)])
                        for hh in range(4):
                            h = 4 * g + hh
                            P.op("pe", lambda e, h=h, hh=hh, mm4=mm4, byi=byi: e.matmul(
                                PS[byi][0:r, h * 64:(h + 1) * 64], lhsT=mm4[0:r, hh * r:(hh + 1) * r], rhs=XDT[0:r, h * 64:(h + 1) * 64],
                                start=(h == 0), stop=(h == 7), skip_group_check=True),
                                reads=[KWM("MM", 0), KWM("XDT")], writes=[("ps", byi)])
                    P.op("act", lambda e, byi=byi: e.activation(out=A1[0:r, :], in_=PS[byi][0:r, :], func=AF.Copy),
                         reads=[("ps", byi)], writes=[KM("A1", pp)])
                    return
                byx = 7
                if not sample:
                    for cc in range(2):
                        P.op("dve", lambda e, cc=cc, c0=c0: e.tensor_copy(out=CTm[:, :, cc, cc * 64:cc * 64 + 64],
                                                                         in_=CT[:, :, c0 + cc * 64:c0 + cc * 64 + 64]),
                             reads=[KWM("CT")], writes=[KWM("CTm", cc)])
                        P.op("dve", lambda e, cc=cc: e.memset(CTm[:, :, cc, (1 - cc) * 64:(1 - cc) * 64 + 64], 0.0),
                             writes=[KWM("CTm", cc)])
                    for cc in range(2):
                        for g in range(2):
                            P.op("pe", lambda e, cc=cc, g=g, byx=byx: e.matmul(
                                PS[byx][:, g * 256:(g + 1) * 256], lhsT=CTm[:, g, cc, :], rhs=MSB[:, g * 256:(g + 1) * 256],
                                start=(cc == 0 and g == 0), stop=(cc == 1 and g == 1), skip_group_check=True),
                                reads=[KWM("CTm", cc), KWM("MSB")], writes=[("ps", byx)])
                        bu = nb()
                        for g in range(2):
                            P.op("pe", lambda e, cc=cc, g=g, bu=bu: e.matmul(
                                PS[bu][:, g * 256:(g + 1) * 256], lhsT=BTOK[cc * 64:cc * 64 + 64, g * 128:(g + 1) * 128],
                                rhs=XEND[cc * 64:cc * 64 + 64, g * 256:(g + 1) * 256], start=(g == 0), stop=(g == 1),
                                skip_group_check=True),
                                reads=[KWM("BTOK", pp), KWM("XEND", pp)], writes=[("ps", bu)])
                        DECX = DECA if cc == 0 else DECB
                        P.op("dve", lambda e, DECX=DECX: e.tensor_tensor(
                            out=MS.rearrange("p (h q) -> p h q", q=64), in0=MS.rearrange("p (h q) -> p h q", q=64),
                            in1=DECX.unsqueeze(2).to_broadcast([128, 8, 64]), op=ALU.mult),
                            reads=[KM("MS"), KM("DEC")], writes=[KM("MS")])
                        P.op("dve", lambda e, bu=bu: e.tensor_tensor(out=MS, in0=MS, in1=PS[bu][:, :], op=ALU.add),
                             reads=[KM("MS"), ("ps", bu)], writes=[KM("MS")])
                        P.op("act", lambda e: e.activation(out=MSB, in_=MS, func=AF.Copy), reads=[KM("MS")], writes=[KWM("MSB")])
                else:
                    P.op("act", lambda e: e.activation(out=TA[0:64], in_=TOT[0:64], func=AF.Exp), reads=[KM("TOT")], writes=[KM("TX")])
                    P.op("dve", lambda e: e.tensor_copy(out=ETR[0:64].rearrange("p (h q) -> p h q", q=64),
                                                        in_=TA[0:64].unsqueeze(2).to_broadcast([64, 8, 64])),
                         reads=[KM("TX")], writes=[KM("ETR")])
                    bd = nb()
                    for a in range(4):
                        P.op("pe", lambda e, a=a, bd=bd: e.matmul(PS[bd][:, a * 16:(a + 1) * 16], lhsT=ETR[0:64, a * 128:(a + 1) * 128],
                                                                  rhs=lastS, start=(a == 0), stop=(a == 3), skip_group_check=True),
                             reads=[KM("ETR"), ("cst",)], writes=[("ps", bd)])
                    P.op("act", lambda e, bd=bd: e.activation(out=DECP, in_=PS[bd][:, 0:64].rearrange("p (a s) -> p a s", a=4),
                                                              func=AF.Copy), reads=[("ps", bd)], writes=[KM("DECP")])
                    for sq in range(16):
                        sn, snew, s0t = SNAT[sq % 2], SNEW[sq % 2], S0T[sq % 2]
                        P.op("sp", lambda e, sq=sq, sn=sn: e.dma_start(
                            out=sn, in_=ss0[sq].rearrange("(a b) p n -> (b p) a n", b=2)), writes=[KM("SNAT", sq % 2)], dma=True)
                        bt_ = nb()
                        for a in range(4):
                            P.op("pe", lambda e, a=a, bt_=bt_, sn=sn: e.transpose(out=PS[bt_][:, a * 128:(a + 1) * 128],
                                                                               in_=sn[:, a, :], identity=identF),
                                 reads=[KM("SNAT", sq % 2), ("cst",)], writes=[("ps", bt_)])
                        P.op("act", lambda e, bt_=bt_, s0t=s0t: e.activation(out=s0t, in_=PS[bt_][:, :], func=AF.Copy),
                             reads=[("ps", bt_)], writes=[KWM("S0T", sq % 2)])
                        P.op("dve", lambda e: e.memset(CTm[:, :, 0, 0:64], 0.0), writes=[KWM("CTm", 0)])
                        P.op("dve", lambda e, sq=sq: e.tensor_copy(out=CTm[:, :, 0, sq * 4:sq * 4 + 4], in_=CT[:, :, sq * 4:sq * 4 + 4]),
                             reads=[KWM("CT")], writes=[KWM("CTm", 0)])
                        for g in range(2):
                            P.op("pe", lambda e, g=g, sq=sq, s0t=s0t, byx=byx: e.matmul(
                                PS[byx][0:64, g * 256:(g + 1) * 256], lhsT=CTm[:, g, 0, 0:64], rhs=s0t[:, g * 256:(g + 1) * 256],
                                start=(sq == 0 and g == 0), stop=(sq == 15 and g == 1), skip_group_check=True),
                                reads=[KWM("CTm", 0), KWM("S0T", sq % 2)], writes=[("ps", byx)])
                        P.op("dve", lambda e, sq=sq: e.tensor_scalar(out=XENDm[0:64], in0=XEND[0:64],
                                                                     scalar1=maskS[0:64, 64 + sq:65 + sq], scalar2=None, op0=ALU.mult),
                             reads=[KWM("XEND", pp), ("maskS",)], writes=[KWM("XENDm")])
                        bn = nb()
                        for a in range(4):
                            P.op("pe", lambda e, a=a, bn=bn: e.matmul(
                                PS[bn][:, a * 128:(a + 1) * 128], lhsT=XENDm[0:64, a * 128:(a + 1) * 128],
                                rhs=BTOK[0:64, (a // 2) * 128:(a // 2) * 128 + 128], start=(a == 0), stop=(a == 3),
                                skip_group_check=True),
                                reads=[KWM("XENDm"), KWM("BTOK", pp)], writes=[("ps", bn)])
                        for a in range(4):
                            P.op("dve", lambda e, a=a, sq=sq, bn=bn, sn=sn, snew=snew: e.scalar_tensor_tensor(
                                out=snew[:, a, :], in0=sn[:, a, :], scalar=DECP[:, a, sq:sq + 1], op0=ALU.mult,
                                in1=PS[bn][:, a * 128:(a + 1) * 128], op1=ALU.add),
                                reads=[KM("SNAT", sq % 2), KM("DECP"), ("ps", bn)], writes=[KM("SNEW", sq % 2)])
                        P.op("sp", lambda e, sq=sq, snew=snew: e.dma_start(
                            out=sss[sq].rearrange("(a b) p n -> (b p) a n", b=2), in_=snew), reads=[KM("SNEW", sq % 2)], dma=True)
                P.op("dve", lambda e, byx=byx: e.tensor_tensor(
                    out=A2[0:r, :].rearrange("p (h q) -> p h q", q=64), in0=PS[byx][0:r, :].rearrange("p (h q) -> p h q", q=64),
                    in1=ECUM[0:r].unsqueeze(2).to_broadcast([r, 8, 64]), op=ALU.mult),
                    reads=[("ps", byx), KM("ECUM")], writes=[KM("A2")])
                P.op("dve", lambda e: e.tensor_tensor(out=A2[0:r, :], in0=A2[0:r, :], in1=A1[0:r, :], op=ALU.add),
                     reads=[KM("A2"), KM("A1", pp)], writes=[KM("A2")])
                P.op("dve", lambda e, x3=x3: e.tensor_tensor(out=A1[0:r, :].rearrange("p (h q) -> p h q", q=64), in0=x3,
                                                             in1=DROW[0:r].unsqueeze(2).to_broadcast([r, 8, 64]), op=ALU.mult),
                     reads=[KM("XTOK", pp), ("cols2",)], writes=[KM("A1", pp)])
                P.op("dve", lambda e: e.tensor_tensor(out=A2[0:r, :], in0=A2[0:r, :], in1=A1[0:r, :], op=ALU.add),
                     reads=[KM("A2"), KM("A1", pp)], writes=[KM("A2")])
                P.op("dve", lambda e, ti=ti: e.tensor_tensor(out=A2[0:r, :], in0=A2[0:r, :], in1=ZS[0:r, ti, :], op=ALU.mult),
                     reads=[KM("A2"), KM("ZS")], writes=[KM("A2")])
                for g in range(2):
                    P.op("act", lambda e, g=g: e.activation(out=A1[0:r, g * 256:(g + 1) * 256], in_=A2[0:r, g * 256:(g + 1) * 256],
                                                            func=AF.Square, accum_out=SSQ[0:r, g:g + 1]),
                         reads=[KM("A2")], writes=[KM("A1", pp), KM("SSQ")])
                P.op("dve", lambda e: e.tensor_scalar(out=RSQ[0:r, 0:2], in0=SSQ[0:r, 0:2], scalar1=1.0 / 256, scalar2=EPS,
                                                      op0=ALU.mult, op1=ALU.add), reads=[KM("SSQ")], writes=[KM("RSQ")])
                P.op("act", lambda e: e.activation(out=RSQ[0:r, 0:2], in_=RSQ[0:r, 0:2], func=AF.Ln),
                     reads=[KM("RSQ")], writes=[KM("RSQ")])
                P.op("act", lambda e: e.activation(out=RSQ[0:r, 0:2], in_=RSQ[0:r, 0:2], func=AF.Exp, scale=-0.5),
                     reads=[KM("RSQ")], writes=[KM("RSQ")])
                for g in range(2):
                    P.op("dve", lambda e, g=g: e.tensor_scalar(out=YN[0:r, g * 256:(g + 1) * 256], in0=A2[0:r, g * 256:(g + 1) * 256],
                                                               scalar1=RSQ[0:r, g:g + 1], scalar2=None, op0=ALU.mult),
                         reads=[KM("A2"), KM("RSQ")], writes=[KWM("YN")])
                b = nb()
                pvy = PS[b][:].bitcast(BF16).rearrange("p (c t) -> p c t", c=8)
                for c in range(4):
                    P.op("pe", lambda e, c=c, pvy=pvy: e.transpose(out=pvy[:, c, 0:r], in_=YN[0:r, c * 128:(c + 1) * 128],
                                                                   identity=ident[0:r, 0:r]),
                         reads=[KWM("YN"), ("ident",)], writes=[("ps", b)])
                for c in range(4):
                    P.op("act", lambda e, c=c, pvy=pvy, c0=c0: e.activation(out=OAT[:, 4 + c, c0:c0 + r], in_=pvy[:, c, 0:r],
                                                                           func=AF.Copy, scale=SNW[:, c:c + 1]),
                         reads=[("ps", b), ("cols2",)], writes=[KW("OAT", 4 + c)])
            _tile(0, 0)
            for ti in range(ntile):
                if ti + 1 < ntile:
                    _tile(ti + 1, 0)
                _tile(ti, 1)
            if si == 3:
                b = nb()
                for a in range(4):
                    P.op("pe", lambda e, a=a, b=b: e.transpose(out=PS[b][:, a * 128:(a + 1) * 128], in_=MS[:, a * 128:(a + 1) * 128],
                                                               identity=identF), reads=[KM("MS"), ("cst",)], writes=[("ps", b)])
                P.op("act", lambda e, b=b: e.activation(out=A1[:, :], in_=PS[b][:, :], func=AF.Copy),
                     reads=[("ps", b)], writes=[KM("A1")])
                P.op("sp", lambda e: e.dma_start(out=ssp.rearrange("(a b) p n -> (b p) a n", b=2),
                                                 in_=A1[:, :].rearrange("p (a n) -> p a n", a=4)), reads=[KM("A1")], dma=True)
            barrier()

        def hgrn_mixer(norm_gi, next_gi):
            pend = list(range(len(STILES)))

            def norm_upto(idx):
                while pend and pend[0] <= idx:
                    gi_ = pend.pop(0)
                    norm_group(norm_gi, *STILES[gi_])
            mamba_setup()
            barrier()
            P.op("dve", lambda e: e.memset(SST[:, :, :], 0.0), writes=[K("SST")])
            P.op("dve", lambda e: e.memset(SBF[:, :, :], 0.0), writes=[KW("SBF")])
            for si, (t0, n) in enumerate(STILES):
                norm_upto(si + 2)
                if si == 4:
                    barrier()
                for h in range(4):
                    hgrn_head(si, t0, n, h, si == 4)
                mamba_tile_group(si, t0, n)
                out_proj(ab_w_out[0], t0, n, 8)
                norm_group(next_gi, t0, n)
                if si == 3:
                    for h in range(4):
                        P.op("sp", lambda e, h=h: e.dma_start(out=hgp[h], in_=SST[:, h, 0:128]),
                             reads=[K("SST", 0, h)], dma=True)
            barrier()

        def gla_head(si, t0, n, h, sample):
            W = gla_w_in[0]
            for vc in range(2):
                b2 = proj_fm(W, 2048 + h * 256 + vc * 128, t0, n)
                evac(AF.Silu, (GTa, GTb)[vc][:, 0:n], K("GT", vc), b2, n)
            b = proj_fm(W, h * 128, t0, n)
            P.op("dve", lambda e, b=b: e.tensor_scalar(out=QV[:, 0:n], in0=PS[b][:, 0:n], scalar1=float(128 ** -0.5),
                                                       scalar2=None, op0=ALU.mult),
                 reads=[("ps", b)], writes=[K("QV")])
            b = proj_fm(W, 512 + h * 128, t0, n)
            evac(AF.Copy, KV[:, 0:n], K("KV"), b, n)
            b = nb()
            P.op("pe", lambda e, b=b: e.matmul(PS[b][:, 0:n], lhsT=WGK[:, h * 128:(h + 1) * 128],
                                               rhs=GKL[:, 0:n], start=True, stop=True),
                 reads=[KW("WGK"), KW("GKL")], writes=[("ps", b)])
            P.op("dve", lambda e, b=b: e.tensor_scalar(out=T1[:, 0:n], in0=PS[b][:, 0:n], scalar1=cols[:, 16 + h:17 + h],
                                                       scalar2=None, op0=ALU.subtract),
                 reads=[("ps", b), ("cols",)], writes=[K("T1")])
            P.op("act", lambda e: e.activation(out=T1[:, 0:n], in_=T1[:, 0:n], func=AF.Exp, scale=-1.0),
                 reads=[K("T1")], writes=[K("T1")])
            P.op("dve", lambda e: e.tensor_scalar(out=T1[:, 0:n], in0=T1[:, 0:n], scalar1=1.0, scalar2=None, op0=ALU.add),
                 reads=[K("T1")], writes=[K("T1")])
            P.op("act", lambda e: e.activation(out=T1[:, 0:n], in_=T1[:, 0:n], func=AF.Ln),
                 reads=[K("T1")], writes=[K("T1")])
            P.op("dve", lambda e: e.tensor_scalar(out=LF[:, 0:n], in0=T1[:, 0:n], scalar1=-1.0 / 16.0,
                                                  scalar2=None, op0=ALU.mult),
                 reads=[K("T1")], writes=[K("LF")])
            for vc in range(2):
                proj_tm(W, 1024 + h * 256 + vc * 128, t0, n,
                        lambda ti, r, vc=vc: VT[0:r, ti, vc * 128:(vc + 1) * 128], KW("VT"))
            gla_core(si, t0, n, h, 256, sample, gs0, glp, gls, 20, 2 * h)

        def gla_gk(t0, n):
            b = proj_fm(gla_w_in[0], 3072, t0, n, ncols=16)
            evac(AF.Copy, GKL[:, 0:n], KW("GKL"), b, n, rows_=16)

        def gla_mixer(norm_gi, next_gi):
            pend = list(range(len(STILES)))

            def norm_upto(idx):
                while pend and pend[0] <= idx:
                    gi_ = pend.pop(0)
                    norm_group(norm_gi, *STILES[gi_])
            barrier()
            P.op("dve", lambda e: e.memset(SST[:, :, :], 0.0), writes=[K("SST")])
            P.op("dve", lambda e: e.memset(SBF[:, :, :], 0.0), writes=[KW("SBF")])
            P.op("pool", lambda e: e.dma_start(out=WGK, in_=gla_w_gk[0]), writes=[KW("WGK")], dma=True)
            for si, (t0, n) in enumerate(STILES):
                norm_upto(si + 2)
                if si == 4:
                    barrier()
                gla_gk(t0, n)
                for h in range(4):
                    gla_head(si, t0, n, h, si == 4)
                out_proj(gla_w_out[0], t0, n, 8)
                norm_group(next_gi, t0, n)
                if si == 3:
                    for h in range(4):
                        P.op("sp", lambda e, h=h: e.dma_start(out=glp[h], in_=SST[:, h, 0:256]),
                             reads=[K("SST", 0, h)], dma=True)
            barrier()

        for layer in range(2):
            norm_to_hT(3 * layer + 0)
            ffn(ffn_w_in[0][layer], ffn_w_out[0][layer])
            if layer == 0:
                hgrn_mixer(3 * layer + 1, 3 * layer + 2)
            else:
                gla_mixer(3 * layer + 1, 3 * layer + 2)
            ffn(ffn_w_in[1][layer], ffn_w_out[1][layer], tile_epilogue=(final_tile if layer == 1 else None))

        with nc.allow_non_contiguous_dma(reason="small strided parameter/state transfers"):
            P.emit()
    return nc


_CACHE = {}


def kernel(**inputs):
    f32 = lambda a: np.ascontiguousarray(np.asarray(a, dtype=np.float32))
    x_prompt = f32(inputs["x_prompt"])
    x_sample = f32(inputs["x_sample"]).reshape(128 * 4, D)
    shared = {k: f32(inputs[k]) for k in ("norm_ffn1", "norm_mix", "norm_ffn2", "norm_final",
                                          "ffn1_w_in", "ffn1_w_out", "ffn2_w_in", "ffn2_w_out")}
    for k in ("ab_w_in", "ab_w_out", "hgrn_lb_logits", "hgrn_norm", "gla_w_in", "gla_w_gk", "gla_b_gk",
              "gla_norm", "gla_w_out", "ssm_conv_w", "ssm_conv_b", "ssm_dt_bias", "ssm_a_log", "ssm_d", "ssm_norm"):
        shared[k] = f32(inputs[k])
    shared["ident"] = np.eye(128, dtype=np.float32).astype(ml_dtypes.bfloat16)
    shared["ones_bf"] = np.ones((128, 128), dtype=np.float32).astype(ml_dtypes.bfloat16)
    jj, ii = np.meshgrid(np.arange(128), np.arange(128), indexing="ij")
    shared["maskP"] = ((jj // 64 == ii // 64) & (jj <= ii)).astype(np.float32)
    mS = np.zeros((128, 128), np.float32)
    mS[:64, :64] = ((jj // 4 == ii // 4) & (jj <= ii))[:64, :64]
    mS[:64, 64:80] = (np.arange(64)[:, None] // 4 == np.arange(16)[None, :])
    shared["maskS"] = mS
    shared["resetP"] = np.broadcast_to((np.arange(512) % 64 != 0).astype(np.float32), (128, 512)).copy()
    shared["resetS"] = np.broadcast_to((np.arange(512) % 4 != 0).astype(np.float32), (128, 512)).copy()
    cst = np.zeros((128, 384), np.float32)
    cst[:, 0:128] = np.eye(128)
    cst[:, 128:256] = 1.0
    cst[:64, 256:320] = (np.arange(64)[:, None] // 4 == np.arange(64)[None, :] // 4)
    cst[:64, 320:336] = (np.arange(64)[:, None] == 4 * np.arange(16)[None, :] + 3)
    cst[:64, 336] = 1.0
    cst[64:, 337] = 1.0
    shared["cst"] = cst
    st_s = f32(inputs["state_ssm"])[0]
    st_c = f32(inputs["state_conv"])[0]
    st_h = f32(inputs["state_hgrn"])[0]
    st_g = f32(inputs["state_gla"])[0]
    if "nc" not in _CACHE:
        _CACHE["nc"] = build_program()
    nc = _CACHE["nc"]
    in_maps = []
    for c in range(N_CORES):
        m = dict(shared)
        m["xp"] = x_prompt[c]
        m["xs"] = x_sample[c * 64:(c + 1) * 64]
        m["hs0"] = st_h[c * 16:(c + 1) * 16]
        m["gs0"] = st_g[c * 16:(c + 1) * 16]
        m["ss0"] = st_s[c * 16:(c + 1) * 16]
        m["cs0"] = st_c[c * 16:(c + 1) * 16]
        in_maps.append(m)
    res = run_bass_kernel_spmd(nc, in_maps, core_ids=list(range(N_CORES)))
    outs = res.results
    y_prompt = np.stack([outs[c]["yp"] for c in range(N_CORES)], axis=0)
    y_sample = np.concatenate([outs[c]["ys"] for c in range(N_CORES)], axis=0).reshape(128, 4, D)
    cat = lambda k: np.concatenate([outs[c][k] for c in range(N_CORES)], axis=0)
    stk = lambda k: np.stack([outs[c][k] for c in range(N_CORES)], axis=0)
    hgrn_p = stk("hgp")[None]
    hgrn_s = cat("hgs")[None]
    gla_p = stk("glp")[None]
    gla_s = cat("gls")[None]
    z = lambda *sh: np.zeros(sh, np.float32)
    return (y_prompt, y_sample, hgrn_p, hgrn_s, stk("ssp")[None], cat("sss")[None],
            stk("cvp")[None], cat("cvs")[None], gla_p, gla_s)
```
